# Optimizing a Trainium2 kernel written in Bass

```python
import math
import jax, jax.numpy as jnp
from jax import lax
import numpy as np

D_MODEL = 2048
BATCH = 1
SEQ = 8192
DEPTH = 4

CHUNK = 64
HGRN_HEADS = 8
HGRN_KDIM = 128
HGRN_VDIM = 128
HGRN_KWIDTH = HGRN_HEADS * HGRN_KDIM
HGRN_VWIDTH = HGRN_HEADS * HGRN_VDIM
DIFF_HEADS = 4
DIFF_HEAD_DIM = 128
DIFF_V_DIM = 2 * DIFF_HEAD_DIM
DIFF_QK_WIDTH = DIFF_HEADS * 2 * DIFF_HEAD_DIM
DIFF_V_WIDTH = DIFF_HEADS * DIFF_V_DIM
ROPE_THETA = 500000.0
ROPE_DIM = DIFF_HEAD_DIM // 4
Q_BLOCK = 128
FFN_HIDDEN = -(-8 * D_MODEL // (3 * 256)) * 256
IN_WIDTHS = (HGRN_KWIDTH, HGRN_KWIDTH, HGRN_VWIDTH, HGRN_VWIDTH,
             DIFF_QK_WIDTH, DIFF_QK_WIDTH, DIFF_V_WIDTH, D_MODEL, D_MODEL)
IN_TOTAL = sum(IN_WIDTHS)
NORM_EPS = 1e-6
SUBLN_EPS = 1e-5

kernel_name = "hgrn2_diffattn_gated_hybrid"


def rms_norm(x, w, eps=NORM_EPS):
    xf = x.astype(jnp.float32)
    y = xf * lax.rsqrt(jnp.mean(xf * xf, axis=-1, keepdims=True) + eps)
    return (y * w.astype(jnp.float32)).astype(x.dtype)


def split_indices():
    idx, acc = [], 0
    for w in IN_WIDTHS[:-1]:
        acc += w
        idx.append(acc)
    return idx


def rope_tables(seq):
    pos = jnp.arange(seq, dtype=jnp.float32)
    inv_freq = ROPE_THETA ** (-jnp.arange(0, ROPE_DIM, 2, dtype=jnp.float32) / ROPE_DIM)
    ang = pos[:, None] * inv_freq[None, :]
    return jnp.cos(ang), jnp.sin(ang)


def partial_rope(t, cos, sin):
    half = ROPE_DIM // 2
    c = cos[None, :, None, None, :]
    s = sin[None, :, None, None, :]
    t1 = t[..., :half].astype(jnp.float32)
    t2 = t[..., half:ROPE_DIM].astype(jnp.float32)
    out = jnp.concatenate([t1 * c - t2 * s, t2 * c + t1 * s,
                           t[..., ROPE_DIM:].astype(jnp.float32)], axis=-1)
    return out.astype(t.dtype)


def hgrn2_mixer(q_raw, f_raw, i_raw, g_raw, lb, gnorm_w):
    B, S, _ = q_raw.shape
    dt = q_raw.dtype
    H, K, V = HGRN_HEADS, HGRN_KDIM, HGRN_VDIM
    q = jax.nn.silu(q_raw.astype(jnp.float32)).reshape(B, S, H, K)
    fr = f_raw.astype(jnp.float32).reshape(B, S, H, K)
    log_lb = jnp.log(jnp.maximum(lb.astype(jnp.float32), jnp.finfo(jnp.float32).tiny)).reshape(H, K)
    log_f = jax.nn.log_sigmoid(fr) + jax.nn.softplus(log_lb - fr)
    k = -jnp.expm1(log_f)
    v = i_raw.astype(jnp.float32).reshape(B, S, H, V)
    nc = S // CHUNK

    def to_chunks(t):
        return t.reshape(B, nc, CHUNK, H, t.shape[-1]).transpose(1, 0, 3, 2, 4)

    causal = jnp.tril(jnp.ones((CHUNK, CHUNK), dtype=bool))

    def step(state, xs):
        qc, kc, vc, lfc = xs
        G = jnp.cumsum(lfc, axis=2)
        diff = G[:, :, :, None, :] - G[:, :, None, :, :]
        decay = jnp.exp(jnp.where(causal[:, :, None], diff, -jnp.inf))
        A = jnp.einsum('bhtk,bhsk,bhtsk->bhts', qc, kc, decay)
        o = (jnp.einsum('bhts,bhsv->bhtv', A, vc)
             + jnp.einsum('bhtk,bhkv->bhtv', qc * jnp.exp(G), state))
        G_last = G[:, :, -1:, :]
        state = (jnp.exp(G_last[:, :, 0, :])[..., None] * state
                 + jnp.einsum('bhsk,bhsv->bhkv', kc * jnp.exp(G_last - G), vc))
        return state, o

    s0 = jnp.zeros((B, H, K, V), jnp.float32)
    _, o = lax.scan(step, s0, (to_chunks(q), to_chunks(k), to_chunks(v), to_chunks(log_f)))
    o = o.transpose(1, 0, 3, 2, 4).reshape(B, S, H, V)
    g = g_raw.astype(jnp.float32).reshape(B, S, H, V)
    o = o * lax.rsqrt(jnp.mean(o * o, axis=-1, keepdims=True) + NORM_EPS)
    o = o * gnorm_w.astype(jnp.float32) * jax.nn.silu(g)
    return o.reshape(B, S, H * V).astype(dt)


def diff_attention(q_raw, k_raw, v_raw, lam, lam_init, subln_w, cos, sin):
    B, S, _ = q_raw.shape
    H, d = DIFF_HEADS, DIFF_HEAD_DIM
    q = partial_rope(q_raw.reshape(B, S, H, 2, d), cos, sin) * (d ** -0.5)
    k = partial_rope(k_raw.reshape(B, S, H, 2, d), cos, sin)
    v = v_raw.reshape(B, S, H, DIFF_V_DIM)
    nb = S // Q_BLOCK
    q_blocks = q.reshape(B, nb, Q_BLOCK, H, 2, d).transpose(1, 0, 2, 3, 4, 5)
    k_chunk = jnp.arange(S) // CHUNK
    q_chunk_blocks = k_chunk.reshape(nb, Q_BLOCK)

    def attend(args):
        qblk, qch = args
        s = jnp.einsum('bqhcd,bkhcd->bhcqk', qblk, k).astype(jnp.float32)
        mask = qch[:, None] >= k_chunk[None, :]
        p = jax.nn.softmax(jnp.where(mask, s, -jnp.inf), axis=-1)
        attn = (p[:, :, 0] - lam * p[:, :, 1]).astype(v.dtype)
        return jnp.einsum('bhqk,bkhe->bqhe', attn, v)

    o = lax.map(attend, (q_blocks, q_chunk_blocks))
    o = o.transpose(1, 0, 2, 3, 4).reshape(B, S, H, DIFF_V_DIM)
    o = rms_norm(o, subln_w, eps=SUBLN_EPS) * (1.0 - lam_init)
    return o.reshape(B, S, H * DIFF_V_DIM)


def setup_inputs(seed: int = 0) -> dict:
    key = jax.random.key(seed)
    ks = jax.random.split(key, 13)
    f32 = jnp.float32
    nrm = lambda k, shape, scale: jax.random.normal(k, shape, f32) * scale
    return {
        "x": nrm(ks[0], (BATCH, SEQ, D_MODEL), 1.0),
        "w_in": nrm(ks[1], (DEPTH, D_MODEL, IN_TOTAL), D_MODEL ** -0.5),
        "hgrn_lower_bounds": nrm(ks[2], (DEPTH, HGRN_KWIDTH), 0.5),
        "hgrn_gnorm_w": 1.0 + nrm(ks[3], (DEPTH, HGRN_VDIM), 0.02),
        "diff_lambda": nrm(ks[4], (DEPTH, 4, DIFF_HEAD_DIM), 0.1),
        "diff_subln_w": 1.0 + nrm(ks[5], (DEPTH, DIFF_V_DIM), 0.02),
        "w_branch_a": nrm(ks[6], (DEPTH, HGRN_VWIDTH, D_MODEL), HGRN_VWIDTH ** -0.5),
        "w_branch_b": nrm(ks[7], (DEPTH, DIFF_V_WIDTH, D_MODEL), DIFF_V_WIDTH ** -0.5),
        "w_out": nrm(ks[8], (DEPTH, D_MODEL, D_MODEL), D_MODEL ** -0.5),
        "norm_w": 1.0 + nrm(ks[9], (DEPTH, 4, D_MODEL), 0.02),
        "w_ffn_in": nrm(ks[10], (DEPTH, D_MODEL, 2 * FFN_HIDDEN), D_MODEL ** -0.5),
        "w_ffn_out": nrm(ks[11], (DEPTH, FFN_HIDDEN, D_MODEL), FFN_HIDDEN ** -0.5),
    }


def reference(x, w_in, hgrn_lower_bounds, hgrn_gnorm_w, diff_lambda, diff_subln_w,
              w_branch_a, w_branch_b, w_out, norm_w, w_ffn_in, w_ffn_out):
    S = x.shape[1]
    cos, sin = rope_tables(S)
    lb_p = jax.nn.softmax(hgrn_lower_bounds.astype(jnp.float32), axis=0)
    lb_all = jnp.cumsum(lb_p, axis=0) - lb_p[0:1]
    splits = split_indices()
    for l in range(DEPTH):
        nw = norm_w[l]
        h = rms_norm(x, nw[0])
        proj = h @ w_in[l]
        hq, hf, hi, hg, dq, dk, dv, ga, gb = jnp.split(proj, splits, axis=-1)
        y_a = hgrn2_mixer(hq, hf, hi, hg, lb_all[l], hgrn_gnorm_w[l]) @ w_branch_a[l]
        lam_init = 0.8 - 0.6 * math.exp(-0.3 * l)
        lp = diff_lambda[l].astype(jnp.float32)
        lam = jnp.exp(jnp.sum(lp[0] * lp[1])) - jnp.exp(jnp.sum(lp[2] * lp[3])) + lam_init
        y_b = diff_attention(dq, dk, dv, lam, lam_init, diff_subln_w[l], cos, sin) @ w_branch_b[l]
        merged = jax.nn.sigmoid(ga) * y_a + jax.nn.sigmoid(gb) * y_b
        x = x + rms_norm(merged @ w_out[l], nw[1])
        h = rms_norm(x, nw[2])
        gate, up = jnp.split(h @ w_ffn_in[l], 2, axis=-1)
        x = x + rms_norm((jax.nn.silu(gate) * up) @ w_ffn_out[l], nw[3])
    return x
```

```python
import math
from contextlib import ExitStack
import numpy as np
import concourse.bass as bass
import concourse.mybir as mybir
from concourse.bass_utils import run_bass_kernel_spmd

F32 = mybir.dt.float32
BF16 = mybir.dt.bfloat16
AF = mybir.ActivationFunctionType
ALU = mybir.AluOpType
AX = mybir.AxisListType

NCORES = 8
DEPTH = 4
D = 2048
SEQ = 8192
TOK = SEQ // NCORES
NT = TOK // 128
KC = D // 128
INW = 11264
FFH = 5632
FC = FFH // 128
NORM_EPS = 1e-6
SUBLN_EPS = 1e-5
SAME_ENGINE_SYNC = True


class Rec:
    ENG = ["pe", "act", "dve", "pool", "sp"]

    def __init__(self, nc, es):
        self.nc = nc
        self.es = es
        self.ops = {e: [] for e in self.ENG}
        self.cnt = {e: 0 for e in self.ENG}
        self.esem = {e: es.enter_context(nc.semaphore("S_" + e)) for e in self.ENG}
        self.seen = {e: {} for e in self.ENG}
        self.lastw = {}
        self.readers = {}
        self.dsem = {}

    def _dsem(self, name):
        if name not in self.dsem:
            self.dsem[name] = [self.es.enter_context(self.nc.semaphore("D_" + name)), 0]
        return self.dsem[name]

    def _need(self, eng, tok, waits):
        kind, key, val = tok
        if kind == "E":
            if key == eng and (eng in ("pe", "sp") or not SAME_ENGINE_SYNC):
                return
            sem = self.esem[key]
        else:
            sem = self.dsem[key][0]
        sk = (kind, key)
        if self.seen[eng].get(sk, 0) >= val:
            return
        self.seen[eng][sk] = val
        waits.append((sem, val))

    def _deps(self, eng, reads, writes):
        waits = []
        for k in reads:
            t = self.lastw.get(k)
            if t is not None:
                self._need(eng, t, waits)
        for k in writes:
            t = self.lastw.get(k)
            if t is not None:
                self._need(eng, t, waits)
            for sk, v in self.readers.get(k, {}).items():
                self._need(eng, (sk[0], sk[1], v), waits)
        return waits

    def _commit(self, tok, reads, writes):
        sk = (tok[0], tok[1])
        for k in reads:
            d = self.readers.setdefault(k, {})
            if d.get(sk, 0) < tok[2]:
                d[sk] = tok[2]
        for k in writes:
            self.lastw[k] = tok
            self.readers[k] = {}

    @staticmethod
    def _excl(reads, writes):
        r2 = [k for k in reads if not (isinstance(k, str) and k.startswith("ps"))]
        w2 = list(writes) + [k for k in reads if isinstance(k, str) and k.startswith("ps") and k not in writes]
        return r2, w2

    def op(self, eng, fn, reads=(), writes=()):
        reads, writes = self._excl(reads, writes)
        waits = self._deps(eng, reads, writes)
        self.cnt[eng] += 1
        tok = ("E", eng, self.cnt[eng])
        self._commit(tok, reads, writes)
        self.ops[eng].append((waits, fn, (self.esem[eng], 1)))

    def dma(self, eng, fn, reads, writes, sem):
        ds = self._dsem(sem)
        waits = self._deps(eng, reads, writes)
        ds[1] += 16
        tok = ("D", sem, ds[1])
        self._commit(tok, reads, writes)
        self.ops[eng].append((waits, fn, (ds[0], 16)))

    def barrier(self):
        for e in self.ENG:
            waits = []
            for f in self.ENG:
                if f != e and self.cnt[f] > 0:
                    self._need(e, ("E", f, self.cnt[f]), waits)
            if e not in ("pe", "sp") and self.cnt[e] > 0 and SAME_ENGINE_SYNC:
                self._need(e, ("E", e, self.cnt[e]), waits)
            for name, (h, tot) in self.dsem.items():
                if tot > 0:
                    self._need(e, ("D", name, tot), waits)
            if waits:
                self.ops[e].append((waits, None, None))
        self.lastw = {}
        self.readers = {}

    def emit(self, block):
        for eng, attr in [("pe", "tensor"), ("act", "scalar"), ("dve", "vector"),
                          ("pool", "gpsimd"), ("sp", "sync")]:
            ops = self.ops[eng]

            def body(e, ops=ops):
                for waits, fn, inc in ops:
                    for sem, v in waits:
                        e.wait_ge(sem, v)
                    if fn is not None:
                        ins = fn(e)
                        ins.then_inc(inc[0], inc[1])

            getattr(block, attr)(body)


class Arena:
    def __init__(self, ap, words):
        self.ap = ap
        self.words = words
        self.top = 0

    def alloc(self, shape, dtype):
        n = int(np.prod(shape))
        w = n if dtype == F32 else (n + 1) // 2
        w = (w + 15) // 16 * 16
        assert self.top + w <= self.words, ("arena overflow", self.top, w, self.words)
        v = self.ap[:, self.top:self.top + w]
        self.top += w
        if dtype != F32:
            v = v.bitcast(dtype)
        v = v[:, 0:n]
        if len(shape) == 2:
            v = v.rearrange("p (a b) -> p a b", b=shape[1])
        elif len(shape) == 3:
            v = v.rearrange("p (a b c) -> p a b c", b=shape[1], c=shape[2])
        return v


def build_program(mode, layers, dbg=None, stop=99):
    nc = bass.Bass("TRN2", target_bir_lowering=False)
    es = ExitStack()
    nl = len(layers)

    def din(name, shape, dt=F32):
        return nc.dram_tensor(name, list(shape), dt, kind="ExternalInput").ap()

    def dscr(name, shape, dt=F32, kind="Internal"):
        return nc.dram_tensor(name, list(shape), dt, kind=kind).ap()

    x_in = din("x_loc", [TOK, D])
    w_in = din("w_in", [nl, D, INW])
    hlb = din("hgrn_lower_bounds", [DEPTH, 1024])
    gnw = din("hgrn_gnorm_w", [nl, 128])
    dlam = din("diff_lambda", [nl, 4, 128])
    slw = din("diff_subln_w", [nl, 256])
    w_a = din("w_branch_a", [nl, 1024, D])
    w_b = din("w_branch_b", [nl, 1024, D])
    w_o = din("w_out", [nl, D, D])
    nrm = din("norm_w", [nl, 4, D])
    w_f1 = din("w_ffn_in", [nl, D, 2 * FFH])
    w_f2 = din("w_ffn_out", [nl, FFH, D])
    cos_in = din("rope_cos", [TOK, 16])
    sin_in = din("rope_sin", [TOK, 16])
    cmat_in = din("cmat", [128, 5, 128])
    ind_in = din("ind", [128, 2])
    relv_in = din("relv", [128, 64])
    qcrow_in = din("qcrow", [128, TOK])
    mcore_in = din("mcore", [128, 16])
    x_out = None
    if mode != "A":
        x_out = nc.dram_tensor("x_out", [TOK, D], F32, kind="ExternalOutput").ap()

    proj_tm = dscr("proj_tm", [TOK, 6144])
    proj_fm = dscr("proj_fm", [5120, TOK])
    a_scr = dscr("a_scr", [FFH, TOK], BF16)
    exk = "ExternalOutput" if mode == "A" else "Internal"
    axk = "ExternalInput" if mode == "B" else "Internal"
    EXB = []
    for par in range(2 if mode == "fused" else 1):
        sfx = "" if par == 0 else "_p1"
        EXB.append((
            dscr("kT_loc" + sfx, [1024, TOK], BF16, kind=exk),
            dscr("v_loc" + sfx, [TOK, 1024], BF16, kind=exk),
            dscr("S_loc" + sfx, [1024, 128], F32, kind=exk),
            dscr("D_loc" + sfx, [128, 8], F32, kind=exk),
            dscr("kT_all" + sfx, [NCORES * 1024, TOK], BF16, kind=axk),
            dscr("v_all" + sfx, [SEQ, 1024], BF16, kind=axk),
            dscr("S_all" + sfx, [NCORES * 1024, 128], F32, kind=axk),
            dscr("D_all" + sfx, [NCORES * 128, 8], F32, kind=axk)))

    carry = None
    if mode in ("A", "B"):
        ck = "ExternalOutput" if mode == "A" else "ExternalInput"
        carry = (dscr("c_qsT", [128, 8, TOK], BF16, kind=ck), dscr("c_ozT", [128, 8, TOK], F32, kind=ck),
                 dscr("c_qT", [128, 8, TOK], BF16, kind=ck))
    ARENA_WORDS = 52480
    arena_t = es.enter_context(nc.sbuf_tensor("arena", [128, ARENA_WORDS], F32))
    ps = [es.enter_context(nc.psum_tensor("ps%d" % i, [128, 512], F32)) for i in range(8)]
    R = Rec(nc, es)
    A = Arena(arena_t[:, :], ARENA_WORDS)
    PK = ["ps%d" % i for i in range(8)]

    def a1(n, dt=F32):
        return A.alloc([1, n], dt)[:, 0, :]

    def ACT(out, in_, func, reads, writes, **kw):
        R.op("act", lambda e: e.activation(out=out, in_=in_, func=func, **kw), reads, writes)

    def TT(eng, out, a, b, op, reads, writes):
        R.op(eng, lambda e: e.tensor_tensor(out, a, b, op), reads, writes)

    def TS(eng, out, a, s1, s2, op0, op1, reads, writes):
        R.op(eng, lambda e: e.tensor_scalar(out, a, s1, s2, op0, op1), reads, writes)

    def TS1(eng, out, a, s1, op, reads, writes):
        R.op(eng, lambda e: e.tensor_single_scalar(out, a, s1, op), reads, writes)

    def STT(eng, out, a, s, b, op0, op1, reads, writes):
        R.op(eng, lambda e: e.scalar_tensor_tensor(out=out, in0=a, scalar=s, in1=b, op0=op0, op1=op1),
             reads, writes)

    def CP(eng, out, in_, reads, writes):
        if eng == "act":
            R.op("act", lambda e: e.activation(out=out, in_=in_, func=AF.Copy), reads, writes)
        else:
            R.op(eng, lambda e: e.tensor_copy(out, in_), reads, writes)

    def MM(out, lhsT, rhs, start, stop, reads, writes):
        R.op("pe", lambda e: e.matmul(out, lhsT, rhs, start=start, stop=stop), reads, writes)

    def TR(out, in_, ident, reads, writes):
        R.op("pe", lambda e: e.transpose(out, in_, ident), reads, writes)

    def MEMSET(eng, out, val, writes):
        R.op(eng, lambda e: e.memset(out, val), [], writes)

    def RECIP(out, in_, reads, writes):
        R.op("dve", lambda e: e.reciprocal(out, in_), reads, writes)

    def LD(out, in_, reads, writes, sem, eng="sp", nc_ok=False):
        if nc_ok:
            R.dma(eng, lambda e: e.dma_start(out=out, in_=in_, allow_slow_non_contiguous=True), reads, writes, sem)
        else:
            R.dma(eng, lambda e: e.dma_start(out=out, in_=in_), reads, writes, sem)

    xT = A.alloc([KC, TOK], F32)
    cmat = A.alloc([5, 128], F32)
    identf, LBD, UBD, LFULL, ONESF = (cmat[:, i, :] for i in range(5))
    cbf = A.alloc([3, 128], BF16)
    identb, maskbd, onesb = (cbf[:, i, :] for i in range(3))
    ind = a1(2)
    relv = a1(64)
    mcore = a1(16)
    nwT = A.alloc([nl * 4, KC], F32)
    gnwT = a1(8)
    epsc = a1(2)
    cosT = A.alloc([NT, 16], F32)
    sinT = A.alloc([NT, 16], F32)
    qcb = a1(TOK, BF16)
    PERSIST = A.top

    LD(cmat, cmat_in, [], ["cmat"], "k_cmat")
    LD(ind, ind_in, [], ["ind"], "k_ind")
    LD(relv, relv_in, [], ["relv"], "k_relv")
    LD(mcore, mcore_in, [], ["mcore"], "k_mcore")
    for i4 in range(nl * 4):
        LD(nwT[:, i4, :], nrm[i4 // 4, i4 % 4].rearrange("(kc p) -> p kc", p=128), [], ["nwT"], "k_nwT", nc_ok=True)
    LD(gnwT[:, 0:nl], gnw.rearrange("l p -> p l"), [], ["gnwT"], "k_gnwT", nc_ok=True)
    LD(cosT, cos_in.rearrange("(i p) f -> p i f", p=128), [], ["cosT"], "k_cosT")
    LD(sinT, sin_in.rearrange("(i p) f -> p i f", p=128), [], ["sinT"], "k_sinT")
    CP("dve", identb, identf, ["cmat"], ["cbf"])
    CP("dve", maskbd, LBD, ["cmat"], ["cbf"])
    CP("dve", onesb, ONESF, ["cmat"], ["cbf"])
    MEMSET("pool", epsc[:, 0:1], NORM_EPS, ["epsc"])
    MEMSET("pool", epsc[:, 1:2], SUBLN_EPS, ["epsc"])

    def eps_ap(eps):
        return epsc[:, 0:1] if eps == NORM_EPS else epsc[:, 1:2]

    xin_t = [a1(D) for _ in range(2)]
    qcf = a1(TOK)
    LD(qcf, qcrow_in, [], ["qcf"], "k_qcf")
    CP("dve", qcb, qcf, ["qcf"], ["qcb"])
    for i in range(NT):
        b = xin_t[i % 2]
        bk = "xin%d" % (i % 2)
        LD(b, x_in[i * 128:(i + 1) * 128, :], [], [bk], bk)
        for g in range(4):
            pi = (i * 4 + g) % 4
            for j in range(4):
                kc = g * 4 + j
                TR(ps[pi][:, j * 128:(j + 1) * 128], b[:, kc * 128:(kc + 1) * 128], identf, [bk, "cmat"], [PK[pi]])
            CP("act" if g % 2 else "dve", xT[:, g * 4:(g + 1) * 4, i * 128:(i + 1) * 128],
               ps[pi][:, :].rearrange("p (a b) -> p a b", b=128), [PK[pi]], ["xT"])
    R.barrier()
    A.top = PERSIST

    def rms_rstd(src, nchunks, ndiv, eps, sqb, rstd, key, t0, tn):
        nh = tn // 512
        for c in range(nchunks):
            sb = sqb[c % 2]
            sk = key + "sq%d" % (c % 2)
            ap, rk = src(c)
            ACT(sb[:, 0:tn], ap, AF.Square, rk, [sk])
            for h in range(nh):
                MM(ps[6 + h][:, :], onesb, sb[:, h * 512:(h + 1) * 512], c == 0, c == nchunks - 1,
                   [sk, "cbf"], [PK[6 + h]])
        rstd_finish(nh, ndiv, eps, rstd, key)

    def rstd_finish(nh, ndiv, eps, rstd, key):
        for h in range(nh):
            ACT(rstd[:, h * 512:(h + 1) * 512], ps[6 + h][:, :], AF.Sqrt, [PK[6 + h], "epsc"], [key + "rstd"],
                bias=eps_ap(eps), scale=1.0 / ndiv)
        RECIP(rstd[:, 0:nh * 512], rstd[:, 0:nh * 512], [key + "rstd"], [key + "rstd"])

    def wload(dst, src, key):
        R.dma("pool", lambda e: e.dma_start(out=dst, in_=src), [], [key], key)

    def evac(idx, out, in_, reads, writes):
        CP("act" if idx % 2 else "dve", out, in_, reads, writes)

    for li, l in enumerate(layers):
        lam_init = 0.8 - 0.6 * math.exp(-0.3 * l)
        A.top = PERSIST
        kT_loc, v_loc, S_loc, D_loc, kT_all, v_all, S_all, D_all = EXB[li % len(EXB)]

        hT = A.alloc([KC, TOK], BF16)
        sqb = [a1(TOK, BF16) for _ in range(2)]
        rstd = a1(TOK)
        rms_rstd(lambda c: (xT[:, c, :], ["xT"]), KC, float(D), NORM_EPS, sqb, rstd, "n1", 0, TOK)
        for c in range(KC):
            STT("dve", hT[:, c, :], xT[:, c, :], nwT[:, li * 4 + 0, c:c + 1], rstd,
                ALU.mult, ALU.mult, ["xT", "nwT", "n1rstd"], ["hT"])
        wbuf = [A.alloc([KC, 512], BF16) for _ in range(2)]
        stg = [a1(512) for _ in range(4)]
        w_l = w_in[li].rearrange("(kc p) c -> p kc c", p=128)
        tm_groups = [0, 1, 2, 3, 4, 5, 8, 9, 10, 11, 12, 13]
        fm_groups = [6, 7, 14, 15, 16, 17, 18, 19, 20, 21]
        order = [("tm", g) for g in tm_groups] if mode != "B" else []
        if mode != "A":
            order += [("fm", g) for g in fm_groups]
        sidx = 0
        pidx = 0
        for gi, (kind, g) in enumerate(order):
            wb = wbuf[gi % 2]
            wk = "wbuf%d" % (gi % 2)
            wload(wb, w_l[:, :, g * 512:(g + 1) * 512], wk)
            if kind == "tm":
                tcol = tm_groups.index(g) * 512
                for i in range(NT):
                    pi = pidx % 4
                    pidx += 1
                    for kc in range(KC):
                        MM(ps[pi][:, :], hT[:, kc, i * 128:(i + 1) * 128], wb[:, kc, :], kc == 0, kc == KC - 1,
                           ["hT", wk], [PK[pi]])
                    sb = stg[sidx % 4]
                    sk = "stg%d" % (sidx % 4)
                    evac(sidx, sb, ps[pi][:, :], [PK[pi]], [sk])
                    LD(proj_tm[i * 128:(i + 1) * 128, tcol:tcol + 512], sb, [sk], [("ptm", tcol // 1024, i)], sk)
                    sidx += 1
            else:
                frow0 = fm_groups.index(g) * 512
                for j in range(4):
                    for h in range(2):
                        pi = pidx % 4
                        pidx += 1
                        for kc in range(KC):
                            MM(ps[pi][:, :], wb[:, kc, j * 128:(j + 1) * 128], hT[:, kc, h * 512:(h + 1) * 512],
                               kc == 0, kc == KC - 1, ["hT", wk], [PK[pi]])
                        sb = stg[sidx % 4]
                        sk = "stg%d" % (sidx % 4)
                        evac(sidx, sb, ps[pi][:, :], [PK[pi]], [sk])
                        r0 = frow0 + j * 128
                        LD(proj_fm[r0:r0 + 128, h * 512:(h + 1) * 512], sb, [sk], [("pfm", r0 // 128)], sk)
                        sidx += 1
        R.barrier()
        A.top = PERSIST

        qsT = A.alloc([8, TOK], BF16)
        RAw = A.top
        ozT = A.alloc([8, TOK], F32)
        qT_off = A.top
        qT = A.alloc([8, TOK], BF16)
        S2TOP = A.top
        if mode == "B":
            LD(qsT, carry[0], [], ["qsT"], "k_cq")
            LD(ozT, carry[1], [], ["ozT"], "k_co")
            LD(qT, carry[2], [], ["qT"], "k_cT")
            R.barrier()
        else:
            qhT = A.alloc([8, 128], BF16)
            ktT = A.alloc([8, 128], BF16)
            Vt = a1(1024, BF16)
            Kh = a1(1024, BF16)
            glast = A.alloc([8, 16], F32)
            dec = A.alloc([8, 2], F32)
            dtot = a1(8)
            lb = a1(1024)
            oml = a1(1024)
            lbmark = A.top
            lbraw = A.alloc([4, 1024], F32)
            LD(lbraw, hlb.partition_broadcast(128), [], ["lbraw"], "k_lbraw")
            ACT(lbraw, lbraw, AF.Exp, ["lbraw"], ["lbraw"])
            TT("dve", lb, lbraw[:, 0, :], lbraw[:, 1, :], ALU.add, ["lbraw"], ["lb"])
            TT("dve", lb, lb, lbraw[:, 2, :], ALU.add, ["lbraw", "lb"], ["lb"])
            TT("dve", lb, lb, lbraw[:, 3, :], ALU.add, ["lbraw", "lb"], ["lb"])
            RECIP(oml, lb, ["lb"], ["oml"])
            if l == 0:
                MEMSET("dve", lb, 0.0, ["lb"])
            else:
                CP("dve", lb, lbraw[:, 1, :], ["lbraw", "oml"], ["lb"])
                for j in range(2, l + 1):
                    TT("dve", lb, lb, lbraw[:, j, :], ALU.add, ["lbraw", "lb"], ["lb"])
            TT("dve", lb, lb, oml, ALU.mult, ["lb", "oml"], ["lb"])
            TS("dve", oml, lb, -1.0, 1.0, ALU.mult, ALU.add, ["lb"], ["oml"])
            R.barrier()
            A.top = lbmark
            Tacc = a1(1024)
            tq = a1(1024)
            tf = a1(1024)
            ti = a1(1024)
            qf = a1(1024)
            lf = a1(1024)
            kk = a1(1024)
            eb = [a1(1024) for _ in range(2)]
            tb = [a1(1024, BF16) for _ in range(3)]
            Am = [a1(128, BF16) for _ in range(2)]
            Sst = A.alloc([8, 128], F32)
            Sbf = A.alloc([8, 128], BF16)
            MEMSET("pool", Tacc, 0.0, ["Tacc"])
            MEMSET("pool", Sst, 0.0, [("Sst", h) for h in range(8)])

            for i in range(NT):
                rows = slice(i * 128, (i + 1) * 128)
                tsl = slice(i * 128, (i + 1) * 128)
                LD(tq, proj_tm[rows, 0:1024], [("ptm", 0, i)], ["tq"], "tq")
                LD(tf, proj_tm[rows, 1024:2048], [("ptm", 1, i)], ["tf"], "tf")
                LD(ti, proj_tm[rows, 2048:3072], [("ptm", 2, i)], ["ti"], "ti")
                ACT(qf, tq, AF.Silu, ["tq"], ["qf"])
                ACT(tf, tf, AF.Sigmoid, ["tf"], ["tf"])
                CP("pool", Vt, ti, ["ti"], ["Vt"])
                TT("dve", tf, tf, oml, ALU.mult, ["tf", "oml"], ["tf"])
                TT("dve", tf, tf, lb, ALU.add, ["tf", "lb"], ["tf"])
                ACT(lf, tf, AF.Ln, ["tf"], ["lf"])
                TS("dve", kk, tf, -1.0, 1.0, ALU.mult, ALU.add, ["tf"], ["kk"])
                for h in range(8):
                    MM(ps[4][:, h * 2:h * 2 + 2], lf[:, h * 128:(h + 1) * 128], ind, True, True, ["lf", "ind"], [PK[4]])
                CP("dve", glast[:, :, 2 * i:2 * i + 2], ps[4][:, 0:16].rearrange("p (h c) -> p h c", c=2),
                   [PK[4]], ["glast"])
                ACT(dec, glast[:, :, 2 * i:2 * i + 2], AF.Exp, ["glast"], ["dec"])
                for h2 in range(2):
                    cs = slice(h2 * 512, (h2 + 1) * 512)
                    MM(ps[0][:, :], LBD, lf[:, cs], True, True, ["lf", "cmat"], [PK[0]])
                    MM(ps[1][:, :], UBD, lf[:, cs], True, True, ["lf", "cmat"], [PK[1]])
                    MM(ps[2][:, :], LFULL, lf[:, cs], True, True, ["lf", "cmat"], [PK[2]])
                    MM(ps[3][:, :], ONESF, lf[:, cs], True, True, ["lf", "cmat"], [PK[3]])
                    ACT(eb[0][:, cs], ps[0][:, :], AF.Exp, [PK[0]], ["eb0"])
                    TT("dve", tb[0][:, cs], qf[:, cs], eb[0][:, cs], ALU.mult, ["qf", "eb0"], ["tb0"])
                    ACT(eb[1][:, cs], ps[0][:, :], AF.Exp, [PK[0]], ["eb1"], scale=-1.0)
                    TT("pool", tb[1][:, cs], kk[:, cs], eb[1][:, cs], ALU.mult, ["kk", "eb1"], ["tb1"])
                    ACT(eb[0][:, cs], ps[1][:, :], AF.Exp, [PK[1]], ["eb0"])
                    TT("dve", Kh[:, cs], kk[:, cs], eb[0][:, cs], ALU.mult, ["kk", "eb0"], ["Kh"])
                    TT("dve", eb[1][:, cs], ps[2][:, :], Tacc[:, cs], ALU.add, [PK[2], "Tacc"], ["eb1"])
                    ACT(eb[1][:, cs], eb[1][:, cs], AF.Exp, ["eb1"], ["eb1"])
                    TT("pool", tb[2][:, cs], qf[:, cs], eb[1][:, cs], ALU.mult, ["qf", "eb1"], ["tb2"])
                    TT("dve", Tacc[:, cs], Tacc[:, cs], ps[3][:, :], ALU.add, [PK[3], "Tacc"], ["Tacc"])
                tcount = 0
                for which in range(3):
                    for g in range(2):
                        pi = 5 + (tcount % 2)
                        tcount += 1
                        pbb = ps[pi][:, :].bitcast(BF16)
                        for j in range(4):
                            h = g * 4 + j
                            TR(pbb[:, j * 128:(j + 1) * 128], tb[which][:, h * 128:(h + 1) * 128], identb,
                               ["tb%d" % which, "cbf"], [PK[pi]])
                        src = pbb[:, 0:512].rearrange("p (a b) -> p a b", b=128)
                        if which == 0:
                            evac(g, qhT[:, g * 4:(g + 1) * 4, :], src, [PK[pi]], ["qhT"])
                        elif which == 1:
                            evac(g, ktT[:, g * 4:(g + 1) * 4, :], src, [PK[pi]], ["ktT"])
                        else:
                            evac(g, qsT[:, g * 4:(g + 1) * 4, tsl], src, [PK[pi]], ["qsT"])
                for h in range(8):
                    hs = slice(h * 128, (h + 1) * 128)
                    am = Am[h % 2]
                    amk = "Am%d" % (h % 2)
                    pA, pS, pO = h % 2, 2 + (h % 2), 5 + (h % 2)
                    MM(ps[pA][:, 0:128], ktT[:, h, :], qhT[:, h, :], True, True, ["qhT", "ktT"], [PK[pA]])
                    TT("dve", am, ps[pA][:, 0:128], maskbd, ALU.mult, [PK[pA], "cbf"], [amk])
                    MM(ps[pO][:, 0:128], Vt[:, hs], am, True, False, ["Vt", amk], [PK[pO]])
                    if i > 0:
                        MM(ps[pO][:, 0:64], Sbf[:, h, :], qhT[:, h, 0:64], False, False, [("Sbf", h), "qhT"], [PK[pO]])
                    MM(ps[pS][:, 0:128], Kh[0:64, hs], Vt[0:64, hs], True, True, ["Kh", "Vt"], [PK[pS]])
                    STT("dve", Sst[:, h, :], Sst[:, h, :], dec[:, h, 0:1], ps[pS][:, 0:128], ALU.mult, ALU.add,
                        [PK[pS], "dec", ("Sst", h)], [("Sst", h)])
                    CP("pool", Sbf[:, h, :], Sst[:, h, :], [("Sst", h)], [("Sbf", h)])
                    MM(ps[pO][:, 64:128], Sbf[:, h, :], qhT[:, h, 64:128], False, True, [("Sbf", h), "qhT"], [PK[pO]])
                    CP("act", ozT[:, h, tsl], ps[pO][:, 0:128], [PK[pO]], ["ozT"])
                    MM(ps[pS][:, 0:128], Kh[64:128, hs], Vt[64:128, hs], True, True, ["Kh", "Vt"], [PK[pS]])
                    STT("dve", Sst[:, h, :], Sst[:, h, :], dec[:, h, 1:2], ps[pS][:, 0:128], ALU.mult, ALU.add,
                        [PK[pS], "dec", ("Sst", h)], [("Sst", h)])
                    CP("pool", Sbf[:, h, :], Sst[:, h, :], [("Sst", h)], [("Sbf", h)])
            LD(S_loc.rearrange("(h k) v -> k h v", k=128), Sst, [("Sst", h) for h in range(8)], ["S_loc"], "sloc")
            R.op("dve", lambda e: e.tensor_reduce(out=dtot, in_=glast, axis=AX.X, op=ALU.add), ["glast"], ["dtot"])
            ACT(dtot, dtot, AF.Exp, ["dtot"], ["dtot"])
            LD(D_loc, dtot, ["dtot"], ["D_loc"], "sloc")
            R.barrier()
            A.top = S2TOP

            kTs = A.alloc([8, TOK], BF16)
            tq = a1(1024)
            tf = a1(1024)
            ti = a1(1024)
            qb = a1(1024, BF16)
            kb = a1(1024, BF16)
            vb = [a1(1024, BF16) for _ in range(2)]
            rt = [A.alloc([8, 16], F32) for _ in range(4)]
            for i in range(NT):
                rows = slice(i * 128, (i + 1) * 128)
                tsl = slice(i * 128, (i + 1) * 128)
                LD(tq, proj_tm[rows, 3072:4096], [("ptm", 3, i)], ["tq"], "tq")
                LD(tf, proj_tm[rows, 4096:5120], [("ptm", 4, i)], ["tf"], "tf")
                LD(ti, proj_tm[rows, 5120:6144], [("ptm", 5, i)], ["ti"], "ti")
                TS1("pool", tq, tq, float(128.0 ** -0.5), ALU.mult, ["tq"], ["tq"])
                cb = cosT[:, i, :].unsqueeze(1).broadcast_to([128, 8, 16])
                sn = sinT[:, i, :].unsqueeze(1).broadcast_to([128, 8, 16])
                for (src, dst, sk_, dk_, e1, e2) in ((tq, qb, "tq", "qb", "dve", "pool"), (tf, kb, "tf", "kb", "pool", "dve")):
                    v3 = src.rearrange("p (j d) -> p j d", d=128)
                    o3 = dst.rearrange("p (j d) -> p j d", d=128)
                    t1 = v3[:, :, 0:16]
                    t2 = v3[:, :, 16:32]
                    TT(e1, rt[0], t1, cb, ALU.mult, [sk_, "cosT"], ["rt0"])
                    TT(e2, rt[1], t2, sn, ALU.mult, [sk_, "sinT"], ["rt1"])
                    TT(e1, o3[:, :, 0:16], rt[0], rt[1], ALU.subtract, ["rt0", "rt1"], [dk_])
                    TT(e1, rt[2], t2, cb, ALU.mult, [sk_, "cosT"], ["rt2"])
                    TT(e2, rt[3], t1, sn, ALU.mult, [sk_, "sinT"], ["rt3"])
                    TT(e2, o3[:, :, 16:32], rt[2], rt[3], ALU.add, ["rt2", "rt3"], [dk_])
                    CP("act", o3[:, :, 32:128], v3[:, :, 32:128], [sk_], [dk_])
                v_ = vb[i % 2]
                vk = "vb%d" % (i % 2)
                CP("act", v_, ti, ["ti"], [vk])
                LD(v_loc[rows, :], v_, [vk], ["v_loc"], vk)
                tcount = 0
                for (srcb, sk_, dstT, dk_) in ((qb, "qb", qT, "qT"), (kb, "kb", kTs, "kTs")):
                    for g in range(2):
                        pi = 4 + (tcount % 4)
                        tcount += 1
                        pbb = ps[pi][:, :].bitcast(BF16)
                        for j in range(4):
                            h = g * 4 + j
                            TR(pbb[:, j * 128:(j + 1) * 128], srcb[:, h * 128:(h + 1) * 128], identb, [sk_, "cbf"], [PK[pi]])
                        evac(g, dstT[:, g * 4:(g + 1) * 4, tsl], pbb[:, 0:512].rearrange("p (a b) -> p a b", b=128),
                             [PK[pi]], [dk_])
            LD(kT_loc.rearrange("(j d) t -> d j t", d=128), kTs, ["kTs"], ["kT_loc"], "kTs")
            R.barrier()
            A.top = S2TOP
        if mode == "A":
            LD(carry[0], qsT, [], ["c0_"], "k_cq")
            LD(carry[1], ozT, [], ["c1_"], "k_co")
            LD(carry[2], qT, [], ["c2_"], "k_cT")
            R.barrier()
        if mode == "A" or stop <= 3:
            break

        if mode == "fused":
            rg = [list(range(NCORES))]
            for (src, dst, nm) in ((kT_loc, kT_all, "kT"), (v_loc, v_all, "v"), (S_loc, S_all, "S"), (D_loc, D_all, "D")):
                R.dma("pool", lambda e, src=src, dst=dst: e.collective_compute(
                    "AllGather", ALU.bypass, replica_groups=rg, ins=[src[:, :]], outs=[dst[:, :]]),
                    [], [nm + "_all"], "cc")
            R.barrier()

        oaT = qsT
        Sr = [A.alloc([8, 128], F32) for _ in range(2)]
        Dall = A.alloc([8, 8], F32)
        Dp = a1(8)
        acc = A.alloc([8, 128], F32)
        sinb = A.alloc([8, 128], BF16)
        sq4 = a1(TOK, BF16)
        rstd4 = a1(TOK)
        gt = [a1(TOK) for _ in range(2)]
        tmp4 = a1(TOK)
        LD(Dall, D_all.rearrange("(r k) h -> k r h", k=128), [], ["Dall"], "k_Dall")
        MEMSET("pool", acc, 0.0, ["acc"])
        for r in range(NCORES - 1):
            sr = Sr[r % 2]
            srk = "Sr%d" % (r % 2)
            LD(sr, S_all[r * 1024:(r + 1) * 1024, :].rearrange("(h k) v -> k h v", k=128), [], [srk], srk)
            TS("dve", Dp, Dall[:, r, :], mcore[:, r:r + 1], mcore[:, 8 + r:9 + r], ALU.mult, ALU.add,
               ["Dall", "mcore"], ["Dp"])
            TS1("pool", sr, sr, mcore[:, r:r + 1], ALU.mult, [srk, "mcore"], [srk])
            for h in range(8):
                STT("dve", acc[:, h, :], acc[:, h, :], Dp[:, h:h + 1], sr[:, h, :], ALU.mult, ALU.add,
                    ["acc", "Dp", srk], ["acc"])
        CP("dve", sinb, acc, ["acc"], ["sinb"])
        for h in range(8):
            g_ = gt[h % 2]
            gk = "gt%d" % (h % 2)
            LD(g_, proj_fm[h * 128:(h + 1) * 128, :], [("pfm", h)], [gk], gk)
            ACT(g_, g_, AF.Silu, [gk], [gk])
            for hf in range(2):
                cs = slice(hf * 512, (hf + 1) * 512)
                MM(ps[hf][:, :], sinb[:, h, :], qsT[:, h, cs], True, True, ["sinb", ("qsT", h)], [PK[hf]])
                TT("dve", ozT[:, h, cs], ozT[:, h, cs], ps[hf][:, :], ALU.add, [PK[hf], ("ozT", h)], [("ozT", h)])
            ACT(sq4, ozT[:, h, :], AF.Square, [("ozT", h)], ["sq4"])
            for hf in range(2):
                MM(ps[6 + hf][:, :], onesb, sq4[:, hf * 512:(hf + 1) * 512], True, True, ["sq4", "cbf"], [PK[6 + hf]])
            rstd_finish(2, 128.0, NORM_EPS, rstd4, "g4")
            STT("dve", tmp4, ozT[:, h, :], gnwT[:, li:li + 1], rstd4, ALU.mult, ALU.mult,
                [("ozT", h), "gnwT", "g4rstd"], ["tmp4"])
            TT("pool", oaT[:, h, :], tmp4, g_, ALU.mult, ["tmp4", gk, ("qsT", h)], [("qsT", h)])
        R.barrier()
        A.top = S2TOP

        if stop <= 4:
            break
        obT = arena_t[:, RAw:RAw + 4096].bitcast(BF16).rearrange("p (a b) -> p a b", b=TOK)
        NSL = 3
        kres = A.alloc([2, SEQ], BF16)
        vbuf = [A.alloc([8, 257], BF16) for _ in range(NSL)]
        SBK = [0, 1, 6, 7]
        LA = 3
        pT = [a1(512, BF16) for _ in range(4)]
        pM = [a1(512, BF16) for _ in range(4)]
        lp = A.alloc([4, 128], F32)
        lpp = a1(128)
        lamc = a1(4)
        slwb = a1(256)
        rr = a1(4)
        of = a1(256)
        tmpo = a1(256)
        onb = a1(256, BF16)
        junk = a1(256)
        for s in range(NSL):
            MEMSET("pool", vbuf[s][:, :, 256:257], 1.0, ["vbuf%d" % s])
        LD(lp, dlam[li].partition_broadcast(128), [], ["lp"], "k_lp")
        TT("dve", lpp, lp[:, 0, :], lp[:, 1, :], ALU.mult, ["lp"], ["lpp"])
        R.op("dve", lambda e: e.tensor_reduce(out=lamc[:, 0:1], in_=lpp, axis=AX.X, op=ALU.add), ["lpp"], ["lamc"])
        TT("dve", lpp, lp[:, 2, :], lp[:, 3, :], ALU.mult, ["lp", "lamc"], ["lpp"])
        R.op("dve", lambda e: e.tensor_reduce(out=lamc[:, 1:2], in_=lpp, axis=AX.X, op=ALU.add), ["lpp"], ["lamc"])
        ACT(lamc[:, 0:2], lamc[:, 0:2], AF.Exp, ["lamc"], ["lamc"])
        TT("dve", lamc[:, 2:3], lamc[:, 0:1], lamc[:, 1:2], ALU.subtract, ["lamc"], ["lamc"])
        TS1("dve", lamc[:, 2:3], lamc[:, 2:3], float(lam_init), ALU.add, ["lamc"], ["lamc"])
        LD(slwb, slw[li].partition_broadcast(128),
           [], ["slwb"], "k_slwb")
        TS1("dve", slwb, slwb, float(1.0 - lam_init), ALU.mult, ["slwb"], ["slwb"])
        step = 0
        for h in range(4):
            for r in range(NCORES):
                LD(kres[:, :, r * 1024:(r + 1) * 1024],
                   kT_all[(r * 8 + 2 * h) * 128:(r * 8 + 2 * h + 2) * 128, :].rearrange("(c d) t -> d c t", d=128),
                   ["kT_all"], ["kres%d" % r], "kres%d" % r)
            for qg in range(4):
                qs_ = slice(qg * 256, (qg + 1) * 256)
                qc2 = qcb[:, qs_].unsqueeze(1).broadcast_to([128, 2, 256])
                steps = [(r, kt) for r in range(NCORES) for kt in range(8)]
                slot_of = {}

                def seg_load(r):
                    nonlocal step
                    s_ = step % NSL
                    step += 1
                    slot_of[r] = s_
                    vbk = "vbuf%d" % s_
                    LD(vbuf[s_][:, :, 0:256], v_all[r * 1024:(r + 1) * 1024, h * 256:(h + 1) * 256].rearrange(
                        "(kt p) e -> p kt e", p=128), ["v_all"], [vbk], vbk)

                def scores(r, kt):
                    s_ = slot_of[r]
                    ktg = r * 8 + kt
                    sb_ = ktg % 4
                    for c in range(2):
                        MM(ps[SBK[sb_]][:, c * 256:(c + 1) * 256], kres[:, c, ktg * 128:(ktg + 1) * 128],
                           qT[:, 2 * h + c, qs_], True, True, ["kres%d" % r, "qT"], [PK[SBK[sb_]]])

                def softmax_part(r, kt):
                    ktg = r * 8 + kt
                    sb_ = ktg % 4
                    ACT(pT[sb_], ps[SBK[sb_]][:, :], AF.Exp, [PK[SBK[sb_]]], ["pT%d" % sb_])
                    STT("dve", pM[sb_].rearrange("p (c q) -> p c q", c=2), qc2, relv[:, ktg:ktg + 1],
                        pT[sb_].rearrange("p (c q) -> p c q", c=2), ALU.is_ge, ALU.mult,
                        ["pT%d" % sb_, "qcb", "relv"], ["pM%d" % sb_])

                def pv(r, kt):
                    s_ = slot_of[r]
                    ktg = r * 8 + kt
                    sb_ = ktg % 4
                    for c in range(2):
                        for qt in range(2):
                            ai = 2 + c * 2 + qt
                            MM(ps[ai][:, 0:257], pM[sb_][:, c * 256 + qt * 128:c * 256 + (qt + 1) * 128],
                               vbuf[s_][:, kt, :], ktg == 0, ktg == 63, ["pM%d" % sb_, "vbuf%d" % s_], [PK[ai]])

                seg_load(0)
                seg_load(1)
                for k0 in range(LA):
                    scores(*steps[k0])
                for k, (r, kt) in enumerate(steps):
                    softmax_part(r, kt)
                    if k + LA < len(steps):
                        r2, kt2 = steps[k + LA]
                        if kt2 == 0 and r2 + 1 < NCORES:
                            seg_load(r2 + 1)
                        scores(r2, kt2)
                    pv(r, kt)
                for qt in range(2):
                    a0, a1_ = 2 + qt, 4 + qt
                    tcol = slice(qg * 256 + qt * 128, qg * 256 + (qt + 1) * 128)
                    RECIP(rr[:, 0:1], ps[a0][:, 256:257], [PK[a0]], ["rr"])
                    RECIP(rr[:, 1:2], ps[a1_][:, 256:257], [PK[a1_], "rr"], ["rr"])
                    TT("dve", rr[:, 1:2], rr[:, 1:2], lamc[:, 2:3], ALU.mult, ["rr", "lamc"], ["rr"])
                    TS1("dve", tmpo, ps[a1_][:, 0:256], rr[:, 1:2], ALU.mult, [PK[a1_], "rr"], ["tmpo"])
                    STT("dve", of, ps[a0][:, 0:256], rr[:, 0:1], tmpo, ALU.mult, ALU.subtract,
                        [PK[a0], "rr", "tmpo"], ["of"])
                    TT("dve", junk, of, of, ALU.mult, ["of"], ["junk"])
                    R.op("dve", lambda e: e.tensor_reduce(out=rr[:, 2:3], in_=junk, axis=AX.X, op=ALU.add), ["junk"], ["ss"])
                    ACT(rr[:, 3:4], rr[:, 2:3], AF.Sqrt, ["ss", "epsc"], ["rs"], bias=eps_ap(SUBLN_EPS), scale=1.0 / 256.0)
                    RECIP(rr[:, 3:4], rr[:, 3:4], ["rs"], ["rs"])
                    STT("dve", onb, of, rr[:, 3:4], slwb, ALU.mult, ALU.mult, ["of", "rs", "slwb"], ["onb"])
                    pbb = ps[6 + qt][:, :].bitcast(BF16)
                    for e2 in range(2):
                        TR(pbb[:, e2 * 128:(e2 + 1) * 128], onb[:, e2 * 128:(e2 + 1) * 128], identb, ["onb", "cbf"], [PK[6 + qt]])
                    evac(qt, obT[:, 2 * h:2 * h + 2, tcol], pbb[:, 0:256].rearrange("p (a b) -> p a b", b=128),
                         [PK[6 + qt]], ["obT"])
        R.barrier()

        if stop <= 5:
            break
        mT = arena_t[:, RAw + 4096:RAw + 4096 + 8192].bitcast(BF16).rearrange("p (a b) -> p a b", b=TOK)
        A.top = S2TOP
        wab = [A.alloc([2, 8, 512], BF16) for _ in range(2)]
        gat = [a1(TOK) for _ in range(2)]
        gbt = [a1(TOK) for _ in range(2)]
        m1 = a1(TOK)
        m2 = a1(TOK)
        wa_l = w_a[li].rearrange("(kc p) c -> p kc c", p=128)
        wb_l = w_b[li].rearrange("(kc p) c -> p kc c", p=128)
        for j in range(KC):
            g = j // 4
            s = g % 2
            wk = "wab%d" % s
            if j % 4 == 0:
                wload(wab[s][:, 0, :, :], wa_l[:, :, g * 512:(g + 1) * 512], wk)
                wload(wab[s][:, 1, :, :], wb_l[:, :, g * 512:(g + 1) * 512], wk)
            jc = slice((j % 4) * 128, (j % 4 + 1) * 128)
            pb0 = (j % 2) * 4
            ga_, gb_ = gat[j % 2], gbt[j % 2]
            gak, gbk = "gat%d" % (j % 2), "gbt%d" % (j % 2)
            LD(ga_, proj_fm[1024 + j * 128:1024 + (j + 1) * 128, :], [("pfm", 8 + j)], [gak], gak)
            LD(gb_, proj_fm[3072 + j * 128:3072 + (j + 1) * 128, :], [("pfm", 24 + j)], [gbk], gbk)
            ACT(ga_, ga_, AF.Sigmoid, [gak], [gak])
            ACT(gb_, gb_, AF.Sigmoid, [gbk], [gbk])
            for br, srcT, sk_ in ((0, oaT, "qsT"), (1, obT, "obT")):
                for hf in range(2):
                    pi = pb0 + br * 2 + hf
                    for kc in range(8):
                        MM(ps[pi][:, :], wab[s][:, br, kc, jc], srcT[:, kc, hf * 512:(hf + 1) * 512], kc == 0, kc == 7,
                           [wk, sk_], [PK[pi]])
            for hf in range(2):
                cs = slice(hf * 512, (hf + 1) * 512)
                TT("dve", m1[:, cs], ga_[:, cs], ps[pb0 + hf][:, :], ALU.mult, [gak, PK[pb0 + hf]], ["m1"])
                TT("dve", m2[:, cs], gb_[:, cs], ps[pb0 + 2 + hf][:, :], ALU.mult, [gbk, PK[pb0 + 2 + hf]], ["m2"])
            TT("pool", mT[:, j, :], m1, m2, ALU.add, ["m1", "m2"], ["mT"])
        R.barrier()

        if stop <= 6:
            break
        A.top = qT_off + 4096
        wo = [A.alloc([KC, 512], BF16) for _ in range(2)]
        zT = A.alloc([KC, 512], F32)
        sq6 = [a1(512, BF16) for _ in range(2)]
        rstd6 = a1(512)
        tmp6 = a1(512)
        wo_l = w_o[li].rearrange("(kc p) c -> p kc c", p=128)
        widx = 0
        for th in range(2):
            tcs = slice(th * 512, (th + 1) * 512)
            for jo in range(KC):
                g = jo // 4
                if jo % 4 == 0:
                    s = widx % 2
                    widx += 1
                    wk = "wo%d" % s
                    wload(wo[s], wo_l[:, :, g * 512:(g + 1) * 512], wk)
                pi = jo % 4
                for kc in range(KC):
                    MM(ps[pi][:, :], wo[s][:, kc, (jo % 4) * 128:(jo % 4 + 1) * 128], mT[:, kc, tcs], kc == 0, kc == KC - 1,
                       [wk, "mT"], [PK[pi]])
                CP("dve", zT[:, jo, :], ps[pi][:, :], [PK[pi]], ["zT"])
                sb = sq6[jo % 2]
                sk = "sq6%d" % (jo % 2)
                ACT(sb, ps[pi][:, :], AF.Square, [PK[pi]], [sk])
                MM(ps[6][:, :], onesb, sb, jo == 0, jo == KC - 1, [sk, "cbf"], [PK[6]])
            rstd_finish(1, float(D), NORM_EPS, rstd6, "n6")
            for c in range(KC):
                STT("dve", tmp6, zT[:, c, :], nwT[:, li * 4 + 1, c:c + 1], rstd6, ALU.mult, ALU.mult,
                    ["zT", "nwT", "n6rstd"], ["tmp6"])
                TT("pool", xT[:, c, tcs], xT[:, c, tcs], tmp6, ALU.add, ["tmp6", "xT"], ["xT"])
        R.barrier()
        A.top = PERSIST

        if stop <= 7:
            break
        sqb = [a1(TOK, BF16) for _ in range(2)]
        rstd = a1(TOK)
        ovl = A.top
        h2T = A.alloc([KC, TOK], BF16)
        rms_rstd(lambda c: (xT[:, c, :], ["xT"]), KC, float(D), NORM_EPS, sqb, rstd, "n7", 0, TOK)
        for c in range(KC):
            STT("dve", h2T[:, c, :], xT[:, c, :], nwT[:, li * 4 + 2, c:c + 1], rstd,
                ALU.mult, ALU.mult, ["xT", "nwT", "n7rstd"], ["h2T"])
        wgu = [A.alloc([2, KC, 256], BF16) for _ in range(2)]
        sgt = [a1(TOK) for _ in range(2)]
        ast = [a1(TOK, BF16) for _ in range(2)]
        wf1_l = w_f1[li].rearrange("(kc p) c -> p kc c", p=128)
        wf2_l = w_f2[li].rearrange("(kc p) c -> p kc c", p=128)
        widx = 0
        for j in range(FC):
            g = j // 2
            if j % 2 == 0:
                s = widx % 2
                widx += 1
                wk = "wgu%d" % s
                wload(wgu[s][:, 0, :, :], wf1_l[:, :, g * 256:(g + 1) * 256], wk)
                wload(wgu[s][:, 1, :, :], wf1_l[:, :, FFH + g * 256:FFH + (g + 1) * 256], wk)
            jc = slice((j % 2) * 128, (j % 2 + 1) * 128)
            pb0 = (j % 2) * 4
            for hf in range(2):
                tcs = slice(hf * 512, (hf + 1) * 512)
                for kc in range(KC):
                    MM(ps[pb0 + hf][:, :], wgu[s][:, 0, kc, jc], h2T[:, kc, tcs], kc == 0, kc == KC - 1,
                       [wk, "h2T"], [PK[pb0 + hf]])
            for hf in range(2):
                tcs = slice(hf * 512, (hf + 1) * 512)
                for kc in range(KC):
                    MM(ps[pb0 + 2 + hf][:, :], wgu[s][:, 1, kc, jc], h2T[:, kc, tcs], kc == 0, kc == KC - 1,
                       [wk, "h2T"], [PK[pb0 + 2 + hf]])
            sg_ = sgt[j % 2]
            sgk = "sgt%d" % (j % 2)
            as_ = ast[j % 2]
            ask = "ast%d" % (j % 2)
            for hf in range(2):
                tcs = slice(hf * 512, (hf + 1) * 512)
                ACT(sg_[:, tcs], ps[pb0 + hf][:, :], AF.Silu, [PK[pb0 + hf]], [sgk])
                TT("dve", as_[:, tcs], sg_[:, tcs], ps[pb0 + 2 + hf][:, :], ALU.mult, [sgk, PK[pb0 + 2 + hf]], [ask])
            LD(a_scr[j * 128:(j + 1) * 128, :], as_, [ask], ["a_scr"], ask)
        R.barrier()
        A.top = ovl
        aT = A.alloc([FC, 512], BF16)
        w2 = [A.alloc([FC, 128], BF16) for _ in range(2)]
        y2 = A.alloc([KC, 512], F32)
        rstd8 = a1(512)
        tmp8 = a1(512)
        w2idx = 0
        for th in range(2):
            tcs = slice(th * 512, (th + 1) * 512)
            LD(aT, a_scr.rearrange("(j p) t -> p j t", p=128)[:, :, tcs], ["a_scr"], ["aT"], "aT")
            for jo in range(KC):
                s2 = w2idx % 2
                w2idx += 1
                wk2 = "w2%d" % s2
                wload(w2[s2], wf2_l[:, :, jo * 128:(jo + 1) * 128], wk2)
                pi = 4 + (jo % 2)
                for kc in range(FC):
                    MM(ps[pi][:, :], w2[s2][:, kc, :], aT[:, kc, :], kc == 0, kc == FC - 1, [wk2, "aT"], [PK[pi]])
                CP("dve", y2[:, jo, :], ps[pi][:, :], [PK[pi]], ["y2"])
                sb = sqb[jo % 2]
                sk = "sq7%d" % (jo % 2)
                ACT(sb[:, 0:512], ps[pi][:, :], AF.Square, [PK[pi]], [sk])
                MM(ps[6][:, :], onesb, sb[:, 0:512], jo == 0, jo == KC - 1, [sk, "cbf"], [PK[6]])
            rstd_finish(1, float(D), NORM_EPS, rstd8, "n8")
            for c in range(KC):
                STT("dve", tmp8, y2[:, c, :], nwT[:, li * 4 + 3, c:c + 1], rstd8, ALU.mult, ALU.mult,
                    ["y2", "nwT", "n8rstd"], ["tmp8"])
                TT("pool", xT[:, c, tcs], xT[:, c, tcs], tmp8, ALU.add, ["tmp8", "xT"], ["xT"])
        R.barrier()

    if dbg:
        for nm in dbg:
            t_ = {"oaT": lambda: oaT, "obT": lambda: obT, "mT": lambda: mT, "qT": lambda: qT}[nm]()
            dd = nc.dram_tensor("dbg_" + nm, [128, t_.shape[1], t_.shape[2]], BF16, kind="ExternalOutput").ap()
            LD(dd, t_, [], ["dbg" + nm], "dbg")
        R.barrier()
    if mode != "A":
        A.top = PERSIST
        ost = [a1(512) for _ in range(4)]
        oi = 0
        for i in range(NT):
            for g in range(4):
                pi = oi % 4
                for j in range(4):
                    kc = g * 4 + j
                    TR(ps[pi][:, j * 128:(j + 1) * 128], xT[:, kc, i * 128:(i + 1) * 128], identf, ["xT", "cmat"], [PK[pi]])
                sb = ost[oi % 4]
                sk = "ost%d" % (oi % 4)
                evac(oi, sb, ps[pi][:, :], [PK[pi]], [sk])
                LD(x_out[i * 128:(i + 1) * 128, g * 512:(g + 1) * 512], sb, [sk], ["x_out"], sk)
                oi += 1
    R.barrier()
    with nc.Block() as block:
        R.emit(block)
    es.close()
    return nc


def _consts(core):
    p = np.arange(128)
    ident = np.eye(128, dtype=np.float32)
    same = (p[:, None] // 64) == (p[None, :] // 64)
    LBD = (same & (p[:, None] <= p[None, :])).astype(np.float32)
    UBD = (same & (p[:, None] > p[None, :])).astype(np.float32)
    LF = (p[:, None] <= p[None, :]).astype(np.float32)
    ONES = np.ones((128, 128), np.float32)
    cmat = np.stack([ident, LBD, UBD, LF, ONES], axis=1).astype(np.float32)
    ind = np.stack([(p < 64), (p >= 64)], axis=1).astype(np.float32)
    kt = np.arange(64)
    relv = (2 * kt[None, :] + (p[:, None] // 64) - 16 * core).astype(np.float32)
    qcrow = np.broadcast_to((np.arange(TOK) // 64).astype(np.float32)[None, :], (128, TOK)).copy()
    m = (np.arange(8) < core).astype(np.float32)
    mcore = np.broadcast_to(np.concatenate([m, 1.0 - m])[None, :], (128, 16)).astype(np.float32).copy()
    pos = np.arange(core * TOK, (core + 1) * TOK, dtype=np.float32)
    inv_freq = (np.float32(500000.0) ** (-np.arange(0, 32, 2, dtype=np.float32) / np.float32(32))).astype(np.float32)
    ang = (pos[:, None] * inv_freq[None, :]).astype(np.float32)
    return dict(cmat=cmat, ind=ind, relv=relv, qcrow=qcrow, mcore=mcore,
                rope_cos=np.cos(ang).astype(np.float32), rope_sin=np.sin(ang).astype(np.float32))


_PROGS = {}


def _prog(mode, layers):
    key = (mode, tuple(layers))
    if key not in _PROGS:
        _PROGS[key] = build_program(mode, list(layers))
    return _PROGS[key]


WNAMES = ["w_in", "hgrn_gnorm_w", "diff_lambda", "diff_subln_w", "w_branch_a", "w_branch_b", "w_out",
          "norm_w", "w_ffn_in", "w_ffn_out"]

FUSED = False


def kernel(**inputs):
    inp = {k: np.ascontiguousarray(np.asarray(v)) for k, v in inputs.items()}
    x = inp["x"].reshape(SEQ, D)
    consts = [_consts(c) for c in range(NCORES)]
    xs = [np.ascontiguousarray(x[c * TOK:(c + 1) * TOK]) for c in range(NCORES)]
    if FUSED:
        nc = _prog("fused", range(DEPTH))
        in_maps = []
        for c in range(NCORES):
            m = dict(consts[c])
            m["x_loc"] = xs[c]
            m["hgrn_lower_bounds"] = inp["hgrn_lower_bounds"]
            for n in WNAMES:
                m[n] = inp[n]
            in_maps.append(m)
        res = run_bass_kernel_spmd(nc, in_maps, core_ids=list(range(NCORES)))
        out = np.concatenate([r["x_out"] for r in res.results], axis=0)
        return out.reshape(1, SEQ, D).astype(np.float32)
    for l in range(DEPTH):
        wl = {n: np.ascontiguousarray(inp[n][l:l + 1]) for n in WNAMES}
        base = []
        for c in range(NCORES):
            m = dict(consts[c])
            m["x_loc"] = xs[c]
            m["hgrn_lower_bounds"] = inp["hgrn_lower_bounds"]
            m.update(wl)
            base.append(m)
        ncA = _prog("A", [l])
        resA = run_bass_kernel_spmd(ncA, base, core_ids=list(range(NCORES)))
        kT_all = np.concatenate([r["kT_loc"] for r in resA.results], axis=0)
        v_all = np.concatenate([r["v_loc"] for r in resA.results], axis=0)
        S_all = np.concatenate([r["S_loc"] for r in resA.results], axis=0)
        D_all = np.concatenate([r["D_loc"] for r in resA.results], axis=0)
        ncB = _prog("B", [l])
        inB = []
        for c in range(NCORES):
            m = dict(base[c])
            m.update(kT_all=kT_all, v_all=v_all, S_all=S_all, D_all=D_all)
            for nm in ("c_qsT", "c_ozT", "c_qT"):
                m[nm] = resA.results[c][nm]
            inB.append(m)
        resB = run_bass_kernel_spmd(ncB, inB, core_ids=list(range(NCORES)))
        xs = [np.ascontiguousarray(r["x_out"]) for r in resB.results]
    out = np.concatenate(xs, axis=0)
    return out.reshape(1, SEQ, D).astype(np.float32)
```

```python
import math
from contextlib import ExitStack
import numpy as np
import concourse.bass as bass
import concourse.mybir as mybir
from concourse.bass_utils import run_bass_kernel_spmd

F32 = mybir.dt.float32
BF16 = mybir.dt.bfloat16
AF = mybir.ActivationFunctionType
ALU = mybir.AluOpType
AX = mybir.AxisListType

NCORES = 8
DEPTH = 4
D = 2048
SEQ = 8192
TOK = SEQ // NCORES
NT = TOK // 128
KC = D // 128
INW = 11264
FFH = 5632
FC = FFH // 128
NORM_EPS = 1e-6
SUBLN_EPS = 1e-5
SAME_ENGINE_SYNC = True


class Rec:
    ENG = ["pe", "act", "dve", "pool", "sp"]

    def __init__(self, nc, es):
        self.nc = nc
        self.es = es
        self.ops = {e: [] for e in self.ENG}
        self.cnt = {e: 0 for e in self.ENG}
        self.esem = {e: es.enter_context(nc.semaphore("S_" + e)) for e in self.ENG}
        self.seen = {e: {} for e in self.ENG}
        self.lastw = {}
        self.readers = {}
        self.dsem = {}

    def _dsem(self, name):
        if name not in self.dsem:
            self.dsem[name] = [self.es.enter_context(self.nc.semaphore("D_" + name)), 0]
        return self.dsem[name]

    def _need(self, eng, tok, waits):
        kind, key, val = tok
        if kind == "E":
            if key == eng and (eng in ("pe", "sp") or not SAME_ENGINE_SYNC):
                return
            sem = self.esem[key]
        else:
            sem = self.dsem[key][0]
        sk = (kind, key)
        if self.seen[eng].get(sk, 0) >= val:
            return
        self.seen[eng][sk] = val
        waits.append((sem, val))

    def _deps(self, eng, reads, writes):
        waits = []
        for k in reads:
            t = self.lastw.get(k)
            if t is not None:
                self._need(eng, t, waits)
        for k in writes:
            t = self.lastw.get(k)
            if t is not None:
                self._need(eng, t, waits)
            for sk, v in self.readers.get(k, {}).items():
                self._need(eng, (sk[0], sk[1], v), waits)
        return waits

    def _commit(self, tok, reads, writes):
        sk = (tok[0], tok[1])
        for k in reads:
            d = self.readers.setdefault(k, {})
            if d.get(sk, 0) < tok[2]:
                d[sk] = tok[2]
        for k in writes:
            self.lastw[k] = tok
            self.readers[k] = {}

    @staticmethod
    def _excl(reads, writes):
        r2 = [k for k in reads if not (isinstance(k, str) and k.startswith("ps"))]
        w2 = list(writes) + [k for k in reads if isinstance(k, str) and k.startswith("ps") and k not in writes]
        return r2, w2

    def op(self, eng, fn, reads=(), writes=()):
        reads, writes = self._excl(reads, writes)
        waits = self._deps(eng, reads, writes)
        self.cnt[eng] += 1
        tok = ("E", eng, self.cnt[eng])
        self._commit(tok, reads, writes)
        self.ops[eng].append((waits, fn, (self.esem[eng], 1)))

    def dma(self, eng, fn, reads, writes, sem):
        ds = self._dsem(sem)
        waits = self._deps(eng, reads, writes)
        ds[1] += 16
        tok = ("D", sem, ds[1])
        self._commit(tok, reads, writes)
        self.ops[eng].append((waits, fn, (ds[0], 16)))

    def barrier(self):
        for e in self.ENG:
            waits = []
            for f in self.ENG:
                if f != e and self.cnt[f] > 0:
                    self._need(e, ("E", f, self.cnt[f]), waits)
            if e not in ("pe", "sp") and self.cnt[e] > 0 and SAME_ENGINE_SYNC:
                self._need(e, ("E", e, self.cnt[e]), waits)
            for name, (h, tot) in self.dsem.items():
                if tot > 0:
                    self._need(e, ("D", name, tot), waits)
            if waits:
                self.ops[e].append((waits, None, None))
        self.lastw = {}
        self.readers = {}

    def emit(self, block):
        for eng, attr in [("pe", "tensor"), ("act", "scalar"), ("dve", "vector"),
                          ("pool", "gpsimd"), ("sp", "sync")]:
            ops = self.ops[eng]

            def body(e, ops=ops):
                for waits, fn, inc in ops:
                    for sem, v in waits:
                        e.wait_ge(sem, v)
                    if fn is not None:
                        ins = fn(e)
                        ins.then_inc(inc[0], inc[1])

            getattr(block, attr)(body)


class Arena:
    def __init__(self, ap, words):
        self.ap = ap
        self.words = words
        self.top = 0

    def alloc(self, shape, dtype):
        n = int(np.prod(shape))
        w = n if dtype == F32 else (n + 1) // 2
        w = (w + 15) // 16 * 16
        assert self.top + w <= self.words, ("arena overflow", self.top, w, self.words)
        v = self.ap[:, self.top:self.top + w]
        self.top += w
        if dtype != F32:
            v = v.bitcast(dtype)
        v = v[:, 0:n]
        if len(shape) == 2:
            v = v.rearrange("p (a b) -> p a b", b=shape[1])
        elif len(shape) == 3:
            v = v.rearrange("p (a b c) -> p a b c", b=shape[1], c=shape[2])
        return v


def build_program(mode, layers, dbg=None, stop=99):
    nc = bass.Bass("TRN2", target_bir_lowering=False)
    es = ExitStack()
    nl = len(layers)

    def din(name, shape, dt=F32):
        return nc.dram_tensor(name, list(shape), dt, kind="ExternalInput").ap()

    def dscr(name, shape, dt=F32, kind="Internal"):
        return nc.dram_tensor(name, list(shape), dt, kind=kind).ap()

    x_in = din("x_loc", [TOK, D])
    w_in = din("w_in", [nl, D, INW])
    hlb = din("hgrn_lower_bounds", [DEPTH, 1024])
    gnw = din("hgrn_gnorm_w", [nl, 128])
    dlam = din("diff_lambda", [nl, 4, 128])
    slw = din("diff_subln_w", [nl, 256])
    w_a = din("w_branch_a", [nl, 1024, D])
    w_b = din("w_branch_b", [nl, 1024, D])
    w_o = din("w_out", [nl, D, D])
    nrm = din("norm_w", [nl, 4, D])
    w_f1 = din("w_ffn_in", [nl, D, 2 * FFH])
    w_f2 = din("w_ffn_out", [nl, FFH, D])
    cos_in = din("rope_cos", [TOK, 16])
    sin_in = din("rope_sin", [TOK, 16])
    cmat_in = din("cmat", [128, 5, 128])
    ind_in = din("ind", [128, 2])
    relv_in = din("relv", [128, 64])
    qcrow_in = din("qcrow", [128, TOK])
    mcore_in = din("mcore", [128, 16])
    x_out = None
    if mode != "A":
        x_out = nc.dram_tensor("x_out", [TOK, D], F32, kind="ExternalOutput").ap()

    proj_tm = dscr("proj_tm", [TOK, 6144])
    proj_fm = dscr("proj_fm", [5120, TOK])
    a_scr = dscr("a_scr", [FFH, TOK], BF16)
    exk = "ExternalOutput" if mode == "A" else "Internal"
    axk = "ExternalInput" if mode == "B" else "Internal"
    EXB = []
    for par in range(2 if mode == "fused" else 1):
        sfx = "" if par == 0 else "_p1"
        EXB.append((
            dscr("kT_loc" + sfx, [1024, TOK], BF16, kind=exk),
            dscr("v_loc" + sfx, [TOK, 1024], BF16, kind=exk),
            dscr("S_loc" + sfx, [1024, 128], F32, kind=exk),
            dscr("D_loc" + sfx, [128, 8], F32, kind=exk),
            dscr("kT_all" + sfx, [NCORES * 1024, TOK], BF16, kind=axk),
            dscr("v_all" + sfx, [SEQ, 1024], BF16, kind=axk),
            dscr("S_all" + sfx, [NCORES * 1024, 128], F32, kind=axk),
            dscr("D_all" + sfx, [NCORES * 128, 8], F32, kind=axk)))

    carry = None
    if mode in ("A", "B"):
        ck = "ExternalOutput" if mode == "A" else "ExternalInput"
        carry = (dscr("c_qsT", [128, 8, TOK], BF16, kind=ck), dscr("c_ozT", [128, 8, TOK], F32, kind=ck),
                 dscr("c_qT", [128, 8, TOK], BF16, kind=ck))
    ARENA_WORDS = 52480
    arena_t = es.enter_context(nc.sbuf_tensor("arena", [128, ARENA_WORDS], F32))
    ps = [es.enter_context(nc.psum_tensor("ps%d" % i, [128, 512], F32)) for i in range(8)]
    R = Rec(nc, es)
    A = Arena(arena_t[:, :], ARENA_WORDS)
    PK = ["ps%d" % i for i in range(8)]

    def a1(n, dt=F32):
        return A.alloc([1, n], dt)[:, 0, :]

    def ACT(out, in_, func, reads, writes, **kw):
        R.op("act", lambda e: e.activation(out=out, in_=in_, func=func, **kw), reads, writes)

    def TT(eng, out, a, b, op, reads, writes):
        R.op(eng, lambda e: e.tensor_tensor(out, a, b, op), reads, writes)

    def TS(eng, out, a, s1, s2, op0, op1, reads, writes):
        R.op(eng, lambda e: e.tensor_scalar(out, a, s1, s2, op0, op1), reads, writes)

    def TS1(eng, out, a, s1, op, reads, writes):
        R.op(eng, lambda e: e.tensor_single_scalar(out, a, s1, op), reads, writes)

    def STT(eng, out, a, s, b, op0, op1, reads, writes):
        R.op(eng, lambda e: e.scalar_tensor_tensor(out=out, in0=a, scalar=s, in1=b, op0=op0, op1=op1),
             reads, writes)

    def CP(eng, out, in_, reads, writes):
        if eng == "act":
            R.op("act", lambda e: e.activation(out=out, in_=in_, func=AF.Copy), reads, writes)
        else:
            R.op(eng, lambda e: e.tensor_copy(out, in_), reads, writes)

    def MM(out, lhsT, rhs, start, stop, reads, writes):
        R.op("pe", lambda e: e.matmul(out, lhsT, rhs, start=start, stop=stop), reads, writes)

    def TR(out, in_, ident, reads, writes):
        R.op("pe", lambda e: e.transpose(out, in_, ident), reads, writes)

    def MEMSET(eng, out, val, writes):
        R.op(eng, lambda e: e.memset(out, val), [], writes)

    def RECIP(out, in_, reads, writes):
        R.op("dve", lambda e: e.reciprocal(out, in_), reads, writes)

    def LD(out, in_, reads, writes, sem, eng="sp", nc_ok=False):
        if nc_ok:
            R.dma(eng, lambda e: e.dma_start(out=out, in_=in_, allow_slow_non_contiguous=True), reads, writes, sem)
        else:
            R.dma(eng, lambda e: e.dma_start(out=out, in_=in_), reads, writes, sem)

    xT = A.alloc([KC, TOK], F32)
    cmat = A.alloc([5, 128], F32)
    identf, LBD, UBD, LFULL, ONESF = (cmat[:, i, :] for i in range(5))
    cbf = A.alloc([3, 128], BF16)
    identb, maskbd, onesb = (cbf[:, i, :] for i in range(3))
    ind = a1(2)
    relv = a1(64)
    mcore = a1(16)
    nwT = A.alloc([nl * 4, KC], F32)
    gnwT = a1(8)
    epsc = a1(2)
    cosT = A.alloc([NT, 16], F32)
    sinT = A.alloc([NT, 16], F32)
    qcb = a1(TOK, BF16)
    PERSIST = A.top

    LD(cmat, cmat_in, [], ["cmat"], "k_cmat")
    LD(ind, ind_in, [], ["ind"], "k_ind")
    LD(relv, relv_in, [], ["relv"], "k_relv")
    LD(mcore, mcore_in, [], ["mcore"], "k_mcore")
    for i4 in range(nl * 4):
        LD(nwT[:, i4, :], nrm[i4 // 4, i4 % 4].rearrange("(kc p) -> p kc", p=128), [], ["nwT"], "k_nwT", nc_ok=True)
    LD(gnwT[:, 0:nl], gnw.rearrange("l p -> p l"), [], ["gnwT"], "k_gnwT", nc_ok=True)
    LD(cosT, cos_in.rearrange("(i p) f -> p i f", p=128), [], ["cosT"], "k_cosT")
    LD(sinT, sin_in.rearrange("(i p) f -> p i f", p=128), [], ["sinT"], "k_sinT")
    CP("dve", identb, identf, ["cmat"], ["cbf"])
    CP("dve", maskbd, LBD, ["cmat"], ["cbf"])
    CP("dve", onesb, ONESF, ["cmat"], ["cbf"])
    MEMSET("pool", epsc[:, 0:1], NORM_EPS, ["epsc"])
    MEMSET("pool", epsc[:, 1:2], SUBLN_EPS, ["epsc"])

    def eps_ap(eps):
        return epsc[:, 0:1] if eps == NORM_EPS else epsc[:, 1:2]

    xin_t = [a1(D) for _ in range(2)]
    qcf = a1(TOK)
    LD(qcf, qcrow_in, [], ["qcf"], "k_qcf")
    CP("dve", qcb, qcf, ["qcf"], ["qcb"])
    for i in range(NT):
        b = xin_t[i % 2]
        bk = "xin%d" % (i % 2)
        LD(b, x_in[i * 128:(i + 1) * 128, :], [], [bk], bk)
        for g in range(4):
            pi = (i * 4 + g) % 4
            for j in range(4):
                kc = g * 4 + j
                TR(ps[pi][:, j * 128:(j + 1) * 128], b[:, kc * 128:(kc + 1) * 128], identf, [bk, "cmat"], [PK[pi]])
            CP("act" if g % 2 else "dve", xT[:, g * 4:(g + 1) * 4, i * 128:(i + 1) * 128],
               ps[pi][:, :].rearrange("p (a b) -> p a b", b=128), [PK[pi]], ["xT"])
    R.barrier()
    A.top = PERSIST

    def rms_rstd(src, nchunks, ndiv, eps, sqb, rstd, key, t0, tn):
        nh = tn // 512
        for c in range(nchunks):
            sb = sqb[c % 2]
            sk = key + "sq%d" % (c % 2)
            ap, rk = src(c)
            ACT(sb[:, 0:tn], ap, AF.Square, rk, [sk])
            for h in range(nh):
                MM(ps[6 + h][:, :], onesb, sb[:, h * 512:(h + 1) * 512], c == 0, c == nchunks - 1,
                   [sk, "cbf"], [PK[6 + h]])
        rstd_finish(nh, ndiv, eps, rstd, key)

    def rstd_finish(nh, ndiv, eps, rstd, key):
        for h in range(nh):
            ACT(rstd[:, h * 512:(h + 1) * 512], ps[6 + h][:, :], AF.Sqrt, [PK[6 + h], "epsc"], [key + "rstd"],
                bias=eps_ap(eps), scale=1.0 / ndiv)
        RECIP(rstd[:, 0:nh * 512], rstd[:, 0:nh * 512], [key + "rstd"], [key + "rstd"])

    def wload(dst, src, key):
        R.dma("pool", lambda e: e.dma_start(out=dst, in_=src), [], [key], key)

    def evac(idx, out, in_, reads, writes):
        CP("act" if idx % 2 else "dve", out, in_, reads, writes)

    for li, l in enumerate(layers):
        lam_init = 0.8 - 0.6 * math.exp(-0.3 * l)
        A.top = PERSIST
        kT_loc, v_loc, S_loc, D_loc, kT_all, v_all, S_all, D_all = EXB[li % len(EXB)]

        hT = A.alloc([KC, TOK], BF16)
        sqb = [a1(TOK, BF16) for _ in range(2)]
        rstd = a1(TOK)
        rms_rstd(lambda c: (xT[:, c, :], ["xT"]), KC, float(D), NORM_EPS, sqb, rstd, "n1", 0, TOK)
        for c in range(KC):
            STT("dve", hT[:, c, :], xT[:, c, :], nwT[:, li * 4 + 0, c:c + 1], rstd,
                ALU.mult, ALU.mult, ["xT", "nwT", "n1rstd"], ["hT"])
        wbuf = [A.alloc([KC, 512], BF16) for _ in range(2)]
        stg = [a1(512) for _ in range(4)]
        w_l = w_in[li].rearrange("(kc p) c -> p kc c", p=128)
        tm_groups = [0, 1, 2, 3, 4, 5, 8, 9, 10, 11, 12, 13]
        fm_groups = [6, 7, 14, 15, 16, 17, 18, 19, 20, 21]
        order = [("tm", g) for g in tm_groups] if mode != "B" else []
        if mode != "A":
            order += [("fm", g) for g in fm_groups]
        sidx = 0
        pidx = 0
        for gi, (kind, g) in enumerate(order):
            wb = wbuf[gi % 2]
            wk = "wbuf%d" % (gi % 2)
            wload(wb, w_l[:, :, g * 512:(g + 1) * 512], wk)
            if kind == "tm":
                tcol = tm_groups.index(g) * 512
                for i in range(NT):
                    pi = pidx % 4
                    pidx += 1
                    for kc in range(KC):
                        MM(ps[pi][:, :], hT[:, kc, i * 128:(i + 1) * 128], wb[:, kc, :], kc == 0, kc == KC - 1,
                           ["hT", wk], [PK[pi]])
                    sb = stg[sidx % 4]
                    sk = "stg%d" % (sidx % 4)
                    evac(sidx, sb, ps[pi][:, :], [PK[pi]], [sk])
                    LD(proj_tm[i * 128:(i + 1) * 128, tcol:tcol + 512], sb, [sk], [("ptm", tcol // 1024, i)], sk)
                    sidx += 1
            else:
                frow0 = fm_groups.index(g) * 512
                for j in range(4):
                    for h in range(2):
                        pi = pidx % 4
                        pidx += 1
                        for kc in range(KC):
                            MM(ps[pi][:, :], wb[:, kc, j * 128:(j + 1) * 128], hT[:, kc, h * 512:(h + 1) * 512],
                               kc == 0, kc == KC - 1, ["hT", wk], [PK[pi]])
                        sb = stg[sidx % 4]
                        sk = "stg%d" % (sidx % 4)
                        evac(sidx, sb, ps[pi][:, :], [PK[pi]], [sk])
                        r0 = frow0 + j * 128
                        LD(proj_fm[r0:r0 + 128, h * 512:(h + 1) * 512], sb, [sk], [("pfm", r0 // 128)], sk)
                        sidx += 1
        R.barrier()
        A.top = PERSIST

        qsT = A.alloc([8, TOK], BF16)
        RAw = A.top
        ozT = A.alloc([8, TOK], F32)
        qT_off = A.top
        qT = A.alloc([8, TOK], BF16)
        S2TOP = A.top
        if mode == "B":
            LD(qsT, carry[0], [], ["qsT"], "k_cq")
            LD(ozT, carry[1], [], ["ozT"], "k_co")
            LD(qT, carry[2], [], ["qT"], "k_cT")
            R.barrier()
        else:
            qhT = A.alloc([8, 128], BF16)
            ktT = A.alloc([8, 128], BF16)
            Vt = a1(1024, BF16)
            Kh = a1(1024, BF16)
            glast = A.alloc([8, 16], F32)
            dec = A.alloc([8, 2], F32)
            dtot = a1(8)
            lb = a1(1024)
            oml = a1(1024)
            lbmark = A.top
            lbraw = A.alloc([4, 1024], F32)
            LD(lbraw, hlb.partition_broadcast(128), [], ["lbraw"], "k_lbraw")
            ACT(lbraw, lbraw, AF.Exp, ["lbraw"], ["lbraw"])
            TT("dve", lb, lbraw[:, 0, :], lbraw[:, 1, :], ALU.add, ["lbraw"], ["lb"])
            TT("dve", lb, lb, lbraw[:, 2, :], ALU.add, ["lbraw", "lb"], ["lb"])
            TT("dve", lb, lb, lbraw[:, 3, :], ALU.add, ["lbraw", "lb"], ["lb"])
            RECIP(oml, lb, ["lb"], ["oml"])
            if l == 0:
                MEMSET("dve", lb, 0.0, ["lb"])
            else:
                CP("dve", lb, lbraw[:, 1, :], ["lbraw", "oml"], ["lb"])
                for j in range(2, l + 1):
                    TT("dve", lb, lb, lbraw[:, j, :], ALU.add, ["lbraw", "lb"], ["lb"])
            TT("dve", lb, lb, oml, ALU.mult, ["lb", "oml"], ["lb"])
            TS("dve", oml, lb, -1.0, 1.0, ALU.mult, ALU.add, ["lb"], ["oml"])
            R.barrier()
            A.top = lbmark
            Tacc = a1(1024)
            tq = a1(1024)
            tf = a1(1024)
            ti = a1(1024)
            qf = a1(1024)
            lf = a1(1024)
            kk = a1(1024)
            eb = [a1(1024) for _ in range(2)]
            tb = [a1(1024, BF16) for _ in range(3)]
            Am8 = A.alloc([8, 128], BF16)
            Sb2 = A.alloc([8, 128], BF16)
            Sst = A.alloc([8, 128], F32)
            Sbf = A.alloc([8, 128], BF16)
            MEMSET("pool", Tacc, 0.0, ["Tacc"])
            MEMSET("pool", Sst, 0.0, [("Sst", h) for h in range(8)])

            for i in range(NT):
                rows = slice(i * 128, (i + 1) * 128)
                tsl = slice(i * 128, (i + 1) * 128)
                LD(tq, proj_tm[rows, 0:1024], [("ptm", 0, i)], ["tq"], "tq")
                LD(tf, proj_tm[rows, 1024:2048], [("ptm", 1, i)], ["tf"], "tf")
                LD(ti, proj_tm[rows, 2048:3072], [("ptm", 2, i)], ["ti"], "ti")
                ACT(qf, tq, AF.Silu, ["tq"], ["qf"])
                ACT(tf, tf, AF.Sigmoid, ["tf"], ["tf"])
                CP("pool", Vt, ti, ["ti"], ["Vt"])
                TT("dve", tf, tf, oml, ALU.mult, ["tf", "oml"], ["tf"])
                TT("dve", tf, tf, lb, ALU.add, ["tf", "lb"], ["tf"])
                ACT(lf, tf, AF.Ln, ["tf"], ["lf"])
                TS("dve", kk, tf, -1.0, 1.0, ALU.mult, ALU.add, ["tf"], ["kk"])
                for h in range(8):
                    MM(ps[4][:, h * 2:h * 2 + 2], lf[:, h * 128:(h + 1) * 128], ind, True, True, ["lf", "ind"], [PK[4]])
                CP("dve", glast[:, :, 2 * i:2 * i + 2], ps[4][:, 0:16].rearrange("p (h c) -> p h c", c=2),
                   [PK[4]], ["glast"])
                ACT(dec, glast[:, :, 2 * i:2 * i + 2], AF.Exp, ["glast"], ["dec"])
                for h2 in range(2):
                    cs = slice(h2 * 512, (h2 + 1) * 512)
                    MM(ps[0][:, :], LBD, lf[:, cs], True, True, ["lf", "cmat"], [PK[0]])
                    MM(ps[1][:, :], UBD, lf[:, cs], True, True, ["lf", "cmat"], [PK[1]])
                    MM(ps[2][:, :], LFULL, lf[:, cs], True, True, ["lf", "cmat"], [PK[2]])
                    MM(ps[3][:, :], ONESF, lf[:, cs], True, True, ["lf", "cmat"], [PK[3]])
                    ACT(eb[0][:, cs], ps[0][:, :], AF.Exp, [PK[0]], ["eb0"])
                    TT("dve", tb[0][:, cs], qf[:, cs], eb[0][:, cs], ALU.mult, ["qf", "eb0"], ["tb0"])
                    ACT(eb[1][:, cs], ps[0][:, :], AF.Exp, [PK[0]], ["eb1"], scale=-1.0)
                    TT("pool", tb[1][:, cs], kk[:, cs], eb[1][:, cs], ALU.mult, ["kk", "eb1"], ["tb1"])
                    ACT(eb[0][:, cs], ps[1][:, :], AF.Exp, [PK[1]], ["eb0"])
                    TT("dve", Kh[:, cs], kk[:, cs], eb[0][:, cs], ALU.mult, ["kk", "eb0"], ["Kh"])
                    TT("dve", eb[1][:, cs], ps[2][:, :], Tacc[:, cs], ALU.add, [PK[2], "Tacc"], ["eb1"])
                    ACT(eb[1][:, cs], eb[1][:, cs], AF.Exp, ["eb1"], ["eb1"])
                    TT("pool", tb[2][:, cs], qf[:, cs], eb[1][:, cs], ALU.mult, ["qf", "eb1"], ["tb2"])
                    TT("dve", Tacc[:, cs], Tacc[:, cs], ps[3][:, :], ALU.add, [PK[3], "Tacc"], ["Tacc"])
                tcount = 0
                for which in range(3):
                    for g in range(2):
                        pi = 5 + (tcount % 2)
                        tcount += 1
                        pbb = ps[pi][:, :].bitcast(BF16)
                        for j in range(4):
                            h = g * 4 + j
                            TR(pbb[:, j * 128:(j + 1) * 128], tb[which][:, h * 128:(h + 1) * 128], identb,
                               ["tb%d" % which, "cbf"], [PK[pi]])
                        src = pbb[:, 0:512].rearrange("p (a b) -> p a b", b=128)
                        if which == 0:
                            evac(g, qhT[:, g * 4:(g + 1) * 4, :], src, [PK[pi]], ["qhT"])
                        elif which == 1:
                            evac(g, ktT[:, g * 4:(g + 1) * 4, :], src, [PK[pi]], ["ktT"])
                        else:
                            evac(g, qsT[:, g * 4:(g + 1) * 4, tsl], src, [PK[pi]], ["qsT"])
                for h in range(8):
                    MM(ps[h // 4][:, (h % 4) * 128:(h % 4 + 1) * 128], ktT[:, h, :], qhT[:, h, :], True, True,
                       ["qhT", "ktT"], [PK[h // 4]])
                for h in range(8):
                    TT("dve", Am8[:, h, :], ps[h // 4][:, (h % 4) * 128:(h % 4 + 1) * 128], maskbd, ALU.mult,
                       [PK[h // 4], "cbf"], [("Am", h)])
                for h in range(8):
                    hs = slice(h * 128, (h + 1) * 128)
                    MM(ps[2 + h // 4][:, (h % 4) * 128:(h % 4 + 1) * 128], Kh[0:64, hs], Vt[0:64, hs], True, True,
                       ["Kh", "Vt"], [PK[2 + h // 4]])
                for h in range(8):
                    STT("dve", Sst[:, h, :], Sst[:, h, :], dec[:, h, 0:1], ps[2 + h // 4][:, (h % 4) * 128:(h % 4 + 1) * 128],
                        ALU.mult, ALU.add, [PK[2 + h // 4], "dec", ("Sst", h)], [("Sst", h)])
                    CP("pool", Sb2[:, h, :], Sst[:, h, :], [("Sst", h)], [("Sb2", h)])
                for h in range(8):
                    hs = slice(h * 128, (h + 1) * 128)
                    pb = 5 + h // 4
                    c0 = (h % 4) * 128
                    MM(ps[pb][:, c0:c0 + 128], Vt[:, hs], Am8[:, h, :], True, False, ["Vt", ("Am", h)], [PK[pb]])
                    if i > 0:
                        MM(ps[pb][:, c0:c0 + 64], Sbf[:, h, :], qhT[:, h, 0:64], False, False, [("Sbf", h), "qhT"], [PK[pb]])
                    MM(ps[pb][:, c0 + 64:c0 + 128], Sb2[:, h, :], qhT[:, h, 64:128], False, True, [("Sb2", h), "qhT"], [PK[pb]])
                for g in range(2):
                    CP("act", ozT[:, g * 4:(g + 1) * 4, tsl], ps[5 + g][:, :].rearrange("p (a b) -> p a b", b=128),
                       [PK[5 + g]], ["ozT"])
                for h in range(8):
                    hs = slice(h * 128, (h + 1) * 128)
                    MM(ps[2 + h // 4][:, (h % 4) * 128:(h % 4 + 1) * 128], Kh[64:128, hs], Vt[64:128, hs], True, True,
                       ["Kh", "Vt"], [PK[2 + h // 4]])
                for h in range(8):
                    STT("dve", Sst[:, h, :], Sst[:, h, :], dec[:, h, 1:2], ps[2 + h // 4][:, (h % 4) * 128:(h % 4 + 1) * 128],
                        ALU.mult, ALU.add, [PK[2 + h // 4], "dec", ("Sst", h)], [("Sst", h)])
                    CP("pool", Sbf[:, h, :], Sst[:, h, :], [("Sst", h)], [("Sbf", h)])
            LD(S_loc.rearrange("(h k) v -> k h v", k=128), Sst, [("Sst", h) for h in range(8)], ["S_loc"], "sloc")
            R.op("dve", lambda e: e.tensor_reduce(out=dtot, in_=glast, axis=AX.X, op=ALU.add), ["glast"], ["dtot"])
            ACT(dtot, dtot, AF.Exp, ["dtot"], ["dtot"])
            LD(D_loc, dtot, ["dtot"], ["D_loc"], "sloc")
            R.barrier()
            A.top = S2TOP

            kTs = A.alloc([8, TOK], BF16)
            tq = a1(1024)
            tf = a1(1024)
            ti = a1(1024)
            qb = a1(1024, BF16)
            kb = a1(1024, BF16)
            vb = [a1(1024, BF16) for _ in range(2)]
            rt = [A.alloc([8, 16], F32) for _ in range(4)]
            for i in range(NT):
                rows = slice(i * 128, (i + 1) * 128)
                tsl = slice(i * 128, (i + 1) * 128)
                LD(tq, proj_tm[rows, 3072:4096], [("ptm", 3, i)], ["tq"], "tq")
                LD(tf, proj_tm[rows, 4096:5120], [("ptm", 4, i)], ["tf"], "tf")
                LD(ti, proj_tm[rows, 5120:6144], [("ptm", 5, i)], ["ti"], "ti")
                TS1("pool", tq, tq, float(128.0 ** -0.5), ALU.mult, ["tq"], ["tq"])
                cb = cosT[:, i, :].unsqueeze(1).broadcast_to([128, 8, 16])
                sn = sinT[:, i, :].unsqueeze(1).broadcast_to([128, 8, 16])
                for (src, dst, sk_, dk_, e1, e2) in ((tq, qb, "tq", "qb", "dve", "pool"), (tf, kb, "tf", "kb", "pool", "dve")):
                    v3 = src.rearrange("p (j d) -> p j d", d=128)
                    o3 = dst.rearrange("p (j d) -> p j d", d=128)
                    t1 = v3[:, :, 0:16]
                    t2 = v3[:, :, 16:32]
                    TT(e1, rt[0], t1, cb, ALU.mult, [sk_, "cosT"], ["rt0"])
                    TT(e2, rt[1], t2, sn, ALU.mult, [sk_, "sinT"], ["rt1"])
                    TT(e1, o3[:, :, 0:16], rt[0], rt[1], ALU.subtract, ["rt0", "rt1"], [dk_])
                    TT(e1, rt[2], t2, cb, ALU.mult, [sk_, "cosT"], ["rt2"])
                    TT(e2, rt[3], t1, sn, ALU.mult, [sk_, "sinT"], ["rt3"])
                    TT(e2, o3[:, :, 16:32], rt[2], rt[3], ALU.add, ["rt2", "rt3"], [dk_])
                    CP("act", o3[:, :, 32:128], v3[:, :, 32:128], [sk_], [dk_])
                v_ = vb[i % 2]
                vk = "vb%d" % (i % 2)
                CP("act", v_, ti, ["ti"], [vk])
                LD(v_loc[rows, :], v_, [vk], ["v_loc"], vk)
                tcount = 0
                for (srcb, sk_, dstT, dk_) in ((qb, "qb", qT, "qT"), (kb, "kb", kTs, "kTs")):
                    for g in range(2):
                        pi = 4 + (tcount % 4)
                        tcount += 1
                        pbb = ps[pi][:, :].bitcast(BF16)
                        for j in range(4):
                            h = g * 4 + j
                            TR(pbb[:, j * 128:(j + 1) * 128], srcb[:, h * 128:(h + 1) * 128], identb, [sk_, "cbf"], [PK[pi]])
                        evac(g, dstT[:, g * 4:(g + 1) * 4, tsl], pbb[:, 0:512].rearrange("p (a b) -> p a b", b=128),
                             [PK[pi]], [dk_])
            LD(kT_loc.rearrange("(j d) t -> d j t", d=128), kTs, ["kTs"], ["kT_loc"], "kTs")
            R.barrier()
            A.top = S2TOP
        if mode == "A":
            LD(carry[0], qsT, [], ["c0_"], "k_cq")
            LD(carry[1], ozT, [], ["c1_"], "k_co")
            LD(carry[2], qT, [], ["c2_"], "k_cT")
            R.barrier()
        if mode == "A" or stop <= 3:
            break

        if mode == "fused":
            rg = [list(range(NCORES))]
            for (src, dst, nm) in ((kT_loc, kT_all, "kT"), (v_loc, v_all, "v"), (S_loc, S_all, "S"), (D_loc, D_all, "D")):
                R.dma("pool", lambda e, src=src, dst=dst: e.collective_compute(
                    "AllGather", ALU.bypass, replica_groups=rg, ins=[src[:, :]], outs=[dst[:, :]]),
                    [], [nm + "_all"], "cc")
            R.barrier()

        oaT = qsT
        Sr = [A.alloc([8, 128], F32) for _ in range(2)]
        Dall = A.alloc([8, 8], F32)
        Dp = a1(8)
        acc = A.alloc([8, 128], F32)
        sinb = A.alloc([8, 128], BF16)
        sq4 = a1(TOK, BF16)
        rstd4 = a1(TOK)
        gt = [a1(TOK) for _ in range(2)]
        tmp4 = a1(TOK)
        LD(Dall, D_all.rearrange("(r k) h -> k r h", k=128), [], ["Dall"], "k_Dall")
        MEMSET("pool", acc, 0.0, ["acc"])
        for r in range(NCORES - 1):
            sr = Sr[r % 2]
            srk = "Sr%d" % (r % 2)
            LD(sr, S_all[r * 1024:(r + 1) * 1024, :].rearrange("(h k) v -> k h v", k=128), [], [srk], srk)
            TS("dve", Dp, Dall[:, r, :], mcore[:, r:r + 1], mcore[:, 8 + r:9 + r], ALU.mult, ALU.add,
               ["Dall", "mcore"], ["Dp"])
            TS1("pool", sr, sr, mcore[:, r:r + 1], ALU.mult, [srk, "mcore"], [srk])
            for h in range(8):
                STT("dve", acc[:, h, :], acc[:, h, :], Dp[:, h:h + 1], sr[:, h, :], ALU.mult, ALU.add,
                    ["acc", "Dp", srk], ["acc"])
        CP("dve", sinb, acc, ["acc"], ["sinb"])
        for h in range(8):
            g_ = gt[h % 2]
            gk = "gt%d" % (h % 2)
            LD(g_, proj_fm[h * 128:(h + 1) * 128, :], [("pfm", h)], [gk], gk)
            ACT(g_, g_, AF.Silu, [gk], [gk])
            for hf in range(2):
                cs = slice(hf * 512, (hf + 1) * 512)
                MM(ps[hf][:, :], sinb[:, h, :], qsT[:, h, cs], True, True, ["sinb", ("qsT", h)], [PK[hf]])
                TT("dve", ozT[:, h, cs], ozT[:, h, cs], ps[hf][:, :], ALU.add, [PK[hf], ("ozT", h)], [("ozT", h)])
            ACT(sq4, ozT[:, h, :], AF.Square, [("ozT", h)], ["sq4"])
            for hf in range(2):
                MM(ps[6 + hf][:, :], onesb, sq4[:, hf * 512:(hf + 1) * 512], True, True, ["sq4", "cbf"], [PK[6 + hf]])
            rstd_finish(2, 128.0, NORM_EPS, rstd4, "g4")
            STT("dve", tmp4, ozT[:, h, :], gnwT[:, li:li + 1], rstd4, ALU.mult, ALU.mult,
                [("ozT", h), "gnwT", "g4rstd"], ["tmp4"])
            TT("pool", oaT[:, h, :], tmp4, g_, ALU.mult, ["tmp4", gk, ("qsT", h)], [("qsT", h)])
        R.barrier()
        A.top = S2TOP

        if stop <= 4:
            break
        obT = arena_t[:, RAw:RAw + 4096].bitcast(BF16).rearrange("p (a b) -> p a b", b=TOK)
        NSL = 3
        kres = A.alloc([2, SEQ], BF16)
        vbuf = [A.alloc([8, 257], BF16) for _ in range(NSL)]
        SBK = [0, 1, 6, 7]
        LA = 3
        pT = [a1(512, BF16) for _ in range(4)]
        pM = [a1(512, BF16) for _ in range(4)]
        lp = A.alloc([4, 128], F32)
        lpp = a1(128)
        lamc = a1(4)
        slwb = a1(256)
        rr = a1(4)
        of = a1(256)
        tmpo = a1(256)
        onb = a1(256, BF16)
        junk = a1(256)
        for s in range(NSL):
            MEMSET("pool", vbuf[s][:, :, 256:257], 1.0, ["vbuf%d" % s])
        LD(lp, dlam[li].partition_broadcast(128), [], ["lp"], "k_lp")
        TT("dve", lpp, lp[:, 0, :], lp[:, 1, :], ALU.mult, ["lp"], ["lpp"])
        R.op("dve", lambda e: e.tensor_reduce(out=lamc[:, 0:1], in_=lpp, axis=AX.X, op=ALU.add), ["lpp"], ["lamc"])
        TT("dve", lpp, lp[:, 2, :], lp[:, 3, :], ALU.mult, ["lp", "lamc"], ["lpp"])
        R.op("dve", lambda e: e.tensor_reduce(out=lamc[:, 1:2], in_=lpp, axis=AX.X, op=ALU.add), ["lpp"], ["lamc"])
        ACT(lamc[:, 0:2], lamc[:, 0:2], AF.Exp, ["lamc"], ["lamc"])
        TT("dve", lamc[:, 2:3], lamc[:, 0:1], lamc[:, 1:2], ALU.subtract, ["lamc"], ["lamc"])
        TS1("dve", lamc[:, 2:3], lamc[:, 2:3], float(lam_init), ALU.add, ["lamc"], ["lamc"])
        LD(slwb, slw[li].partition_broadcast(128),
           [], ["slwb"], "k_slwb")
        TS1("dve", slwb, slwb, float(1.0 - lam_init), ALU.mult, ["slwb"], ["slwb"])
        step = 0
        for h in range(4):
            for r in range(NCORES):
                LD(kres[:, :, r * 1024:(r + 1) * 1024],
                   kT_all[(r * 8 + 2 * h) * 128:(r * 8 + 2 * h + 2) * 128, :].rearrange("(c d) t -> d c t", d=128),
                   ["kT_all"], ["kres%d" % r], "kres%d" % r)
            for qg in range(4):
                qs_ = slice(qg * 256, (qg + 1) * 256)
                qc2 = qcb[:, qs_].unsqueeze(1).broadcast_to([128, 2, 256])
                steps = [(r, kt) for r in range(NCORES) for kt in range(8)]
                slot_of = {}

                def seg_load(r):
                    nonlocal step
                    s_ = step % NSL
                    step += 1
                    slot_of[r] = s_
                    vbk = "vbuf%d" % s_
                    LD(vbuf[s_][:, :, 0:256], v_all[r * 1024:(r + 1) * 1024, h * 256:(h + 1) * 256].rearrange(
                        "(kt p) e -> p kt e", p=128), ["v_all"], [vbk], vbk)

                def scores(r, kt):
                    s_ = slot_of[r]
                    ktg = r * 8 + kt
                    sb_ = ktg % 4
                    for c in range(2):
                        MM(ps[SBK[sb_]][:, c * 256:(c + 1) * 256], kres[:, c, ktg * 128:(ktg + 1) * 128],
                           qT[:, 2 * h + c, qs_], True, True, ["kres%d" % r, "qT"], [PK[SBK[sb_]]])

                def softmax_part(r, kt):
                    ktg = r * 8 + kt
                    sb_ = ktg % 4
                    ACT(pT[sb_], ps[SBK[sb_]][:, :], AF.Exp, [PK[SBK[sb_]]], ["pT%d" % sb_])
                    STT("dve", pM[sb_].rearrange("p (c q) -> p c q", c=2), qc2, relv[:, ktg:ktg + 1],
                        pT[sb_].rearrange("p (c q) -> p c q", c=2), ALU.is_ge, ALU.mult,
                        ["pT%d" % sb_, "qcb", "relv"], ["pM%d" % sb_])

                def pv(r, kt):
                    s_ = slot_of[r]
                    ktg = r * 8 + kt
                    sb_ = ktg % 4
                    for c in range(2):
                        for qt in range(2):
                            ai = 2 + c * 2 + qt
                            MM(ps[ai][:, 0:257], pM[sb_][:, c * 256 + qt * 128:c * 256 + (qt + 1) * 128],
                               vbuf[s_][:, kt, :], ktg == 0, ktg == 63, ["pM%d" % sb_, "vbuf%d" % s_], [PK[ai]])

                seg_load(0)
                seg_load(1)
                for k0 in range(LA):
                    scores(*steps[k0])
                for k, (r, kt) in enumerate(steps):
                    softmax_part(r, kt)
                    if k + LA < len(steps):
                        r2, kt2 = steps[k + LA]
                        if kt2 == 0 and r2 + 1 < NCORES:
                            seg_load(r2 + 1)
                        scores(r2, kt2)
                    pv(r, kt)
                for qt in range(2):
                    a0, a1_ = 2 + qt, 4 + qt
                    tcol = slice(qg * 256 + qt * 128, qg * 256 + (qt + 1) * 128)
                    RECIP(rr[:, 0:1], ps[a0][:, 256:257], [PK[a0]], ["rr"])
                    RECIP(rr[:, 1:2], ps[a1_][:, 256:257], [PK[a1_], "rr"], ["rr"])
                    TT("dve", rr[:, 1:2], rr[:, 1:2], lamc[:, 2:3], ALU.mult, ["rr", "lamc"], ["rr"])
                    TS1("dve", tmpo, ps[a1_][:, 0:256], rr[:, 1:2], ALU.mult, [PK[a1_], "rr"], ["tmpo"])
                    STT("dve", of, ps[a0][:, 0:256], rr[:, 0:1], tmpo, ALU.mult, ALU.subtract,
                        [PK[a0], "rr", "tmpo"], ["of"])
                    TT("dve", junk, of, of, ALU.mult, ["of"], ["junk"])
                    R.op("dve", lambda e: e.tensor_reduce(out=rr[:, 2:3], in_=junk, axis=AX.X, op=ALU.add), ["junk"], ["ss"])
                    ACT(rr[:, 3:4], rr[:, 2:3], AF.Sqrt, ["ss", "epsc"], ["rs"], bias=eps_ap(SUBLN_EPS), scale=1.0 / 256.0)
                    RECIP(rr[:, 3:4], rr[:, 3:4], ["rs"], ["rs"])
                    STT("dve", onb, of, rr[:, 3:4], slwb, ALU.mult, ALU.mult, ["of", "rs", "slwb"], ["onb"])
                    pbb = ps[6 + qt][:, :].bitcast(BF16)
                    for e2 in range(2):
                        TR(pbb[:, e2 * 128:(e2 + 1) * 128], onb[:, e2 * 128:(e2 + 1) * 128], identb, ["onb", "cbf"], [PK[6 + qt]])
                    evac(qt, obT[:, 2 * h:2 * h + 2, tcol], pbb[:, 0:256].rearrange("p (a b) -> p a b", b=128),
                         [PK[6 + qt]], ["obT"])
        R.barrier()

        if stop <= 5:
            break
        mT = arena_t[:, RAw + 4096:RAw + 4096 + 8192].bitcast(BF16).rearrange("p (a b) -> p a b", b=TOK)
        A.top = S2TOP
        wab = [A.alloc([2, 8, 512], BF16) for _ in range(2)]
        gat = [a1(TOK) for _ in range(2)]
        gbt = [a1(TOK) for _ in range(2)]
        m1 = a1(TOK)
        m2 = a1(TOK)
        wa_l = w_a[li].rearrange("(kc p) c -> p kc c", p=128)
        wb_l = w_b[li].rearrange("(kc p) c -> p kc c", p=128)
        for j in range(KC):
            g = j // 4
            s = g % 2
            wk = "wab%d" % s
            if j % 4 == 0:
                wload(wab[s][:, 0, :, :], wa_l[:, :, g * 512:(g + 1) * 512], wk)
                wload(wab[s][:, 1, :, :], wb_l[:, :, g * 512:(g + 1) * 512], wk)
            jc = slice((j % 4) * 128, (j % 4 + 1) * 128)
            pb0 = (j % 2) * 4
            ga_, gb_ = gat[j % 2], gbt[j % 2]
            gak, gbk = "gat%d" % (j % 2), "gbt%d" % (j % 2)
            LD(ga_, proj_fm[1024 + j * 128:1024 + (j + 1) * 128, :], [("pfm", 8 + j)], [gak], gak)
            LD(gb_, proj_fm[3072 + j * 128:3072 + (j + 1) * 128, :], [("pfm", 24 + j)], [gbk], gbk)
            ACT(ga_, ga_, AF.Sigmoid, [gak], [gak])
            ACT(gb_, gb_, AF.Sigmoid, [gbk], [gbk])
            for br, srcT, sk_ in ((0, oaT, "qsT"), (1, obT, "obT")):
                for hf in range(2):
                    pi = pb0 + br * 2 + hf
                    for kc in range(8):
                        MM(ps[pi][:, :], wab[s][:, br, kc, jc], srcT[:, kc, hf * 512:(hf + 1) * 512], kc == 0, kc == 7,
                           [wk, sk_], [PK[pi]])
            for hf in range(2):
                cs = slice(hf * 512, (hf + 1) * 512)
                TT("dve", m1[:, cs], ga_[:, cs], ps[pb0 + hf][:, :], ALU.mult, [gak, PK[pb0 + hf]], ["m1"])
                TT("dve", m2[:, cs], gb_[:, cs], ps[pb0 + 2 + hf][:, :], ALU.mult, [gbk, PK[pb0 + 2 + hf]], ["m2"])
            TT("pool", mT[:, j, :], m1, m2, ALU.add, ["m1", "m2"], ["mT"])
        R.barrier()

        if stop <= 6:
            break
        A.top = qT_off + 4096
        wo = [A.alloc([KC, 512], BF16) for _ in range(2)]
        zT = A.alloc([KC, 512], F32)
        sq6 = [a1(512, BF16) for _ in range(2)]
        rstd6 = a1(512)
        tmp6 = a1(512)
        wo_l = w_o[li].rearrange("(kc p) c -> p kc c", p=128)
        widx = 0
        for th in range(2):
            tcs = slice(th * 512, (th + 1) * 512)
            for jo in range(KC):
                g = jo // 4
                if jo % 4 == 0:
                    s = widx % 2
                    widx += 1
                    wk = "wo%d" % s
                    wload(wo[s], wo_l[:, :, g * 512:(g + 1) * 512], wk)
                pi = jo % 4
                for kc in range(KC):
                    MM(ps[pi][:, :], wo[s][:, kc, (jo % 4) * 128:(jo % 4 + 1) * 128], mT[:, kc, tcs], kc == 0, kc == KC - 1,
                       [wk, "mT"], [PK[pi]])
                CP("dve", zT[:, jo, :], ps[pi][:, :], [PK[pi]], ["zT"])
                sb = sq6[jo % 2]
                sk = "sq6%d" % (jo % 2)
                ACT(sb, ps[pi][:, :], AF.Square, [PK[pi]], [sk])
                MM(ps[6][:, :], onesb, sb, jo == 0, jo == KC - 1, [sk, "cbf"], [PK[6]])
            rstd_finish(1, float(D), NORM_EPS, rstd6, "n6")
            for c in range(KC):
                STT("dve", tmp6, zT[:, c, :], nwT[:, li * 4 + 1, c:c + 1], rstd6, ALU.mult, ALU.mult,
                    ["zT", "nwT", "n6rstd"], ["tmp6"])
                TT("pool", xT[:, c, tcs], xT[:, c, tcs], tmp6, ALU.add, ["tmp6", "xT"], ["xT"])
        R.barrier()
        A.top = PERSIST

        if stop <= 7:
            break
        sqb = [a1(TOK, BF16) for _ in range(2)]
        rstd = a1(TOK)
        ovl = A.top
        h2T = A.alloc([KC, TOK], BF16)
        rms_rstd(lambda c: (xT[:, c, :], ["xT"]), KC, float(D), NORM_EPS, sqb, rstd, "n7", 0, TOK)
        for c in range(KC):
            STT("dve", h2T[:, c, :], xT[:, c, :], nwT[:, li * 4 + 2, c:c + 1], rstd,
                ALU.mult, ALU.mult, ["xT", "nwT", "n7rstd"], ["h2T"])
        wgu = [A.alloc([2, KC, 256], BF16) for _ in range(2)]
        sgt = [a1(TOK) for _ in range(2)]
        ast = [a1(TOK, BF16) for _ in range(2)]
        wf1_l = w_f1[li].rearrange("(kc p) c -> p kc c", p=128)
        wf2_l = w_f2[li].rearrange("(kc p) c -> p kc c", p=128)
        widx = 0
        for j in range(FC):
            g = j // 2
            if j % 2 == 0:
                s = widx % 2
                widx += 1
                wk = "wgu%d" % s
                wload(wgu[s][:, 0, :, :], wf1_l[:, :, g * 256:(g + 1) * 256], wk)
                wload(wgu[s][:, 1, :, :], wf1_l[:, :, FFH + g * 256:FFH + (g + 1) * 256], wk)
            jc = slice((j % 2) * 128, (j % 2 + 1) * 128)
            pb0 = (j % 2) * 4
            for hf in range(2):
                tcs = slice(hf * 512, (hf + 1) * 512)
                for kc in range(KC):
                    MM(ps[pb0 + hf][:, :], wgu[s][:, 0, kc, jc], h2T[:, kc, tcs], kc == 0, kc == KC - 1,
                       [wk, "h2T"], [PK[pb0 + hf]])
            for hf in range(2):
                tcs = slice(hf * 512, (hf + 1) * 512)
                for kc in range(KC):
                    MM(ps[pb0 + 2 + hf][:, :], wgu[s][:, 1, kc, jc], h2T[:, kc, tcs], kc == 0, kc == KC - 1,
                       [wk, "h2T"], [PK[pb0 + 2 + hf]])
            sg_ = sgt[j % 2]
            sgk = "sgt%d" % (j % 2)
            as_ = ast[j % 2]
            ask = "ast%d" % (j % 2)
            for hf in range(2):
                tcs = slice(hf * 512, (hf + 1) * 512)
                ACT(sg_[:, tcs], ps[pb0 + hf][:, :], AF.Silu, [PK[pb0 + hf]], [sgk])
                TT("dve", as_[:, tcs], sg_[:, tcs], ps[pb0 + 2 + hf][:, :], ALU.mult, [sgk, PK[pb0 + 2 + hf]], [ask])
            LD(a_scr[j * 128:(j + 1) * 128, :], as_, [ask], ["a_scr"], ask)
        R.barrier()
        A.top = ovl
        aT = A.alloc([FC, 512], BF16)
        w2 = [A.alloc([FC, 128], BF16) for _ in range(2)]
        y2 = A.alloc([KC, 512], F32)
        rstd8 = a1(512)
        tmp8 = a1(512)
        w2idx = 0
        for th in range(2):
            tcs = slice(th * 512, (th + 1) * 512)
            LD(aT, a_scr.rearrange("(j p) t -> p j t", p=128)[:, :, tcs], ["a_scr"], ["aT"], "aT")
            for jo in range(KC):
                s2 = w2idx % 2
                w2idx += 1
                wk2 = "w2%d" % s2
                wload(w2[s2], wf2_l[:, :, jo * 128:(jo + 1) * 128], wk2)
                pi = 4 + (jo % 2)
                for kc in range(FC):
                    MM(ps[pi][:, :], w2[s2][:, kc, :], aT[:, kc, :], kc == 0, kc == FC - 1, [wk2, "aT"], [PK[pi]])
                CP("dve", y2[:, jo, :], ps[pi][:, :], [PK[pi]], ["y2"])
                sb = sqb[jo % 2]
                sk = "sq7%d" % (jo % 2)
                ACT(sb[:, 0:512], ps[pi][:, :], AF.Square, [PK[pi]], [sk])
                MM(ps[6][:, :], onesb, sb[:, 0:512], jo == 0, jo == KC - 1, [sk, "cbf"], [PK[6]])
            rstd_finish(1, float(D), NORM_EPS, rstd8, "n8")
            for c in range(KC):
                STT("dve", tmp8, y2[:, c, :], nwT[:, li * 4 + 3, c:c + 1], rstd8, ALU.mult, ALU.mult,
                    ["y2", "nwT", "n8rstd"], ["tmp8"])
                TT("pool", xT[:, c, tcs], xT[:, c, tcs], tmp8, ALU.add, ["tmp8", "xT"], ["xT"])
        R.barrier()

    if dbg:
        for nm in dbg:
            t_ = {"oaT": lambda: oaT, "obT": lambda: obT, "mT": lambda: mT, "qT": lambda: qT}[nm]()
            dd = nc.dram_tensor("dbg_" + nm, [128, t_.shape[1], t_.shape[2]], BF16, kind="ExternalOutput").ap()
            LD(dd, t_, [], ["dbg" + nm], "dbg")
        R.barrier()
    if mode != "A":
        A.top = PERSIST
        ost = [a1(512) for _ in range(4)]
        oi = 0
        for i in range(NT):
            for g in range(4):
                pi = oi % 4
                for j in range(4):
                    kc = g * 4 + j
                    TR(ps[pi][:, j * 128:(j + 1) * 128], xT[:, kc, i * 128:(i + 1) * 128], identf, ["xT", "cmat"], [PK[pi]])
                sb = ost[oi % 4]
                sk = "ost%d" % (oi % 4)
                evac(oi, sb, ps[pi][:, :], [PK[pi]], [sk])
                LD(x_out[i * 128:(i + 1) * 128, g * 512:(g + 1) * 512], sb, [sk], ["x_out"], sk)
                oi += 1
    R.barrier()
    with nc.Block() as block:
        R.emit(block)
    es.close()
    return nc


def _consts(core):
    p = np.arange(128)
    ident = np.eye(128, dtype=np.float32)
    same = (p[:, None] // 64) == (p[None, :] // 64)
    LBD = (same & (p[:, None] <= p[None, :])).astype(np.float32)
    UBD = (same & (p[:, None] > p[None, :])).astype(np.float32)
    LF = (p[:, None] <= p[None, :]).astype(np.float32)
    ONES = np.ones((128, 128), np.float32)
    cmat = np.stack([ident, LBD, UBD, LF, ONES], axis=1).astype(np.float32)
    ind = np.stack([(p < 64), (p >= 64)], axis=1).astype(np.float32)
    kt = np.arange(64)
    relv = (2 * kt[None, :] + (p[:, None] // 64) - 16 * core).astype(np.float32)
    qcrow = np.broadcast_to((np.arange(TOK) // 64).astype(np.float32)[None, :], (128, TOK)).copy()
    m = (np.arange(8) < core).astype(np.float32)
    mcore = np.broadcast_to(np.concatenate([m, 1.0 - m])[None, :], (128, 16)).astype(np.float32).copy()
    pos = np.arange(core * TOK, (core + 1) * TOK, dtype=np.float32)
    inv_freq = (np.float32(500000.0) ** (-np.arange(0, 32, 2, dtype=np.float32) / np.float32(32))).astype(np.float32)
    ang = (pos[:, None] * inv_freq[None, :]).astype(np.float32)
    return dict(cmat=cmat, ind=ind, relv=relv, qcrow=qcrow, mcore=mcore,
                rope_cos=np.cos(ang).astype(np.float32), rope_sin=np.sin(ang).astype(np.float32))


_PROGS = {}


def _prog(mode, layers):
    key = (mode, tuple(layers))
    if key not in _PROGS:
        _PROGS[key] = build_program(mode, list(layers))
    return _PROGS[key]


WNAMES = ["w_in", "hgrn_gnorm_w", "diff_lambda", "diff_subln_w", "w_branch_a", "w_branch_b", "w_out",
          "norm_w", "w_ffn_in", "w_ffn_out"]

FUSED = False


def kernel(**inputs):
    inp = {k: np.ascontiguousarray(np.asarray(v)) for k, v in inputs.items()}
    x = inp["x"].reshape(SEQ, D)
    consts = [_consts(c) for c in range(NCORES)]
    xs = [np.ascontiguousarray(x[c * TOK:(c + 1) * TOK]) for c in range(NCORES)]
    if FUSED:
        nc = _prog("fused", range(DEPTH))
        in_maps = []
        for c in range(NCORES):
            m = dict(consts[c])
            m["x_loc"] = xs[c]
            m["hgrn_lower_bounds"] = inp["hgrn_lower_bounds"]
            for n in WNAMES:
                m[n] = inp[n]
            in_maps.append(m)
        res = run_bass_kernel_spmd(nc, in_maps, core_ids=list(range(NCORES)))
        out = np.concatenate([r["x_out"] for r in res.results], axis=0)
        return out.reshape(1, SEQ, D).astype(np.float32)
    for l in range(DEPTH):
        wl = {n: np.ascontiguousarray(inp[n][l:l + 1]) for n in WNAMES}
        base = []
        for c in range(NCORES):
            m = dict(consts[c])
            m["x_loc"] = xs[c]
            m["hgrn_lower_bounds"] = inp["hgrn_lower_bounds"]
            m.update(wl)
            base.append(m)
        ncA = _prog("A", [l])
        resA = run_bass_kernel_spmd(ncA, base, core_ids=list(range(NCORES)))
        kT_all = np.concatenate([r["kT_loc"] for r in resA.results], axis=0)
        v_all = np.concatenate([r["v_loc"] for r in resA.results], axis=0)
        S_all = np.concatenate([r["S_loc"] for r in resA.results], axis=0)
        D_all = np.concatenate([r["D_loc"] for r in resA.results], axis=0)
        ncB = _prog("B", [l])
        inB = []
        for c in range(NCORES):
            m = dict(base[c])
            m.update(kT_all=kT_all, v_all=v_all, S_all=S_all, D_all=D_all)
            for nm in ("c_qsT", "c_ozT", "c_qT"):
                m[nm] = resA.results[c][nm]
            inB.append(m)
        resB = run_bass_kernel_spmd(ncB, inB, core_ids=list(range(NCORES)))
        xs = [np.ascontiguousarray(r["x_out"]) for r in resB.results]
    out = np.concatenate(xs, axis=0)
    return out.reshape(1, SEQ, D).astype(np.float32)
```

```python
import math
from contextlib import ExitStack
import numpy as np
import concourse.bass as bass
import concourse.mybir as mybir
from concourse.bass_utils import run_bass_kernel_spmd

F32 = mybir.dt.float32
BF16 = mybir.dt.bfloat16
AF = mybir.ActivationFunctionType
ALU = mybir.AluOpType
AX = mybir.AxisListType

NCORES = 8
DEPTH = 4
D = 2048
SEQ = 8192
TOK = SEQ // NCORES
NT = TOK // 128
KC = D // 128
INW = 11264
FFH = 5632
FC = FFH // 128
NORM_EPS = 1e-6
SUBLN_EPS = 1e-5
SAME_ENGINE_SYNC = True


class Rec:
    ENG = ["pe", "act", "dve", "pool", "sp"]

    def __init__(self, nc, es):
        self.nc = nc
        self.es = es
        self.ops = {e: [] for e in self.ENG}
        self.cnt = {e: 0 for e in self.ENG}
        self.esem = {e: es.enter_context(nc.semaphore("S_" + e)) for e in self.ENG}
        self.seen = {e: {} for e in self.ENG}
        self.lastw = {}
        self.readers = {}
        self.dsem = {}

    def _dsem(self, name):
        if name not in self.dsem:
            self.dsem[name] = [self.es.enter_context(self.nc.semaphore("D_" + name)), 0]
        return self.dsem[name]

    def _need(self, eng, tok, waits):
        kind, key, val = tok
        if kind == "E":
            if key == eng and (eng in ("pe", "sp") or not SAME_ENGINE_SYNC):
                return
            sem = self.esem[key]
        else:
            sem = self.dsem[key][0]
        sk = (kind, key)
        if self.seen[eng].get(sk, 0) >= val:
            return
        self.seen[eng][sk] = val
        waits.append((sem, val))

    def _deps(self, eng, reads, writes):
        waits = []
        for k in reads:
            t = self.lastw.get(k)
            if t is not None:
                self._need(eng, t, waits)
        for k in writes:
            t = self.lastw.get(k)
            if t is not None:
                self._need(eng, t, waits)
            for sk, v in self.readers.get(k, {}).items():
                self._need(eng, (sk[0], sk[1], v), waits)
        return waits

    def _commit(self, tok, reads, writes):
        sk = (tok[0], tok[1])
        for k in reads:
            d = self.readers.setdefault(k, {})
            if d.get(sk, 0) < tok[2]:
                d[sk] = tok[2]
        for k in writes:
            self.lastw[k] = tok
            self.readers[k] = {}

    @staticmethod
    def _excl(reads, writes):
        r2 = [k for k in reads if not (isinstance(k, str) and k.startswith("ps"))]
        w2 = list(writes) + [k for k in reads if isinstance(k, str) and k.startswith("ps") and k not in writes]
        return r2, w2

    def op(self, eng, fn, reads=(), writes=()):
        reads, writes = self._excl(reads, writes)
        waits = self._deps(eng, reads, writes)
        self.cnt[eng] += 1
        tok = ("E", eng, self.cnt[eng])
        self._commit(tok, reads, writes)
        self.ops[eng].append((waits, fn, (self.esem[eng], 1)))

    def dma(self, eng, fn, reads, writes, sem):
        ds = self._dsem(sem)
        waits = self._deps(eng, reads, writes)
        ds[1] += 16
        tok = ("D", sem, ds[1])
        self._commit(tok, reads, writes)
        self.ops[eng].append((waits, fn, (ds[0], 16)))

    def barrier(self):
        for e in self.ENG:
            waits = []
            for f in self.ENG:
                if f != e and self.cnt[f] > 0:
                    self._need(e, ("E", f, self.cnt[f]), waits)
            if e not in ("pe", "sp") and self.cnt[e] > 0 and SAME_ENGINE_SYNC:
                self._need(e, ("E", e, self.cnt[e]), waits)
            for name, (h, tot) in self.dsem.items():
                if tot > 0:
                    self._need(e, ("D", name, tot), waits)
            if waits:
                self.ops[e].append((waits, None, None))
        self.lastw = {}
        self.readers = {}

    def emit(self, block):
        for eng, attr in [("pe", "tensor"), ("act", "scalar"), ("dve", "vector"),
                          ("pool", "gpsimd"), ("sp", "sync")]:
            ops = self.ops[eng]

            def body(e, ops=ops):
                for waits, fn, inc in ops:
                    for sem, v in waits:
                        e.wait_ge(sem, v)
                    if fn is not None:
                        ins = fn(e)
                        ins.then_inc(inc[0], inc[1])

            getattr(block, attr)(body)


class Arena:
    def __init__(self, ap, words):
        self.ap = ap
        self.words = words
        self.top = 0

    def alloc(self, shape, dtype):
        n = int(np.prod(shape))
        w = n if dtype == F32 else (n + 1) // 2
        w = (w + 15) // 16 * 16
        assert self.top + w <= self.words, ("arena overflow", self.top, w, self.words)
        v = self.ap[:, self.top:self.top + w]
        self.top += w
        if dtype != F32:
            v = v.bitcast(dtype)
        v = v[:, 0:n]
        if len(shape) == 2:
            v = v.rearrange("p (a b) -> p a b", b=shape[1])
        elif len(shape) == 3:
            v = v.rearrange("p (a b c) -> p a b c", b=shape[1], c=shape[2])
        return v


def build_program(mode, layers, dbg=None, stop=99):
    nc = bass.Bass("TRN2", target_bir_lowering=False)
    es = ExitStack()
    nl = len(layers)

    def din(name, shape, dt=F32):
        return nc.dram_tensor(name, list(shape), dt, kind="ExternalInput").ap()

    def dscr(name, shape, dt=F32, kind="Internal"):
        return nc.dram_tensor(name, list(shape), dt, kind=kind).ap()

    x_in = din("x_loc", [TOK, D])
    w_in = din("w_in", [nl, D, INW])
    hlb = din("hgrn_lower_bounds", [DEPTH, 1024])
    gnw = din("hgrn_gnorm_w", [nl, 128])
    dlam = din("diff_lambda", [nl, 4, 128])
    slw = din("diff_subln_w", [nl, 256])
    w_a = din("w_branch_a", [nl, 1024, D])
    w_b = din("w_branch_b", [nl, 1024, D])
    w_o = din("w_out", [nl, D, D])
    nrm = din("norm_w", [nl, 4, D])
    w_f1 = din("w_ffn_in", [nl, D, 2 * FFH])
    w_f2 = din("w_ffn_out", [nl, FFH, D])
    cos_in = din("rope_cos", [TOK, 16])
    sin_in = din("rope_sin", [TOK, 16])
    cmat_in = din("cmat", [128, 5, 128])
    ind_in = din("ind", [128, 2])
    relv_in = din("relv", [128, 64])
    qcrow_in = din("qcrow", [128, TOK])
    mcore_in = din("mcore", [128, 16])
    x_out = None
    if mode != "A":
        x_out = nc.dram_tensor("x_out", [TOK, D], F32, kind="ExternalOutput").ap()

    proj_tm = dscr("proj_tm", [TOK, 6144])
    proj_fm = dscr("proj_fm", [5120, TOK])
    a_scr = dscr("a_scr", [FFH, TOK], BF16)
    exk = "ExternalOutput" if mode == "A" else "Internal"
    axk = "ExternalInput" if mode == "B" else "Internal"
    EXB = []
    for par in range(2 if mode == "fused" else 1):
        sfx = "" if par == 0 else "_p1"
        EXB.append((
            dscr("kT_loc" + sfx, [1024, TOK], BF16, kind=exk),
            dscr("v_loc" + sfx, [TOK, 1024], BF16, kind=exk),
            dscr("S_loc" + sfx, [1024, 128], F32, kind=exk),
            dscr("D_loc" + sfx, [128, 8], F32, kind=exk),
            dscr("kT_all" + sfx, [NCORES * 1024, TOK], BF16, kind=axk),
            dscr("v_all" + sfx, [SEQ, 1024], BF16, kind=axk),
            dscr("S_all" + sfx, [NCORES * 1024, 128], F32, kind=axk),
            dscr("D_all" + sfx, [NCORES * 128, 8], F32, kind=axk)))

    carry = None
    if mode in ("A", "B"):
        ck = "ExternalOutput" if mode == "A" else "ExternalInput"
        carry = (dscr("c_qsT", [128, 8, TOK], BF16, kind=ck), dscr("c_ozT", [128, 8, TOK], F32, kind=ck),
                 dscr("c_qT", [128, 8, TOK], BF16, kind=ck))
    ARENA_WORDS = 52480
    arena_t = es.enter_context(nc.sbuf_tensor("arena", [128, ARENA_WORDS], F32))
    ps = [es.enter_context(nc.psum_tensor("ps%d" % i, [128, 512], F32)) for i in range(8)]
    R = Rec(nc, es)
    A = Arena(arena_t[:, :], ARENA_WORDS)
    PK = ["ps%d" % i for i in range(8)]

    def a1(n, dt=F32):
        return A.alloc([1, n], dt)[:, 0, :]

    def ACT(out, in_, func, reads, writes, **kw):
        R.op("act", lambda e: e.activation(out=out, in_=in_, func=func, **kw), reads, writes)

    def TT(eng, out, a, b, op, reads, writes):
        R.op(eng, lambda e: e.tensor_tensor(out, a, b, op), reads, writes)

    def TS(eng, out, a, s1, s2, op0, op1, reads, writes):
        R.op(eng, lambda e: e.tensor_scalar(out, a, s1, s2, op0, op1), reads, writes)

    def TS1(eng, out, a, s1, op, reads, writes):
        R.op(eng, lambda e: e.tensor_single_scalar(out, a, s1, op), reads, writes)

    def STT(eng, out, a, s, b, op0, op1, reads, writes):
        R.op(eng, lambda e: e.scalar_tensor_tensor(out=out, in0=a, scalar=s, in1=b, op0=op0, op1=op1),
             reads, writes)

    def CP(eng, out, in_, reads, writes):
        if eng == "act":
            R.op("act", lambda e: e.activation(out=out, in_=in_, func=AF.Copy), reads, writes)
        else:
            R.op(eng, lambda e: e.tensor_copy(out, in_), reads, writes)

    def MM(out, lhsT, rhs, start, stop, reads, writes):
        R.op("pe", lambda e: e.matmul(out, lhsT, rhs, start=start, stop=stop), reads, writes)

    def TR(out, in_, ident, reads, writes):
        R.op("pe", lambda e: e.transpose(out, in_, ident), reads, writes)

    def MEMSET(eng, out, val, writes):
        R.op(eng, lambda e: e.memset(out, val), [], writes)

    def RECIP(out, in_, reads, writes):
        R.op("dve", lambda e: e.reciprocal(out, in_), reads, writes)

    def LD(out, in_, reads, writes, sem, eng="sp", nc_ok=False):
        if nc_ok:
            R.dma(eng, lambda e: e.dma_start(out=out, in_=in_, allow_slow_non_contiguous=True), reads, writes, sem)
        else:
            R.dma(eng, lambda e: e.dma_start(out=out, in_=in_), reads, writes, sem)

    xT = A.alloc([KC, TOK], F32)
    cmat = A.alloc([5, 128], F32)
    identf, LBD, UBD, LFULL, ONESF = (cmat[:, i, :] for i in range(5))
    cbf = A.alloc([3, 128], BF16)
    identb, maskbd, onesb = (cbf[:, i, :] for i in range(3))
    ind = a1(2)
    relv = a1(64)
    mcore = a1(16)
    nwT = A.alloc([nl * 4, KC], F32)
    gnwT = a1(8)
    epsc = a1(2)
    cosT = A.alloc([NT, 16], F32)
    sinT = A.alloc([NT, 16], F32)
    qcb = a1(TOK, BF16)
    PERSIST = A.top

    LD(cmat, cmat_in, [], ["cmat"], "k_cmat")
    LD(ind, ind_in, [], ["ind"], "k_ind")
    LD(relv, relv_in, [], ["relv"], "k_relv")
    LD(mcore, mcore_in, [], ["mcore"], "k_mcore")
    for i4 in range(nl * 4):
        LD(nwT[:, i4, :], nrm[i4 // 4, i4 % 4].rearrange("(kc p) -> p kc", p=128), [], ["nwT"], "k_nwT", nc_ok=True)
    LD(gnwT[:, 0:nl], gnw.rearrange("l p -> p l"), [], ["gnwT"], "k_gnwT", nc_ok=True)
    LD(cosT, cos_in.rearrange("(i p) f -> p i f", p=128), [], ["cosT"], "k_cosT")
    LD(sinT, sin_in.rearrange("(i p) f -> p i f", p=128), [], ["sinT"], "k_sinT")
    CP("dve", identb, identf, ["cmat"], ["cbf"])
    CP("dve", maskbd, LBD, ["cmat"], ["cbf"])
    CP("dve", onesb, ONESF, ["cmat"], ["cbf"])
    MEMSET("pool", epsc[:, 0:1], NORM_EPS, ["epsc"])
    MEMSET("pool", epsc[:, 1:2], SUBLN_EPS, ["epsc"])

    def eps_ap(eps):
        return epsc[:, 0:1] if eps == NORM_EPS else epsc[:, 1:2]

    xin_t = [a1(D) for _ in range(2)]
    qcf = a1(TOK)
    LD(qcf, qcrow_in, [], ["qcf"], "k_qcf")
    CP("dve", qcb, qcf, ["qcf"], ["qcb"])
    for i in range(NT):
        b = xin_t[i % 2]
        bk = "xin%d" % (i % 2)
        LD(b, x_in[i * 128:(i + 1) * 128, :], [], [bk], bk)
        for g in range(4):
            pi = (i * 4 + g) % 4
            for j in range(4):
                kc = g * 4 + j
                TR(ps[pi][:, j * 128:(j + 1) * 128], b[:, kc * 128:(kc + 1) * 128], identf, [bk, "cmat"], [PK[pi]])
            CP("act" if g % 2 else "dve", xT[:, g * 4:(g + 1) * 4, i * 128:(i + 1) * 128],
               ps[pi][:, :].rearrange("p (a b) -> p a b", b=128), [PK[pi]], ["xT"])
    R.barrier()
    A.top = PERSIST

    def rms_rstd(src, nchunks, ndiv, eps, sqb, rstd, key, t0, tn):
        nh = tn // 512
        for c in range(nchunks):
            sb = sqb[c % 2]
            sk = key + "sq%d" % (c % 2)
            ap, rk = src(c)
            ACT(sb[:, 0:tn], ap, AF.Square, rk, [sk])
            for h in range(nh):
                MM(ps[6 + h][:, :], onesb, sb[:, h * 512:(h + 1) * 512], c == 0, c == nchunks - 1,
                   [sk, "cbf"], [PK[6 + h]])
        rstd_finish(nh, ndiv, eps, rstd, key)

    def rstd_finish(nh, ndiv, eps, rstd, key):
        for h in range(nh):
            ACT(rstd[:, h * 512:(h + 1) * 512], ps[6 + h][:, :], AF.Sqrt, [PK[6 + h], "epsc"], [key + "rstd"],
                bias=eps_ap(eps), scale=1.0 / ndiv)
        RECIP(rstd[:, 0:nh * 512], rstd[:, 0:nh * 512], [key + "rstd"], [key + "rstd"])

    def wload(dst, src, key):
        R.dma("pool", lambda e: e.dma_start(out=dst, in_=src), [], [key], key)

    def evac(idx, out, in_, reads, writes):
        CP("act" if idx % 2 else "dve", out, in_, reads, writes)

    for li, l in enumerate(layers):
        lam_init = 0.8 - 0.6 * math.exp(-0.3 * l)
        A.top = PERSIST
        kT_loc, v_loc, S_loc, D_loc, kT_all, v_all, S_all, D_all = EXB[li % len(EXB)]

        hT = A.alloc([KC, TOK], BF16)
        sqb = [a1(TOK, BF16) for _ in range(2)]
        rstd = a1(TOK)
        rms_rstd(lambda c: (xT[:, c, :], ["xT"]), KC, float(D), NORM_EPS, sqb, rstd, "n1", 0, TOK)
        for c in range(KC):
            STT("dve", hT[:, c, :], xT[:, c, :], nwT[:, li * 4 + 0, c:c + 1], rstd,
                ALU.mult, ALU.mult, ["xT", "nwT", "n1rstd"], ["hT"])
        wbuf = [A.alloc([KC, 512], BF16) for _ in range(2)]
        stg = [a1(512) for _ in range(4)]
        w_l = w_in[li].rearrange("(kc p) c -> p kc c", p=128)
        tm_groups = [0, 1, 2, 3, 4, 5, 8, 9, 10, 11, 12, 13]
        fm_groups = [6, 7, 14, 15, 16, 17, 18, 19, 20, 21]
        order = [("tm", g) for g in tm_groups] if mode != "B" else []
        if mode != "A":
            order += [("fm", g) for g in fm_groups]
        sidx = 0
        pidx = 0
        for gi, (kind, g) in enumerate(order):
            wb = wbuf[gi % 2]
            wk = "wbuf%d" % (gi % 2)
            wload(wb, w_l[:, :, g * 512:(g + 1) * 512], wk)
            if kind == "tm":
                tcol = tm_groups.index(g) * 512
                for i in range(NT):
                    pi = pidx % 4
                    pidx += 1
                    for kc in range(KC):
                        MM(ps[pi][:, :], hT[:, kc, i * 128:(i + 1) * 128], wb[:, kc, :], kc == 0, kc == KC - 1,
                           ["hT", wk], [PK[pi]])
                    sb = stg[sidx % 4]
                    sk = "stg%d" % (sidx % 4)
                    evac(sidx, sb, ps[pi][:, :], [PK[pi]], [sk])
                    LD(proj_tm[i * 128:(i + 1) * 128, tcol:tcol + 512], sb, [sk], [("ptm", tcol // 1024, i)], sk)
                    sidx += 1
            else:
                frow0 = fm_groups.index(g) * 512
                for j in range(4):
                    for h in range(2):
                        pi = pidx % 4
                        pidx += 1
                        for kc in range(KC):
                            MM(ps[pi][:, :], wb[:, kc, j * 128:(j + 1) * 128], hT[:, kc, h * 512:(h + 1) * 512],
                               kc == 0, kc == KC - 1, ["hT", wk], [PK[pi]])
                        sb = stg[sidx % 4]
                        sk = "stg%d" % (sidx % 4)
                        evac(sidx, sb, ps[pi][:, :], [PK[pi]], [sk])
                        r0 = frow0 + j * 128
                        LD(proj_fm[r0:r0 + 128, h * 512:(h + 1) * 512], sb, [sk], [("pfm", r0 // 128)], sk)
                        sidx += 1
        R.barrier()
        A.top = PERSIST

        qsT = A.alloc([8, TOK], BF16)
        RAw = A.top
        ozT = A.alloc([8, TOK], F32)
        qT_off = A.top
        qT = A.alloc([8, TOK], BF16)
        S2TOP = A.top
        if mode == "B":
            LD(qsT, carry[0], [], ["qsT"], "k_cq")
            LD(ozT, carry[1], [], ["ozT"], "k_co")
            LD(qT, carry[2], [], ["qT"], "k_cT")
            R.barrier()
        else:
            qhT = A.alloc([8, 128], BF16)
            ktT = A.alloc([8, 128], BF16)
            Vt = a1(1024, BF16)
            Kh = a1(1024, BF16)
            glast = A.alloc([8, 16], F32)
            dec = A.alloc([8, 2], F32)
            dtot = a1(8)
            lb = a1(1024)
            oml = a1(1024)
            lbmark = A.top
            lbraw = A.alloc([4, 1024], F32)
            LD(lbraw, hlb.partition_broadcast(128), [], ["lbraw"], "k_lbraw")
            ACT(lbraw, lbraw, AF.Exp, ["lbraw"], ["lbraw"])
            TT("dve", lb, lbraw[:, 0, :], lbraw[:, 1, :], ALU.add, ["lbraw"], ["lb"])
            TT("dve", lb, lb, lbraw[:, 2, :], ALU.add, ["lbraw", "lb"], ["lb"])
            TT("dve", lb, lb, lbraw[:, 3, :], ALU.add, ["lbraw", "lb"], ["lb"])
            RECIP(oml, lb, ["lb"], ["oml"])
            if l == 0:
                MEMSET("dve", lb, 0.0, ["lb"])
            else:
                CP("dve", lb, lbraw[:, 1, :], ["lbraw", "oml"], ["lb"])
                for j in range(2, l + 1):
                    TT("dve", lb, lb, lbraw[:, j, :], ALU.add, ["lbraw", "lb"], ["lb"])
            TT("dve", lb, lb, oml, ALU.mult, ["lb", "oml"], ["lb"])
            TS("dve", oml, lb, -1.0, 1.0, ALU.mult, ALU.add, ["lb"], ["oml"])
            R.barrier()
            A.top = lbmark
            Tacc = a1(1024)
            tq = a1(1024)
            tf = a1(1024)
            ti = a1(1024)
            qf = a1(1024)
            lf = a1(1024)
            kk = a1(1024)
            eb = [a1(1024) for _ in range(2)]
            tb = [a1(1024, BF16) for _ in range(3)]
            Am8 = A.alloc([8, 128], BF16)
            Sb2 = A.alloc([8, 128], BF16)
            Sst = A.alloc([8, 128], F32)
            Sbf = A.alloc([8, 128], BF16)
            MEMSET("pool", Tacc, 0.0, ["Tacc"])
            MEMSET("pool", Sst, 0.0, [("Sst", h) for h in range(8)])

            for i in range(NT):
                rows = slice(i * 128, (i + 1) * 128)
                tsl = slice(i * 128, (i + 1) * 128)
                LD(tq, proj_tm[rows, 0:1024], [("ptm", 0, i)], ["tq"], "tq")
                LD(tf, proj_tm[rows, 1024:2048], [("ptm", 1, i)], ["tf"], "tf")
                LD(ti, proj_tm[rows, 2048:3072], [("ptm", 2, i)], ["ti"], "ti")
                ACT(qf, tq, AF.Silu, ["tq"], ["qf"])
                ACT(tf, tf, AF.Sigmoid, ["tf"], ["tf"])
                CP("pool", Vt, ti, ["ti"], ["Vt"])
                TT("dve", tf, tf, oml, ALU.mult, ["tf", "oml"], ["tf"])
                TT("dve", tf, tf, lb, ALU.add, ["tf", "lb"], ["tf"])
                ACT(lf, tf, AF.Ln, ["tf"], ["lf"])
                TS("dve", kk, tf, -1.0, 1.0, ALU.mult, ALU.add, ["tf"], ["kk"])
                for h in range(8):
                    MM(ps[4][:, h * 2:h * 2 + 2], lf[:, h * 128:(h + 1) * 128], ind, True, True, ["lf", "ind"], [PK[4]])
                CP("dve", glast[:, :, 2 * i:2 * i + 2], ps[4][:, 0:16].rearrange("p (h c) -> p h c", c=2),
                   [PK[4]], ["glast"])
                ACT(dec, glast[:, :, 2 * i:2 * i + 2], AF.Exp, ["glast"], ["dec"])
                for h2 in range(2):
                    cs = slice(h2 * 512, (h2 + 1) * 512)
                    MM(ps[0][:, :], LBD, lf[:, cs], True, True, ["lf", "cmat"], [PK[0]])
                    MM(ps[1][:, :], UBD, lf[:, cs], True, True, ["lf", "cmat"], [PK[1]])
                    MM(ps[2][:, :], LFULL, lf[:, cs], True, True, ["lf", "cmat"], [PK[2]])
                    MM(ps[3][:, :], ONESF, lf[:, cs], True, True, ["lf", "cmat"], [PK[3]])
                    ACT(eb[0][:, cs], ps[0][:, :], AF.Exp, [PK[0]], ["eb0"])
                    TT("dve", tb[0][:, cs], qf[:, cs], eb[0][:, cs], ALU.mult, ["qf", "eb0"], ["tb0"])
                    ACT(eb[1][:, cs], ps[0][:, :], AF.Exp, [PK[0]], ["eb1"], scale=-1.0)
                    TT("pool", tb[1][:, cs], kk[:, cs], eb[1][:, cs], ALU.mult, ["kk", "eb1"], ["tb1"])
                    ACT(eb[0][:, cs], ps[1][:, :], AF.Exp, [PK[1]], ["eb0"])
                    TT("dve", Kh[:, cs], kk[:, cs], eb[0][:, cs], ALU.mult, ["kk", "eb0"], ["Kh"])
                    TT("dve", eb[1][:, cs], ps[2][:, :], Tacc[:, cs], ALU.add, [PK[2], "Tacc"], ["eb1"])
                    ACT(eb[1][:, cs], eb[1][:, cs], AF.Exp, ["eb1"], ["eb1"])
                    TT("pool", tb[2][:, cs], qf[:, cs], eb[1][:, cs], ALU.mult, ["qf", "eb1"], ["tb2"])
                    TT("dve", Tacc[:, cs], Tacc[:, cs], ps[3][:, :], ALU.add, [PK[3], "Tacc"], ["Tacc"])
                tcount = 0
                for which in range(3):
                    for g in range(2):
                        pi = 5 + (tcount % 2)
                        tcount += 1
                        pbb = ps[pi][:, :].bitcast(BF16)
                        for j in range(4):
                            h = g * 4 + j
                            TR(pbb[:, j * 128:(j + 1) * 128], tb[which][:, h * 128:(h + 1) * 128], identb,
                               ["tb%d" % which, "cbf"], [PK[pi]])
                        src = pbb[:, 0:512].rearrange("p (a b) -> p a b", b=128)
                        if which == 0:
                            evac(g, qhT[:, g * 4:(g + 1) * 4, :], src, [PK[pi]], ["qhT"])
                        elif which == 1:
                            evac(g, ktT[:, g * 4:(g + 1) * 4, :], src, [PK[pi]], ["ktT"])
                        else:
                            evac(g, qsT[:, g * 4:(g + 1) * 4, tsl], src, [PK[pi]], ["qsT"])
                for h in range(8):
                    MM(ps[h // 4][:, (h % 4) * 128:(h % 4 + 1) * 128], ktT[:, h, :], qhT[:, h, :], True, True,
                       ["qhT", "ktT"], [PK[h // 4]])
                for h in range(8):
                    TT("dve", Am8[:, h, :], ps[h // 4][:, (h % 4) * 128:(h % 4 + 1) * 128], maskbd, ALU.mult,
                       [PK[h // 4], "cbf"], [("Am", h)])
                for h in range(8):
                    hs = slice(h * 128, (h + 1) * 128)
                    MM(ps[2 + h // 4][:, (h % 4) * 128:(h % 4 + 1) * 128], Kh[0:64, hs], Vt[0:64, hs], True, True,
                       ["Kh", "Vt"], [PK[2 + h // 4]])
                for h in range(8):
                    STT("dve", Sst[:, h, :], Sst[:, h, :], dec[:, h, 0:1], ps[2 + h // 4][:, (h % 4) * 128:(h % 4 + 1) * 128],
                        ALU.mult, ALU.add, [PK[2 + h // 4], "dec", ("Sst", h)], [("Sst", h)])
                    CP("pool", Sb2[:, h, :], Sst[:, h, :], [("Sst", h)], [("Sb2", h)])
                for h in range(8):
                    hs = slice(h * 128, (h + 1) * 128)
                    pb = 5 + h // 4
                    c0 = (h % 4) * 128
                    MM(ps[pb][:, c0:c0 + 128], Vt[:, hs], Am8[:, h, :], True, False, ["Vt", ("Am", h)], [PK[pb]])
                    if i > 0:
                        MM(ps[pb][:, c0:c0 + 64], Sbf[:, h, :], qhT[:, h, 0:64], False, False, [("Sbf", h), "qhT"], [PK[pb]])
                    MM(ps[pb][:, c0 + 64:c0 + 128], Sb2[:, h, :], qhT[:, h, 64:128], False, True, [("Sb2", h), "qhT"], [PK[pb]])
                for g in range(2):
                    CP("act", ozT[:, g * 4:(g + 1) * 4, tsl], ps[5 + g][:, :].rearrange("p (a b) -> p a b", b=128),
                       [PK[5 + g]], ["ozT"])
                for h in range(8):
                    hs = slice(h * 128, (h + 1) * 128)
                    MM(ps[2 + h // 4][:, (h % 4) * 128:(h % 4 + 1) * 128], Kh[64:128, hs], Vt[64:128, hs], True, True,
                       ["Kh", "Vt"], [PK[2 + h // 4]])
                for h in range(8):
                    STT("dve", Sst[:, h, :], Sst[:, h, :], dec[:, h, 1:2], ps[2 + h // 4][:, (h % 4) * 128:(h % 4 + 1) * 128],
                        ALU.mult, ALU.add, [PK[2 + h // 4], "dec", ("Sst", h)], [("Sst", h)])
                    CP("pool", Sbf[:, h, :], Sst[:, h, :], [("Sst", h)], [("Sbf", h)])
            LD(S_loc.rearrange("(h k) v -> k h v", k=128), Sst, [("Sst", h) for h in range(8)], ["S_loc"], "sloc")
            R.op("dve", lambda e: e.tensor_reduce(out=dtot, in_=glast, axis=AX.X, op=ALU.add), ["glast"], ["dtot"])
            ACT(dtot, dtot, AF.Exp, ["dtot"], ["dtot"])
            LD(D_loc, dtot, ["dtot"], ["D_loc"], "sloc")
            R.barrier()
            A.top = S2TOP

            kTs = A.alloc([8, TOK], BF16)
            tqs = [a1(1024) for _ in range(2)]
            tfs = [a1(1024) for _ in range(2)]
            tis = [a1(1024) for _ in range(2)]
            qb = a1(1024, BF16)
            kb = a1(1024, BF16)
            vb = [a1(1024, BF16) for _ in range(2)]
            rt = [A.alloc([8, 16], F32) for _ in range(4)]
            for i in range(NT):
                rows = slice(i * 128, (i + 1) * 128)
                tsl = slice(i * 128, (i + 1) * 128)
                tq, tf, ti = tqs[i % 2], tfs[i % 2], tis[i % 2]
                tqk, tfk, tik = "tq%d" % (i % 2), "tf%d" % (i % 2), "ti%d" % (i % 2)
                LD(tq, proj_tm[rows, 3072:4096], [("ptm", 3, i)], [tqk], tqk)
                LD(tf, proj_tm[rows, 4096:5120], [("ptm", 4, i)], [tfk], tfk)
                LD(ti, proj_tm[rows, 5120:6144], [("ptm", 5, i)], [tik], tik)
                TS1("pool", tq, tq, float(128.0 ** -0.5), ALU.mult, [tqk], [tqk])
                cb = cosT[:, i, :].unsqueeze(1).broadcast_to([128, 8, 16])
                sn = sinT[:, i, :].unsqueeze(1).broadcast_to([128, 8, 16])
                for (src, dst, sk_, dk_, e1, e2) in ((tq, qb, tqk, "qb", "dve", "pool"), (tf, kb, tfk, "kb", "pool", "dve")):
                    v3 = src.rearrange("p (j d) -> p j d", d=128)
                    o3 = dst.rearrange("p (j d) -> p j d", d=128)
                    t1 = v3[:, :, 0:16]
                    t2 = v3[:, :, 16:32]
                    TT(e1, rt[0], t1, cb, ALU.mult, [sk_, "cosT"], ["rt0"])
                    TT(e2, rt[1], t2, sn, ALU.mult, [sk_, "sinT"], ["rt1"])
                    TT(e1, o3[:, :, 0:16], rt[0], rt[1], ALU.subtract, ["rt0", "rt1"], [dk_])
                    TT(e1, rt[2], t2, cb, ALU.mult, [sk_, "cosT"], ["rt2"])
                    TT(e2, rt[3], t1, sn, ALU.mult, [sk_, "sinT"], ["rt3"])
                    TT(e2, o3[:, :, 16:32], rt[2], rt[3], ALU.add, ["rt2", "rt3"], [dk_])
                    CP("act", o3[:, :, 32:128], v3[:, :, 32:128], [sk_], [dk_])
                v_ = vb[i % 2]
                vk = "vb%d" % (i % 2)
                CP("act", v_, ti, [tik], [vk])
                LD(v_loc[rows, :], v_, [vk], ["v_loc"], vk)
                tcount = 0
                for (srcb, sk_, dstT, dk_) in ((qb, "qb", qT, "qT"), (kb, "kb", kTs, "kTs")):
                    for g in range(2):
                        pi = 4 + (tcount % 4)
                        tcount += 1
                        pbb = ps[pi][:, :].bitcast(BF16)
                        for j in range(4):
                            h = g * 4 + j
                            TR(pbb[:, j * 128:(j + 1) * 128], srcb[:, h * 128:(h + 1) * 128], identb, [sk_, "cbf"], [PK[pi]])
                        evac(g, dstT[:, g * 4:(g + 1) * 4, tsl], pbb[:, 0:512].rearrange("p (a b) -> p a b", b=128),
                             [PK[pi]], [dk_])
            LD(kT_loc.rearrange("(j d) t -> d j t", d=128), kTs, ["kTs"], ["kT_loc"], "kTs")
            R.barrier()
            A.top = S2TOP
        if mode == "A":
            LD(carry[0], qsT, [], ["c0_"], "k_cq")
            LD(carry[1], ozT, [], ["c1_"], "k_co")
            LD(carry[2], qT, [], ["c2_"], "k_cT")
            R.barrier()
        if mode == "A" or stop <= 3:
            break

        if mode == "fused":
            rg = [list(range(NCORES))]
            for (src, dst, nm) in ((kT_loc, kT_all, "kT"), (v_loc, v_all, "v"), (S_loc, S_all, "S"), (D_loc, D_all, "D")):
                R.dma("pool", lambda e, src=src, dst=dst: e.collective_compute(
                    "AllGather", ALU.bypass, replica_groups=rg, ins=[src[:, :]], outs=[dst[:, :]]),
                    [], [nm + "_all"], "cc")
            R.barrier()

        oaT = qsT
        Sr = [A.alloc([8, 128], F32) for _ in range(2)]
        Dall = A.alloc([8, 8], F32)
        Dp = a1(8)
        acc = A.alloc([8, 128], F32)
        sinb = A.alloc([8, 128], BF16)
        sq4s = [a1(TOK, BF16) for _ in range(2)]
        rstd4s = [a1(TOK) for _ in range(2)]
        gt = [a1(TOK) for _ in range(2)]
        tmp4s = [a1(TOK) for _ in range(2)]
        LD(Dall, D_all.rearrange("(r k) h -> k r h", k=128), [], ["Dall"], "k_Dall")
        MEMSET("pool", acc, 0.0, ["acc"])
        for r in range(NCORES - 1):
            sr = Sr[r % 2]
            srk = "Sr%d" % (r % 2)
            LD(sr, S_all[r * 1024:(r + 1) * 1024, :].rearrange("(h k) v -> k h v", k=128), [], [srk], srk)
            TS("dve", Dp, Dall[:, r, :], mcore[:, r:r + 1], mcore[:, 8 + r:9 + r], ALU.mult, ALU.add,
               ["Dall", "mcore"], ["Dp"])
            TS1("pool", sr, sr, mcore[:, r:r + 1], ALU.mult, [srk, "mcore"], [srk])
            for h in range(8):
                STT("dve", acc[:, h, :], acc[:, h, :], Dp[:, h:h + 1], sr[:, h, :], ALU.mult, ALU.add,
                    ["acc", "Dp", srk], ["acc"])
        CP("dve", sinb, acc, ["acc"], ["sinb"])
        for h in range(8):
            g_ = gt[h % 2]
            gk = "gt%d" % (h % 2)
            LD(g_, proj_fm[h * 128:(h + 1) * 128, :], [("pfm", h)], [gk], gk)
            ACT(g_, g_, AF.Silu, [gk], [gk])
            for hf in range(2):
                cs = slice(hf * 512, (hf + 1) * 512)
                MM(ps[hf][:, :], sinb[:, h, :], qsT[:, h, cs], True, True, ["sinb", ("qsT", h)], [PK[hf]])
                TT("dve", ozT[:, h, cs], ozT[:, h, cs], ps[hf][:, :], ALU.add, [PK[hf], ("ozT", h)], [("ozT", h)])
            sq4, rstd4, tmp4 = sq4s[h % 2], rstd4s[h % 2], tmp4s[h % 2]
            p4 = "%d" % (h % 2)
            ACT(sq4, ozT[:, h, :], AF.Square, [("ozT", h)], ["sq4" + p4])
            for hf in range(2):
                MM(ps[6 + hf][:, :], onesb, sq4[:, hf * 512:(hf + 1) * 512], True, True, ["sq4" + p4, "cbf"], [PK[6 + hf]])
            rstd_finish(2, 128.0, NORM_EPS, rstd4, "g4" + p4)
            STT("dve", tmp4, ozT[:, h, :], gnwT[:, li:li + 1], rstd4, ALU.mult, ALU.mult,
                [("ozT", h), "gnwT", "g4" + p4 + "rstd"], ["tmp4" + p4])
            TT("pool", oaT[:, h, :], tmp4, g_, ALU.mult, ["tmp4" + p4, gk, ("qsT", h)], [("qsT", h)])
        R.barrier()
        A.top = S2TOP

        if stop <= 4:
            break
        obT = arena_t[:, RAw:RAw + 4096].bitcast(BF16).rearrange("p (a b) -> p a b", b=TOK)
        NSL = 3
        kres = A.alloc([2, SEQ], BF16)
        vbuf = [A.alloc([8, 257], BF16) for _ in range(NSL)]
        SBK = [0, 1, 6, 7]
        LA = 3
        pT = [a1(512, BF16) for _ in range(4)]
        pM = [a1(512, BF16) for _ in range(4)]
        lp = A.alloc([4, 128], F32)
        lpp = a1(128)
        lamc = a1(4)
        slwb = a1(256)
        rr = a1(4)
        of = a1(256)
        tmpo = a1(256)
        onb = a1(256, BF16)
        junk = a1(256)
        for s in range(NSL):
            MEMSET("pool", vbuf[s][:, :, 256:257], 1.0, ["vbuf%d" % s])
        LD(lp, dlam[li].partition_broadcast(128), [], ["lp"], "k_lp")
        TT("dve", lpp, lp[:, 0, :], lp[:, 1, :], ALU.mult, ["lp"], ["lpp"])
        R.op("dve", lambda e: e.tensor_reduce(out=lamc[:, 0:1], in_=lpp, axis=AX.X, op=ALU.add), ["lpp"], ["lamc"])
        TT("dve", lpp, lp[:, 2, :], lp[:, 3, :], ALU.mult, ["lp", "lamc"], ["lpp"])
        R.op("dve", lambda e: e.tensor_reduce(out=lamc[:, 1:2], in_=lpp, axis=AX.X, op=ALU.add), ["lpp"], ["lamc"])
        ACT(lamc[:, 0:2], lamc[:, 0:2], AF.Exp, ["lamc"], ["lamc"])
        TT("dve", lamc[:, 2:3], lamc[:, 0:1], lamc[:, 1:2], ALU.subtract, ["lamc"], ["lamc"])
        TS1("dve", lamc[:, 2:3], lamc[:, 2:3], float(lam_init), ALU.add, ["lamc"], ["lamc"])
        LD(slwb, slw[li].partition_broadcast(128),
           [], ["slwb"], "k_slwb")
        TS1("dve", slwb, slwb, float(1.0 - lam_init), ALU.mult, ["slwb"], ["slwb"])
        step = 0
        for h in range(4):
            for r in range(NCORES):
                LD(kres[:, :, r * 1024:(r + 1) * 1024],
                   kT_all[(r * 8 + 2 * h) * 128:(r * 8 + 2 * h + 2) * 128, :].rearrange("(c d) t -> d c t", d=128),
                   ["kT_all"], ["kres%d" % r], "kres%d" % r)
            for qg in range(4):
                qs_ = slice(qg * 256, (qg + 1) * 256)
                qc2 = qcb[:, qs_].unsqueeze(1).broadcast_to([128, 2, 256])
                steps = [(r, kt) for r in range(NCORES) for kt in range(8)]
                slot_of = {}

                def seg_load(r):
                    nonlocal step
                    s_ = step % NSL
                    step += 1
                    slot_of[r] = s_
                    vbk = "vbuf%d" % s_
                    LD(vbuf[s_][:, :, 0:256], v_all[r * 1024:(r + 1) * 1024, h * 256:(h + 1) * 256].rearrange(
                        "(kt p) e -> p kt e", p=128), ["v_all"], [vbk], vbk)

                def scores(r, kt):
                    s_ = slot_of[r]
                    ktg = r * 8 + kt
                    sb_ = ktg % 4
                    for c in range(2):
                        MM(ps[SBK[sb_]][:, c * 256:(c + 1) * 256], kres[:, c, ktg * 128:(ktg + 1) * 128],
                           qT[:, 2 * h + c, qs_], True, True, ["kres%d" % r, "qT"], [PK[SBK[sb_]]])

                def softmax_part(r, kt):
                    ktg = r * 8 + kt
                    sb_ = ktg % 4
                    ACT(pT[sb_], ps[SBK[sb_]][:, :], AF.Exp, [PK[SBK[sb_]]], ["pT%d" % sb_])
                    STT("dve", pM[sb_].rearrange("p (c q) -> p c q", c=2), qc2, relv[:, ktg:ktg + 1],
                        pT[sb_].rearrange("p (c q) -> p c q", c=2), ALU.is_ge, ALU.mult,
                        ["pT%d" % sb_, "qcb", "relv"], ["pM%d" % sb_])

                def pv(r, kt):
                    s_ = slot_of[r]
                    ktg = r * 8 + kt
                    sb_ = ktg % 4
                    for c in range(2):
                        for qt in range(2):
                            ai = 2 + c * 2 + qt
                            MM(ps[ai][:, 0:257], pM[sb_][:, c * 256 + qt * 128:c * 256 + (qt + 1) * 128],
                               vbuf[s_][:, kt, :], ktg == 0, ktg == 63, ["pM%d" % sb_, "vbuf%d" % s_], [PK[ai]])

                seg_load(0)
                seg_load(1)
                for k0 in range(LA):
                    scores(*steps[k0])
                for k, (r, kt) in enumerate(steps):
                    softmax_part(r, kt)
                    if k + LA < len(steps):
                        r2, kt2 = steps[k + LA]
                        if kt2 == 0 and r2 + 1 < NCORES:
                            seg_load(r2 + 1)
                        scores(r2, kt2)
                    pv(r, kt)
                for qt in range(2):
                    a0, a1_ = 2 + qt, 4 + qt
                    tcol = slice(qg * 256 + qt * 128, qg * 256 + (qt + 1) * 128)
                    RECIP(rr[:, 0:1], ps[a0][:, 256:257], [PK[a0]], ["rr"])
                    RECIP(rr[:, 1:2], ps[a1_][:, 256:257], [PK[a1_], "rr"], ["rr"])
                    TT("dve", rr[:, 1:2], rr[:, 1:2], lamc[:, 2:3], ALU.mult, ["rr", "lamc"], ["rr"])
                    TS1("dve", tmpo, ps[a1_][:, 0:256], rr[:, 1:2], ALU.mult, [PK[a1_], "rr"], ["tmpo"])
                    STT("dve", of, ps[a0][:, 0:256], rr[:, 0:1], tmpo, ALU.mult, ALU.subtract,
                        [PK[a0], "rr", "tmpo"], ["of"])
                    TT("dve", junk, of, of, ALU.mult, ["of"], ["junk"])
                    R.op("dve", lambda e: e.tensor_reduce(out=rr[:, 2:3], in_=junk, axis=AX.X, op=ALU.add), ["junk"], ["ss"])
                    ACT(rr[:, 3:4], rr[:, 2:3], AF.Sqrt, ["ss", "epsc"], ["rs"], bias=eps_ap(SUBLN_EPS), scale=1.0 / 256.0)
                    RECIP(rr[:, 3:4], rr[:, 3:4], ["rs"], ["rs"])
                    STT("dve", onb, of, rr[:, 3:4], slwb, ALU.mult, ALU.mult, ["of", "rs", "slwb"], ["onb"])
                    pbb = ps[6 + qt][:, :].bitcast(BF16)
                    for e2 in range(2):
                        TR(pbb[:, e2 * 128:(e2 + 1) * 128], onb[:, e2 * 128:(e2 + 1) * 128], identb, ["onb", "cbf"], [PK[6 + qt]])
                    evac(qt, obT[:, 2 * h:2 * h + 2, tcol], pbb[:, 0:256].rearrange("p (a b) -> p a b", b=128),
                         [PK[6 + qt]], ["obT"])
        R.barrier()

        if stop <= 5:
            break
        mT = arena_t[:, RAw + 4096:RAw + 4096 + 8192].bitcast(BF16).rearrange("p (a b) -> p a b", b=TOK)
        A.top = S2TOP
        wab = [A.alloc([2, 8, 512], BF16) for _ in range(2)]
        gat = [a1(TOK) for _ in range(2)]
        gbt = [a1(TOK) for _ in range(2)]
        m1 = a1(TOK)
        m2 = a1(TOK)
        wa_l = w_a[li].rearrange("(kc p) c -> p kc c", p=128)
        wb_l = w_b[li].rearrange("(kc p) c -> p kc c", p=128)
        for j in range(KC):
            g = j // 4
            s = g % 2
            wk = "wab%d" % s
            if j % 4 == 0:
                wload(wab[s][:, 0, :, :], wa_l[:, :, g * 512:(g + 1) * 512], wk)
                wload(wab[s][:, 1, :, :], wb_l[:, :, g * 512:(g + 1) * 512], wk)
            jc = slice((j % 4) * 128, (j % 4 + 1) * 128)
            pb0 = (j % 2) * 4
            ga_, gb_ = gat[j % 2], gbt[j % 2]
            gak, gbk = "gat%d" % (j % 2), "gbt%d" % (j % 2)
            LD(ga_, proj_fm[1024 + j * 128:1024 + (j + 1) * 128, :], [("pfm", 8 + j)], [gak], gak)
            LD(gb_, proj_fm[3072 + j * 128:3072 + (j + 1) * 128, :], [("pfm", 24 + j)], [gbk], gbk)
            ACT(ga_, ga_, AF.Sigmoid, [gak], [gak])
            ACT(gb_, gb_, AF.Sigmoid, [gbk], [gbk])
            for br, srcT, sk_ in ((0, oaT, "qsT"), (1, obT, "obT")):
                for hf in range(2):
                    pi = pb0 + br * 2 + hf
                    for kc in range(8):
                        MM(ps[pi][:, :], wab[s][:, br, kc, jc], srcT[:, kc, hf * 512:(hf + 1) * 512], kc == 0, kc == 7,
                           [wk, sk_], [PK[pi]])
            for hf in range(2):
                cs = slice(hf * 512, (hf + 1) * 512)
                TT("dve", m1[:, cs], ga_[:, cs], ps[pb0 + hf][:, :], ALU.mult, [gak, PK[pb0 + hf]], ["m1"])
                TT("dve", m2[:, cs], gb_[:, cs], ps[pb0 + 2 + hf][:, :], ALU.mult, [gbk, PK[pb0 + 2 + hf]], ["m2"])
            TT("pool", mT[:, j, :], m1, m2, ALU.add, ["m1", "m2"], ["mT"])
        R.barrier()

        if stop <= 6:
            break
        A.top = qT_off + 4096
        wo = [A.alloc([KC, 512], BF16) for _ in range(2)]
        zT = A.alloc([KC, 512], F32)
        sq6 = [a1(512, BF16) for _ in range(2)]
        rstd6 = a1(512)
        tmp6 = a1(512)
        wo_l = w_o[li].rearrange("(kc p) c -> p kc c", p=128)
        widx = 0
        for th in range(2):
            tcs = slice(th * 512, (th + 1) * 512)
            for jo in range(KC):
                g = jo // 4
                if jo % 4 == 0:
                    s = widx % 2
                    widx += 1
                    wk = "wo%d" % s
                    wload(wo[s], wo_l[:, :, g * 512:(g + 1) * 512], wk)
                pi = jo % 4
                for kc in range(KC):
                    MM(ps[pi][:, :], wo[s][:, kc, (jo % 4) * 128:(jo % 4 + 1) * 128], mT[:, kc, tcs], kc == 0, kc == KC - 1,
                       [wk, "mT"], [PK[pi]])
                CP("dve", zT[:, jo, :], ps[pi][:, :], [PK[pi]], ["zT"])
                sb = sq6[jo % 2]
                sk = "sq6%d" % (jo % 2)
                ACT(sb, ps[pi][:, :], AF.Square, [PK[pi]], [sk])
                MM(ps[6][:, :], onesb, sb, jo == 0, jo == KC - 1, [sk, "cbf"], [PK[6]])
            rstd_finish(1, float(D), NORM_EPS, rstd6, "n6")
            for c in range(KC):
                STT("dve", tmp6, zT[:, c, :], nwT[:, li * 4 + 1, c:c + 1], rstd6, ALU.mult, ALU.mult,
                    ["zT", "nwT", "n6rstd"], ["tmp6"])
                TT("pool", xT[:, c, tcs], xT[:, c, tcs], tmp6, ALU.add, ["tmp6", "xT"], ["xT"])
        R.barrier()
        A.top = PERSIST

        if stop <= 7:
            break
        sqb = [a1(TOK, BF16) for _ in range(2)]
        rstd = a1(TOK)
        ovl = A.top
        h2T = A.alloc([KC, TOK], BF16)
        rms_rstd(lambda c: (xT[:, c, :], ["xT"]), KC, float(D), NORM_EPS, sqb, rstd, "n7", 0, TOK)
        for c in range(KC):
            STT("dve", h2T[:, c, :], xT[:, c, :], nwT[:, li * 4 + 2, c:c + 1], rstd,
                ALU.mult, ALU.mult, ["xT", "nwT", "n7rstd"], ["h2T"])
        wgu = [A.alloc([2, KC, 256], BF16) for _ in range(2)]
        sgt = [a1(TOK) for _ in range(2)]
        ast = [a1(TOK, BF16) for _ in range(2)]
        wf1_l = w_f1[li].rearrange("(kc p) c -> p kc c", p=128)
        wf2_l = w_f2[li].rearrange("(kc p) c -> p kc c", p=128)
        widx = 0
        for j in range(FC):
            g = j // 2
            if j % 2 == 0:
                s = widx % 2
                widx += 1
                wk = "wgu%d" % s
                wload(wgu[s][:, 0, :, :], wf1_l[:, :, g * 256:(g + 1) * 256], wk)
                wload(wgu[s][:, 1, :, :], wf1_l[:, :, FFH + g * 256:FFH + (g + 1) * 256], wk)
            jc = slice((j % 2) * 128, (j % 2 + 1) * 128)
            pb0 = (j % 2) * 4
            for hf in range(2):
                tcs = slice(hf * 512, (hf + 1) * 512)
                for kc in range(KC):
                    MM(ps[pb0 + hf][:, :], wgu[s][:, 0, kc, jc], h2T[:, kc, tcs], kc == 0, kc == KC - 1,
                       [wk, "h2T"], [PK[pb0 + hf]])
            for hf in range(2):
                tcs = slice(hf * 512, (hf + 1) * 512)
                for kc in range(KC):
                    MM(ps[pb0 + 2 + hf][:, :], wgu[s][:, 1, kc, jc], h2T[:, kc, tcs], kc == 0, kc == KC - 1,
                       [wk, "h2T"], [PK[pb0 + 2 + hf]])
            sg_ = sgt[j % 2]
            sgk = "sgt%d" % (j % 2)
            as_ = ast[j % 2]
            ask = "ast%d" % (j % 2)
            for hf in range(2):
                tcs = slice(hf * 512, (hf + 1) * 512)
                ACT(sg_[:, tcs], ps[pb0 + hf][:, :], AF.Silu, [PK[pb0 + hf]], [sgk])
                TT("dve", as_[:, tcs], sg_[:, tcs], ps[pb0 + 2 + hf][:, :], ALU.mult, [sgk, PK[pb0 + 2 + hf]], [ask])
            LD(a_scr[j * 128:(j + 1) * 128, :], as_, [ask], ["a_scr"], ask)
        R.barrier()
        A.top = ovl
        aT = A.alloc([FC, 512], BF16)
        w2 = [A.alloc([FC, 128], BF16) for _ in range(2)]
        y2 = A.alloc([KC, 512], F32)
        rstd8 = a1(512)
        tmp8 = a1(512)
        w2idx = 0
        for th in range(2):
            tcs = slice(th * 512, (th + 1) * 512)
            LD(aT, a_scr.rearrange("(j p) t -> p j t", p=128)[:, :, tcs], ["a_scr"], ["aT"], "aT")
            for jo in range(KC):
                s2 = w2idx % 2
                w2idx += 1
                wk2 = "w2%d" % s2
                wload(w2[s2], wf2_l[:, :, jo * 128:(jo + 1) * 128], wk2)
                pi = 4 + (jo % 2)
                for kc in range(FC):
                    MM(ps[pi][:, :], w2[s2][:, kc, :], aT[:, kc, :], kc == 0, kc == FC - 1, [wk2, "aT"], [PK[pi]])
                CP("dve", y2[:, jo, :], ps[pi][:, :], [PK[pi]], ["y2"])
                sb = sqb[jo % 2]
                sk = "sq7%d" % (jo % 2)
                ACT(sb[:, 0:512], ps[pi][:, :], AF.Square, [PK[pi]], [sk])
                MM(ps[6][:, :], onesb, sb[:, 0:512], jo == 0, jo == KC - 1, [sk, "cbf"], [PK[6]])
            rstd_finish(1, float(D), NORM_EPS, rstd8, "n8")
            for c in range(KC):
                STT("dve", tmp8, y2[:, c, :], nwT[:, li * 4 + 3, c:c + 1], rstd8, ALU.mult, ALU.mult,
                    ["y2", "nwT", "n8rstd"], ["tmp8"])
                TT("pool", xT[:, c, tcs], xT[:, c, tcs], tmp8, ALU.add, ["tmp8", "xT"], ["xT"])
        R.barrier()

    if dbg:
        for nm in dbg:
            t_ = {"oaT": lambda: oaT, "obT": lambda: obT, "mT": lambda: mT, "qT": lambda: qT}[nm]()
            dd = nc.dram_tensor("dbg_" + nm, [128, t_.shape[1], t_.shape[2]], BF16, kind="ExternalOutput").ap()
            LD(dd, t_, [], ["dbg" + nm], "dbg")
        R.barrier()
    if mode != "A":
        A.top = PERSIST
        ost = [a1(512) for _ in range(4)]
        oi = 0
        for i in range(NT):
            for g in range(4):
                pi = oi % 4
                for j in range(4):
                    kc = g * 4 + j
                    TR(ps[pi][:, j * 128:(j + 1) * 128], xT[:, kc, i * 128:(i + 1) * 128], identf, ["xT", "cmat"], [PK[pi]])
                sb = ost[oi % 4]
                sk = "ost%d" % (oi % 4)
                evac(oi, sb, ps[pi][:, :], [PK[pi]], [sk])
                LD(x_out[i * 128:(i + 1) * 128, g * 512:(g + 1) * 512], sb, [sk], ["x_out"], sk)
                oi += 1
    R.barrier()
    with nc.Block() as block:
        R.emit(block)
    es.close()
    return nc


def _consts(core):
    p = np.arange(128)
    ident = np.eye(128, dtype=np.float32)
    same = (p[:, None] // 64) == (p[None, :] // 64)
    LBD = (same & (p[:, None] <= p[None, :])).astype(np.float32)
    UBD = (same & (p[:, None] > p[None, :])).astype(np.float32)
    LF = (p[:, None] <= p[None, :]).astype(np.float32)
    ONES = np.ones((128, 128), np.float32)
    cmat = np.stack([ident, LBD, UBD, LF, ONES], axis=1).astype(np.float32)
    ind = np.stack([(p < 64), (p >= 64)], axis=1).astype(np.float32)
    kt = np.arange(64)
    relv = (2 * kt[None, :] + (p[:, None] // 64) - 16 * core).astype(np.float32)
    qcrow = np.broadcast_to((np.arange(TOK) // 64).astype(np.float32)[None, :], (128, TOK)).copy()
    m = (np.arange(8) < core).astype(np.float32)
    mcore = np.broadcast_to(np.concatenate([m, 1.0 - m])[None, :], (128, 16)).astype(np.float32).copy()
    pos = np.arange(core * TOK, (core + 1) * TOK, dtype=np.float32)
    inv_freq = (np.float32(500000.0) ** (-np.arange(0, 32, 2, dtype=np.float32) / np.float32(32))).astype(np.float32)
    ang = (pos[:, None] * inv_freq[None, :]).astype(np.float32)
    return dict(cmat=cmat, ind=ind, relv=relv, qcrow=qcrow, mcore=mcore,
                rope_cos=np.cos(ang).astype(np.float32), rope_sin=np.sin(ang).astype(np.float32))


_PROGS = {}


def _prog(mode, layers):
    key = (mode, tuple(layers))
    if key not in _PROGS:
        _PROGS[key] = build_program(mode, list(layers))
    return _PROGS[key]


WNAMES = ["w_in", "hgrn_gnorm_w", "diff_lambda", "diff_subln_w", "w_branch_a", "w_branch_b", "w_out",
          "norm_w", "w_ffn_in", "w_ffn_out"]

FUSED = False


def kernel(**inputs):
    inp = {k: np.ascontiguousarray(np.asarray(v)) for k, v in inputs.items()}
    x = inp["x"].reshape(SEQ, D)
    consts = [_consts(c) for c in range(NCORES)]
    xs = [np.ascontiguousarray(x[c * TOK:(c + 1) * TOK]) for c in range(NCORES)]
    if FUSED:
        nc = _prog("fused", range(DEPTH))
        in_maps = []
        for c in range(NCORES):
            m = dict(consts[c])
            m["x_loc"] = xs[c]
            m["hgrn_lower_bounds"] = inp["hgrn_lower_bounds"]
            for n in WNAMES:
                m[n] = inp[n]
            in_maps.append(m)
        res = run_bass_kernel_spmd(nc, in_maps, core_ids=list(range(NCORES)))
        out = np.concatenate([r["x_out"] for r in res.results], axis=0)
        return out.reshape(1, SEQ, D).astype(np.float32)
    for l in range(DEPTH):
        wl = {n: np.ascontiguousarray(inp[n][l:l + 1]) for n in WNAMES}
        base = []
        for c in range(NCORES):
            m = dict(consts[c])
            m["x_loc"] = xs[c]
            m["hgrn_lower_bounds"] = inp["hgrn_lower_bounds"]
            m.update(wl)
            base.append(m)
        ncA = _prog("A", [l])
        resA = run_bass_kernel_spmd(ncA, base, core_ids=list(range(NCORES)))
        kT_all = np.concatenate([r["kT_loc"] for r in resA.results], axis=0)
        v_all = np.concatenate([r["v_loc"] for r in resA.results], axis=0)
        S_all = np.concatenate([r["S_loc"] for r in resA.results], axis=0)
        D_all = np.concatenate([r["D_loc"] for r in resA.results], axis=0)
        ncB = _prog("B", [l])
        inB = []
        for c in range(NCORES):
            m = dict(base[c])
            m.update(kT_all=kT_all, v_all=v_all, S_all=S_all, D_all=D_all)
            for nm in ("c_qsT", "c_ozT", "c_qT"):
                m[nm] = resA.results[c][nm]
            inB.append(m)
        resB = run_bass_kernel_spmd(ncB, inB, core_ids=list(range(NCORES)))
        xs = [np.ascontiguousarray(r["x_out"]) for r in resB.results]
    out = np.concatenate(xs, axis=0)
    return out.reshape(1, SEQ, D).astype(np.float32)
```

```python
import math
from contextlib import ExitStack
import numpy as np
import concourse.bass as bass
import concourse.mybir as mybir
from concourse.bass_utils import run_bass_kernel_spmd

F32 = mybir.dt.float32
BF16 = mybir.dt.bfloat16
AF = mybir.ActivationFunctionType
ALU = mybir.AluOpType
AX = mybir.AxisListType

NCORES = 8
DEPTH = 4
D = 2048
SEQ = 8192
TOK = SEQ // NCORES
NT = TOK // 128
KC = D // 128
INW = 11264
FFH = 5632
FC = FFH // 128
NORM_EPS = 1e-6
SUBLN_EPS = 1e-5
SAME_ENGINE_SYNC = True


class Rec:
    ENG = ["pe", "act", "dve", "pool", "sp"]

    def __init__(self, nc, es):
        self.nc = nc
        self.es = es
        self.ops = {e: [] for e in self.ENG}
        self.cnt = {e: 0 for e in self.ENG}
        self.esem = {e: es.enter_context(nc.semaphore("S_" + e)) for e in self.ENG}
        self.seen = {e: {} for e in self.ENG}
        self.lastw = {}
        self.readers = {}
        self.dsem = {}

    def _dsem(self, name):
        if name not in self.dsem:
            self.dsem[name] = [self.es.enter_context(self.nc.semaphore("D_" + name)), 0]
        return self.dsem[name]

    def _need(self, eng, tok, waits):
        kind, key, val = tok
        if kind == "E":
            if key == eng and (eng in ("pe", "sp") or not SAME_ENGINE_SYNC):
                return
            sem = self.esem[key]
        else:
            sem = self.dsem[key][0]
        sk = (kind, key)
        if self.seen[eng].get(sk, 0) >= val:
            return
        self.seen[eng][sk] = val
        waits.append((sem, val))

    def _deps(self, eng, reads, writes):
        waits = []
        for k in reads:
            t = self.lastw.get(k)
            if t is not None:
                self._need(eng, t, waits)
        for k in writes:
            t = self.lastw.get(k)
            if t is not None:
                self._need(eng, t, waits)
            for sk, v in self.readers.get(k, {}).items():
                self._need(eng, (sk[0], sk[1], v), waits)
        return waits

    def _commit(self, tok, reads, writes):
        sk = (tok[0], tok[1])
        for k in reads:
            d = self.readers.setdefault(k, {})
            if d.get(sk, 0) < tok[2]:
                d[sk] = tok[2]
        for k in writes:
            self.lastw[k] = tok
            self.readers[k] = {}

    @staticmethod
    def _excl(reads, writes):
        r2 = [k for k in reads if not (isinstance(k, str) and k.startswith("ps"))]
        w2 = list(writes) + [k for k in reads if isinstance(k, str) and k.startswith("ps") and k not in writes]
        return r2, w2

    def op(self, eng, fn, reads=(), writes=()):
        reads, writes = self._excl(reads, writes)
        waits = self._deps(eng, reads, writes)
        self.cnt[eng] += 1
        tok = ("E", eng, self.cnt[eng])
        self._commit(tok, reads, writes)
        self.ops[eng].append((waits, fn, (self.esem[eng], 1)))

    def dma(self, eng, fn, reads, writes, sem):
        ds = self._dsem(sem)
        waits = self._deps(eng, reads, writes)
        ds[1] += 16
        tok = ("D", sem, ds[1])
        self._commit(tok, reads, writes)
        self.ops[eng].append((waits, fn, (ds[0], 16)))

    def barrier(self):
        for e in self.ENG:
            waits = []
            for f in self.ENG:
                if f != e and self.cnt[f] > 0:
                    self._need(e, ("E", f, self.cnt[f]), waits)
            if e not in ("pe", "sp") and self.cnt[e] > 0 and SAME_ENGINE_SYNC:
                self._need(e, ("E", e, self.cnt[e]), waits)
            for name, (h, tot) in self.dsem.items():
                if tot > 0:
                    self._need(e, ("D", name, tot), waits)
            if waits:
                self.ops[e].append((waits, None, None))
        self.lastw = {}
        self.readers = {}

    def emit(self, block):
        for eng, attr in [("pe", "tensor"), ("act", "scalar"), ("dve", "vector"),
                          ("pool", "gpsimd"), ("sp", "sync")]:
            ops = self.ops[eng]

            def body(e, ops=ops):
                for waits, fn, inc in ops:
                    for sem, v in waits:
                        e.wait_ge(sem, v)
                    if fn is not None:
                        ins = fn(e)
                        ins.then_inc(inc[0], inc[1])

            getattr(block, attr)(body)


class Arena:
    def __init__(self, ap, words):
        self.ap = ap
        self.words = words
        self.top = 0

    def alloc(self, shape, dtype):
        n = int(np.prod(shape))
        w = n if dtype == F32 else (n + 1) // 2
        w = (w + 15) // 16 * 16
        assert self.top + w <= self.words, ("arena overflow", self.top, w, self.words)
        v = self.ap[:, self.top:self.top + w]
        self.top += w
        if dtype != F32:
            v = v.bitcast(dtype)
        v = v[:, 0:n]
        if len(shape) == 2:
            v = v.rearrange("p (a b) -> p a b", b=shape[1])
        elif len(shape) == 3:
            v = v.rearrange("p (a b c) -> p a b c", b=shape[1], c=shape[2])
        return v


def build_program(mode, layers, dbg=None, stop=99):
    nc = bass.Bass("TRN2", target_bir_lowering=False)
    es = ExitStack()
    nl = len(layers)

    def din(name, shape, dt=F32):
        return nc.dram_tensor(name, list(shape), dt, kind="ExternalInput").ap()

    def dscr(name, shape, dt=F32, kind="Internal"):
        return nc.dram_tensor(name, list(shape), dt, kind=kind).ap()

    x_in = din("x_loc", [TOK, D])
    w_in = din("w_in", [nl, D, INW])
    hlb = din("hgrn_lower_bounds", [DEPTH, 1024])
    gnw = din("hgrn_gnorm_w", [nl, 128])
    dlam = din("diff_lambda", [nl, 4, 128])
    slw = din("diff_subln_w", [nl, 256])
    w_a = din("w_branch_a", [nl, 1024, D])
    w_b = din("w_branch_b", [nl, 1024, D])
    w_o = din("w_out", [nl, D, D])
    nrm = din("norm_w", [nl, 4, D])
    w_f1 = din("w_ffn_in", [nl, D, 2 * FFH])
    w_f2 = din("w_ffn_out", [nl, FFH, D])
    cos_in = din("rope_cos", [TOK, 16])
    sin_in = din("rope_sin", [TOK, 16])
    cmat_in = din("cmat", [128, 5, 128])
    ind_in = din("ind", [128, 2])
    relv_in = din("relv", [128, 64])
    qcrow_in = din("qcrow", [128, TOK])
    mcore_in = din("mcore", [128, 16])
    x_out = None
    if mode != "A":
        x_out = nc.dram_tensor("x_out", [TOK, D], F32, kind="ExternalOutput").ap()

    proj_tm = dscr("proj_tm", [TOK, 6144])
    proj_fm = dscr("proj_fm", [5120, TOK])
    a_scr = dscr("a_scr", [FFH, TOK], BF16)
    exk = "ExternalOutput" if mode == "A" else "Internal"
    axk = "ExternalInput" if mode == "B" else "Internal"
    EXB = []
    for par in range(2 if mode == "fused" else 1):
        sfx = "" if par == 0 else "_p1"
        EXB.append((
            dscr("kT_loc" + sfx, [1024, TOK], BF16, kind=exk),
            dscr("v_loc" + sfx, [TOK, 1024], BF16, kind=exk),
            dscr("S_loc" + sfx, [1024, 128], F32, kind=exk),
            dscr("D_loc" + sfx, [128, 8], F32, kind=exk),
            dscr("kT_all" + sfx, [NCORES * 1024, TOK], BF16, kind=axk),
            dscr("v_all" + sfx, [SEQ, 1024], BF16, kind=axk),
            dscr("S_all" + sfx, [NCORES * 1024, 128], F32, kind=axk),
            dscr("D_all" + sfx, [NCORES * 128, 8], F32, kind=axk)))

    carry = None
    if mode in ("A", "B"):
        ck = "ExternalOutput" if mode == "A" else "ExternalInput"
        carry = (dscr("c_qsT", [128, 8, TOK], BF16, kind=ck), dscr("c_ozT", [128, 8, TOK], F32, kind=ck),
                 dscr("c_qT", [128, 8, TOK], BF16, kind=ck))
    ARENA_WORDS = 52480
    arena_t = es.enter_context(nc.sbuf_tensor("arena", [128, ARENA_WORDS], F32))
    ps = [es.enter_context(nc.psum_tensor("ps%d" % i, [128, 512], F32)) for i in range(8)]
    R = Rec(nc, es)
    A = Arena(arena_t[:, :], ARENA_WORDS)
    PK = ["ps%d" % i for i in range(8)]

    def a1(n, dt=F32):
        return A.alloc([1, n], dt)[:, 0, :]

    def ACT(out, in_, func, reads, writes, **kw):
        R.op("act", lambda e: e.activation(out=out, in_=in_, func=func, **kw), reads, writes)

    def TT(eng, out, a, b, op, reads, writes):
        R.op(eng, lambda e: e.tensor_tensor(out, a, b, op), reads, writes)

    def TS(eng, out, a, s1, s2, op0, op1, reads, writes):
        R.op(eng, lambda e: e.tensor_scalar(out, a, s1, s2, op0, op1), reads, writes)

    def TS1(eng, out, a, s1, op, reads, writes):
        R.op(eng, lambda e: e.tensor_single_scalar(out, a, s1, op), reads, writes)

    def STT(eng, out, a, s, b, op0, op1, reads, writes):
        R.op(eng, lambda e: e.scalar_tensor_tensor(out=out, in0=a, scalar=s, in1=b, op0=op0, op1=op1),
             reads, writes)

    def CP(eng, out, in_, reads, writes):
        if eng == "act":
            R.op("act", lambda e: e.activation(out=out, in_=in_, func=AF.Copy), reads, writes)
        else:
            R.op(eng, lambda e: e.tensor_copy(out, in_), reads, writes)

    def MM(out, lhsT, rhs, start, stop, reads, writes):
        R.op("pe", lambda e: e.matmul(out, lhsT, rhs, start=start, stop=stop), reads, writes)

    def TR(out, in_, ident, reads, writes):
        R.op("pe", lambda e: e.transpose(out, in_, ident), reads, writes)

    def MEMSET(eng, out, val, writes):
        R.op(eng, lambda e: e.memset(out, val), [], writes)

    def RECIP(out, in_, reads, writes):
        R.op("dve", lambda e: e.reciprocal(out, in_), reads, writes)

    def LD(out, in_, reads, writes, sem, eng="sp", nc_ok=False):
        if nc_ok:
            R.dma(eng, lambda e: e.dma_start(out=out, in_=in_, allow_slow_non_contiguous=True), reads, writes, sem)
        else:
            R.dma(eng, lambda e: e.dma_start(out=out, in_=in_), reads, writes, sem)

    xT = A.alloc([KC, TOK], F32)
    cmat = A.alloc([5, 128], F32)
    identf, LBD, UBD, LFULL, ONESF = (cmat[:, i, :] for i in range(5))
    cbf = A.alloc([3, 128], BF16)
    identb, maskbd, onesb = (cbf[:, i, :] for i in range(3))
    ind = a1(2)
    relv = a1(64)
    mcore = a1(16)
    nwT = A.alloc([nl * 4, KC], F32)
    gnwT = a1(8)
    epsc = a1(2)
    cosT = A.alloc([NT, 16], F32)
    sinT = A.alloc([NT, 16], F32)
    qcb = a1(TOK, BF16)
    PERSIST = A.top

    LD(cmat, cmat_in, [], ["cmat"], "k_cmat")
    LD(ind, ind_in, [], ["ind"], "k_ind")
    LD(relv, relv_in, [], ["relv"], "k_relv")
    LD(mcore, mcore_in, [], ["mcore"], "k_mcore")
    for i4 in range(nl * 4):
        LD(nwT[:, i4, :], nrm[i4 // 4, i4 % 4].rearrange("(kc p) -> p kc", p=128), [], ["nwT"], "k_nwT", nc_ok=True)
    LD(gnwT[:, 0:nl], gnw.rearrange("l p -> p l"), [], ["gnwT"], "k_gnwT", nc_ok=True)
    LD(cosT, cos_in.rearrange("(i p) f -> p i f", p=128), [], ["cosT"], "k_cosT")
    LD(sinT, sin_in.rearrange("(i p) f -> p i f", p=128), [], ["sinT"], "k_sinT")
    CP("dve", identb, identf, ["cmat"], ["cbf"])
    CP("dve", maskbd, LBD, ["cmat"], ["cbf"])
    CP("dve", onesb, ONESF, ["cmat"], ["cbf"])
    MEMSET("pool", epsc[:, 0:1], NORM_EPS, ["epsc"])
    MEMSET("pool", epsc[:, 1:2], SUBLN_EPS, ["epsc"])

    def eps_ap(eps):
        return epsc[:, 0:1] if eps == NORM_EPS else epsc[:, 1:2]

    xin_t = [a1(D) for _ in range(2)]
    qcf = a1(TOK)
    LD(qcf, qcrow_in, [], ["qcf"], "k_qcf")
    CP("dve", qcb, qcf, ["qcf"], ["qcb"])
    for i in range(NT):
        b = xin_t[i % 2]
        bk = "xin%d" % (i % 2)
        LD(b, x_in[i * 128:(i + 1) * 128, :], [], [bk], bk)
        for g in range(4):
            pi = (i * 4 + g) % 4
            for j in range(4):
                kc = g * 4 + j
                TR(ps[pi][:, j * 128:(j + 1) * 128], b[:, kc * 128:(kc + 1) * 128], identf, [bk, "cmat"], [PK[pi]])
            CP("act" if g % 2 else "dve", xT[:, g * 4:(g + 1) * 4, i * 128:(i + 1) * 128],
               ps[pi][:, :].rearrange("p (a b) -> p a b", b=128), [PK[pi]], ["xT"])
    R.barrier()
    A.top = PERSIST

    def rms_rstd(src, nchunks, ndiv, eps, sqb, rstd, key, t0, tn):
        nh = tn // 512
        for c in range(nchunks):
            sb = sqb[c % 2]
            sk = key + "sq%d" % (c % 2)
            ap, rk = src(c)
            ACT(sb[:, 0:tn], ap, AF.Square, rk, [sk])
            for h in range(nh):
                MM(ps[6 + h][:, :], onesb, sb[:, h * 512:(h + 1) * 512], c == 0, c == nchunks - 1,
                   [sk, "cbf"], [PK[6 + h]])
        rstd_finish(nh, ndiv, eps, rstd, key)

    def rstd_finish(nh, ndiv, eps, rstd, key):
        for h in range(nh):
            ACT(rstd[:, h * 512:(h + 1) * 512], ps[6 + h][:, :], AF.Sqrt, [PK[6 + h], "epsc"], [key + "rstd"],
                bias=eps_ap(eps), scale=1.0 / ndiv)
        RECIP(rstd[:, 0:nh * 512], rstd[:, 0:nh * 512], [key + "rstd"], [key + "rstd"])

    def wload(dst, src, key):
        R.dma("pool", lambda e: e.dma_start(out=dst, in_=src), [], [key], key)

    def evac(idx, out, in_, reads, writes):
        CP("act" if idx % 2 else "dve", out, in_, reads, writes)

    for li, l in enumerate(layers):
        lam_init = 0.8 - 0.6 * math.exp(-0.3 * l)
        A.top = PERSIST
        kT_loc, v_loc, S_loc, D_loc, kT_all, v_all, S_all, D_all = EXB[li % len(EXB)]

        hT = A.alloc([KC, TOK], BF16)
        sqb = [a1(TOK, BF16) for _ in range(2)]
        rstd = a1(TOK)
        rms_rstd(lambda c: (xT[:, c, :], ["xT"]), KC, float(D), NORM_EPS, sqb, rstd, "n1", 0, TOK)
        for c in range(KC):
            STT("dve", hT[:, c, :], xT[:, c, :], nwT[:, li * 4 + 0, c:c + 1], rstd,
                ALU.mult, ALU.mult, ["xT", "nwT", "n1rstd"], ["hT"])
        wbuf = [A.alloc([KC, 512], BF16) for _ in range(2)]
        stg = [a1(512) for _ in range(4)]
        w_l = w_in[li].rearrange("(kc p) c -> p kc c", p=128)
        tm_groups = [0, 1, 2, 3, 4, 5, 8, 9, 10, 11, 12, 13]
        fm_groups = [6, 7, 14, 15, 16, 17, 18, 19, 20, 21]
        order = [("tm", g) for g in tm_groups] if mode != "B" else []
        if mode != "A":
            order += [("fm", g) for g in fm_groups]
        sidx = 0
        pidx = 0
        for gi, (kind, g) in enumerate(order):
            wb = wbuf[gi % 2]
            wk = "wbuf%d" % (gi % 2)
            wload(wb, w_l[:, :, g * 512:(g + 1) * 512], wk)
            if kind == "tm":
                tcol = tm_groups.index(g) * 512
                for i in range(NT):
                    pi = pidx % 4
                    pidx += 1
                    for kc in range(KC):
                        MM(ps[pi][:, :], hT[:, kc, i * 128:(i + 1) * 128], wb[:, kc, :], kc == 0, kc == KC - 1,
                           ["hT", wk], [PK[pi]])
                    sb = stg[sidx % 4]
                    sk = "stg%d" % (sidx % 4)
                    evac(sidx, sb, ps[pi][:, :], [PK[pi]], [sk])
                    LD(proj_tm[i * 128:(i + 1) * 128, tcol:tcol + 512], sb, [sk], [("ptm", tcol // 1024, i)], sk)
                    sidx += 1
            else:
                frow0 = fm_groups.index(g) * 512
                for j in range(4):
                    for h in range(2):
                        pi = pidx % 4
                        pidx += 1
                        for kc in range(KC):
                            MM(ps[pi][:, :], wb[:, kc, j * 128:(j + 1) * 128], hT[:, kc, h * 512:(h + 1) * 512],
                               kc == 0, kc == KC - 1, ["hT", wk], [PK[pi]])
                        sb = stg[sidx % 4]
                        sk = "stg%d" % (sidx % 4)
                        evac(sidx, sb, ps[pi][:, :], [PK[pi]], [sk])
                        r0 = frow0 + j * 128
                        LD(proj_fm[r0:r0 + 128, h * 512:(h + 1) * 512], sb, [sk], [("pfm", r0 // 128)], sk)
                        sidx += 1
        R.barrier()
        A.top = PERSIST

        qsT = A.alloc([8, TOK], BF16)
        RAw = A.top
        ozT = A.alloc([8, TOK], F32)
        qT_off = A.top
        qT = A.alloc([8, TOK], BF16)
        S2TOP = A.top
        if mode == "B":
            LD(qsT, carry[0], [], ["qsT"], "k_cq")
            LD(ozT, carry[1], [], ["ozT"], "k_co")
            LD(qT, carry[2], [], ["qT"], "k_cT")
            R.barrier()
        else:
            qhT = A.alloc([8, 128], BF16)
            ktT = A.alloc([8, 128], BF16)
            Vt = a1(1024, BF16)
            Kh = a1(1024, BF16)
            glast = A.alloc([8, 16], F32)
            dec = A.alloc([8, 2], F32)
            dtot = a1(8)
            lb = a1(1024)
            oml = a1(1024)
            lbmark = A.top
            lbraw = A.alloc([4, 1024], F32)
            LD(lbraw, hlb.partition_broadcast(128), [], ["lbraw"], "k_lbraw")
            ACT(lbraw, lbraw, AF.Exp, ["lbraw"], ["lbraw"])
            TT("dve", lb, lbraw[:, 0, :], lbraw[:, 1, :], ALU.add, ["lbraw"], ["lb"])
            TT("dve", lb, lb, lbraw[:, 2, :], ALU.add, ["lbraw", "lb"], ["lb"])
            TT("dve", lb, lb, lbraw[:, 3, :], ALU.add, ["lbraw", "lb"], ["lb"])
            RECIP(oml, lb, ["lb"], ["oml"])
            if l == 0:
                MEMSET("dve", lb, 0.0, ["lb"])
            else:
                CP("dve", lb, lbraw[:, 1, :], ["lbraw", "oml"], ["lb"])
                for j in range(2, l + 1):
                    TT("dve", lb, lb, lbraw[:, j, :], ALU.add, ["lbraw", "lb"], ["lb"])
            TT("dve", lb, lb, oml, ALU.mult, ["lb", "oml"], ["lb"])
            TS("dve", oml, lb, -1.0, 1.0, ALU.mult, ALU.add, ["lb"], ["oml"])
            R.barrier()
            A.top = lbmark
            Tacc = a1(1024)
            tq = a1(1024)
            tf = a1(1024)
            ti = a1(1024)
            qf = a1(1024)
            lf = a1(1024)
            kk = a1(1024)
            eb = [a1(1024) for _ in range(2)]
            tb = [a1(1024, BF16) for _ in range(3)]
            Am8 = A.alloc([8, 128], BF16)
            Sb2 = A.alloc([8, 128], BF16)
            Sst = A.alloc([8, 128], F32)
            Sbf = A.alloc([8, 128], BF16)
            MEMSET("pool", Tacc, 0.0, ["Tacc"])
            MEMSET("pool", Sst, 0.0, [("Sst", h) for h in range(8)])

            for i in range(NT):
                rows = slice(i * 128, (i + 1) * 128)
                tsl = slice(i * 128, (i + 1) * 128)
                LD(tq, proj_tm[rows, 0:1024], [("ptm", 0, i)], ["tq"], "tq")
                LD(tf, proj_tm[rows, 1024:2048], [("ptm", 1, i)], ["tf"], "tf")
                LD(ti, proj_tm[rows, 2048:3072], [("ptm", 2, i)], ["ti"], "ti")
                ACT(qf, tq, AF.Silu, ["tq"], ["qf"])
                ACT(tf, tf, AF.Sigmoid, ["tf"], ["tf"])
                CP("pool", Vt, ti, ["ti"], ["Vt"])
                TT("dve", tf, tf, oml, ALU.mult, ["tf", "oml"], ["tf"])
                TT("dve", tf, tf, lb, ALU.add, ["tf", "lb"], ["tf"])
                ACT(lf, tf, AF.Ln, ["tf"], ["lf"])
                TS("dve", kk, tf, -1.0, 1.0, ALU.mult, ALU.add, ["tf"], ["kk"])
                for h in range(8):
                    MM(ps[4][:, h * 2:h * 2 + 2], lf[:, h * 128:(h + 1) * 128], ind, True, True, ["lf", "ind"], [PK[4]])
                CP("dve", glast[:, :, 2 * i:2 * i + 2], ps[4][:, 0:16].rearrange("p (h c) -> p h c", c=2),
                   [PK[4]], ["glast"])
                ACT(dec, glast[:, :, 2 * i:2 * i + 2], AF.Exp, ["glast"], ["dec"])
                for h2 in range(2):
                    cs = slice(h2 * 512, (h2 + 1) * 512)
                    MM(ps[0][:, :], LBD, lf[:, cs], True, True, ["lf", "cmat"], [PK[0]])
                    MM(ps[1][:, :], UBD, lf[:, cs], True, True, ["lf", "cmat"], [PK[1]])
                    MM(ps[2][:, :], LFULL, lf[:, cs], True, True, ["lf", "cmat"], [PK[2]])
                    MM(ps[3][:, :], ONESF, lf[:, cs], True, True, ["lf", "cmat"], [PK[3]])
                    ACT(eb[0][:, cs], ps[0][:, :], AF.Exp, [PK[0]], ["eb0"])
                    TT("dve", tb[0][:, cs], qf[:, cs], eb[0][:, cs], ALU.mult, ["qf", "eb0"], ["tb0"])
                    ACT(eb[1][:, cs], ps[0][:, :], AF.Exp, [PK[0]], ["eb1"], scale=-1.0)
                    TT("pool", tb[1][:, cs], kk[:, cs], eb[1][:, cs], ALU.mult, ["kk", "eb1"], ["tb1"])
                    ACT(eb[0][:, cs], ps[1][:, :], AF.Exp, [PK[1]], ["eb0"])
                    TT("dve", Kh[:, cs], kk[:, cs], eb[0][:, cs], ALU.mult, ["kk", "eb0"], ["Kh"])
                    TT("dve", eb[1][:, cs], ps[2][:, :], Tacc[:, cs], ALU.add, [PK[2], "Tacc"], ["eb1"])
                    ACT(eb[1][:, cs], eb[1][:, cs], AF.Exp, ["eb1"], ["eb1"])
                    TT("pool", tb[2][:, cs], qf[:, cs], eb[1][:, cs], ALU.mult, ["qf", "eb1"], ["tb2"])
                    TT("dve", Tacc[:, cs], Tacc[:, cs], ps[3][:, :], ALU.add, [PK[3], "Tacc"], ["Tacc"])
                tcount = 0
                for which in range(3):
                    for g in range(2):
                        pi = 5 + (tcount % 2)
                        tcount += 1
                        pbb = ps[pi][:, :].bitcast(BF16)
                        for j in range(4):
                            h = g * 4 + j
                            TR(pbb[:, j * 128:(j + 1) * 128], tb[which][:, h * 128:(h + 1) * 128], identb,
                               ["tb%d" % which, "cbf"], [PK[pi]])
                        src = pbb[:, 0:512].rearrange("p (a b) -> p a b", b=128)
                        if which == 0:
                            evac(g, qhT[:, g * 4:(g + 1) * 4, :], src, [PK[pi]], ["qhT"])
                        elif which == 1:
                            evac(g, ktT[:, g * 4:(g + 1) * 4, :], src, [PK[pi]], ["ktT"])
                        else:
                            evac(g, qsT[:, g * 4:(g + 1) * 4, tsl], src, [PK[pi]], ["qsT"])
                for h in range(8):
                    MM(ps[h // 4][:, (h % 4) * 128:(h % 4 + 1) * 128], ktT[:, h, :], qhT[:, h, :], True, True,
                       ["qhT", "ktT"], [PK[h // 4]])
                for h in range(8):
                    TT("dve", Am8[:, h, :], ps[h // 4][:, (h % 4) * 128:(h % 4 + 1) * 128], maskbd, ALU.mult,
                       [PK[h // 4], "cbf"], [("Am", h)])
                for h in range(8):
                    hs = slice(h * 128, (h + 1) * 128)
                    MM(ps[2 + h // 4][:, (h % 4) * 128:(h % 4 + 1) * 128], Kh[0:64, hs], Vt[0:64, hs], True, True,
                       ["Kh", "Vt"], [PK[2 + h // 4]])
                for h in range(8):
                    STT("dve", Sst[:, h, :], Sst[:, h, :], dec[:, h, 0:1], ps[2 + h // 4][:, (h % 4) * 128:(h % 4 + 1) * 128],
                        ALU.mult, ALU.add, [PK[2 + h // 4], "dec", ("Sst", h)], [("Sst", h)])
                    CP("pool", Sb2[:, h, :], Sst[:, h, :], [("Sst", h)], [("Sb2", h)])
                for h in range(8):
                    hs = slice(h * 128, (h + 1) * 128)
                    pb = 5 + h // 4
                    c0 = (h % 4) * 128
                    MM(ps[pb][:, c0:c0 + 128], Vt[:, hs], Am8[:, h, :], True, False, ["Vt", ("Am", h)], [PK[pb]])
                    if i > 0:
                        MM(ps[pb][:, c0:c0 + 64], Sbf[:, h, :], qhT[:, h, 0:64], False, False, [("Sbf", h), "qhT"], [PK[pb]])
                    MM(ps[pb][:, c0 + 64:c0 + 128], Sb2[:, h, :], qhT[:, h, 64:128], False, True, [("Sb2", h), "qhT"], [PK[pb]])
                for g in range(2):
                    CP("act", ozT[:, g * 4:(g + 1) * 4, tsl], ps[5 + g][:, :].rearrange("p (a b) -> p a b", b=128),
                       [PK[5 + g]], ["ozT"])
                for h in range(8):
                    hs = slice(h * 128, (h + 1) * 128)
                    MM(ps[2 + h // 4][:, (h % 4) * 128:(h % 4 + 1) * 128], Kh[64:128, hs], Vt[64:128, hs], True, True,
                       ["Kh", "Vt"], [PK[2 + h // 4]])
                for h in range(8):
                    STT("dve", Sst[:, h, :], Sst[:, h, :], dec[:, h, 1:2], ps[2 + h // 4][:, (h % 4) * 128:(h % 4 + 1) * 128],
                        ALU.mult, ALU.add, [PK[2 + h // 4], "dec", ("Sst", h)], [("Sst", h)])
                    CP("pool", Sbf[:, h, :], Sst[:, h, :], [("Sst", h)], [("Sbf", h)])
            LD(S_loc.rearrange("(h k) v -> k h v", k=128), Sst, [("Sst", h) for h in range(8)], ["S_loc"], "sloc")
            R.op("dve", lambda e: e.tensor_reduce(out=dtot, in_=glast, axis=AX.X, op=ALU.add), ["glast"], ["dtot"])
            ACT(dtot, dtot, AF.Exp, ["dtot"], ["dtot"])
            LD(D_loc, dtot, ["dtot"], ["D_loc"], "sloc")
            R.barrier()
            A.top = S2TOP

            kTs = A.alloc([8, TOK], BF16)
            tqs = [a1(1024) for _ in range(2)]
            tfs = [a1(1024) for _ in range(2)]
            tis = [a1(1024) for _ in range(2)]
            qb = a1(1024, BF16)
            kb = a1(1024, BF16)
            vb = [a1(1024, BF16) for _ in range(2)]
            rt = [A.alloc([8, 16], F32) for _ in range(4)]
            for i in range(NT):
                rows = slice(i * 128, (i + 1) * 128)
                tsl = slice(i * 128, (i + 1) * 128)
                tq, tf, ti = tqs[i % 2], tfs[i % 2], tis[i % 2]
                tqk, tfk, tik = "tq%d" % (i % 2), "tf%d" % (i % 2), "ti%d" % (i % 2)
                LD(tq, proj_tm[rows, 3072:4096], [("ptm", 3, i)], [tqk], tqk)
                LD(tf, proj_tm[rows, 4096:5120], [("ptm", 4, i)], [tfk], tfk)
                LD(ti, proj_tm[rows, 5120:6144], [("ptm", 5, i)], [tik], tik)
                TS1("pool", tq, tq, float(128.0 ** -0.5), ALU.mult, [tqk], [tqk])
                cb = cosT[:, i, :].unsqueeze(1).broadcast_to([128, 8, 16])
                sn = sinT[:, i, :].unsqueeze(1).broadcast_to([128, 8, 16])
                for (src, dst, sk_, dk_, e1, e2) in ((tq, qb, tqk, "qb", "dve", "pool"), (tf, kb, tfk, "kb", "pool", "dve")):
                    v3 = src.rearrange("p (j d) -> p j d", d=128)
                    o3 = dst.rearrange("p (j d) -> p j d", d=128)
                    t1 = v3[:, :, 0:16]
                    t2 = v3[:, :, 16:32]
                    TT(e1, rt[0], t1, cb, ALU.mult, [sk_, "cosT"], ["rt0"])
                    TT(e2, rt[1], t2, sn, ALU.mult, [sk_, "sinT"], ["rt1"])
                    TT(e1, o3[:, :, 0:16], rt[0], rt[1], ALU.subtract, ["rt0", "rt1"], [dk_])
                    TT(e1, rt[2], t2, cb, ALU.mult, [sk_, "cosT"], ["rt2"])
                    TT(e2, rt[3], t1, sn, ALU.mult, [sk_, "sinT"], ["rt3"])
                    TT(e2, o3[:, :, 16:32], rt[2], rt[3], ALU.add, ["rt2", "rt3"], [dk_])
                    CP("act", o3[:, :, 32:128], v3[:, :, 32:128], [sk_], [dk_])
                v_ = vb[i % 2]
                vk = "vb%d" % (i % 2)
                CP("act", v_, ti, [tik], [vk])
                LD(v_loc[rows, :], v_, [vk], ["v_loc"], vk)
                tcount = 0
                for (srcb, sk_, dstT, dk_) in ((qb, "qb", qT, "qT"), (kb, "kb", kTs, "kTs")):
                    for g in range(2):
                        pi = 4 + (tcount % 4)
                        tcount += 1
                        pbb = ps[pi][:, :].bitcast(BF16)
                        for j in range(4):
                            h = g * 4 + j
                            TR(pbb[:, j * 128:(j + 1) * 128], srcb[:, h * 128:(h + 1) * 128], identb, [sk_, "cbf"], [PK[pi]])
                        evac(g, dstT[:, g * 4:(g + 1) * 4, tsl], pbb[:, 0:512].rearrange("p (a b) -> p a b", b=128),
                             [PK[pi]], [dk_])
            LD(kT_loc.rearrange("(j d) t -> d j t", d=128), kTs, ["kTs"], ["kT_loc"], "kTs")
            R.barrier()
            A.top = S2TOP
        if mode == "A":
            LD(carry[0], qsT, [], ["c0_"], "k_cq")
            LD(carry[1], ozT, [], ["c1_"], "k_co")
            LD(carry[2], qT, [], ["c2_"], "k_cT")
            R.barrier()
        if mode == "A" or stop <= 3:
            break

        if mode == "fused":
            rg = [list(range(NCORES))]
            for (src, dst, nm) in ((kT_loc, kT_all, "kT"), (v_loc, v_all, "v"), (S_loc, S_all, "S"), (D_loc, D_all, "D")):
                R.dma("pool", lambda e, src=src, dst=dst: e.collective_compute(
                    "AllGather", ALU.bypass, replica_groups=rg, ins=[src[:, :]], outs=[dst[:, :]]),
                    [], [nm + "_all"], "cc")
            R.barrier()

        oaT = qsT
        Sr = [A.alloc([8, 128], F32) for _ in range(2)]
        Dall = A.alloc([8, 8], F32)
        Dp = a1(8)
        acc = A.alloc([8, 128], F32)
        sinb = A.alloc([8, 128], BF16)
        sq4s = [a1(TOK, BF16) for _ in range(2)]
        rstd4s = [a1(TOK) for _ in range(2)]
        gt = [a1(TOK) for _ in range(2)]
        tmp4s = [a1(TOK) for _ in range(2)]
        LD(Dall, D_all.rearrange("(r k) h -> k r h", k=128), [], ["Dall"], "k_Dall")
        MEMSET("pool", acc, 0.0, ["acc"])
        for r in range(NCORES - 1):
            sr = Sr[r % 2]
            srk = "Sr%d" % (r % 2)
            LD(sr, S_all[r * 1024:(r + 1) * 1024, :].rearrange("(h k) v -> k h v", k=128), [], [srk], srk)
            TS("dve", Dp, Dall[:, r, :], mcore[:, r:r + 1], mcore[:, 8 + r:9 + r], ALU.mult, ALU.add,
               ["Dall", "mcore"], ["Dp"])
            TS1("pool", sr, sr, mcore[:, r:r + 1], ALU.mult, [srk, "mcore"], [srk])
            for h in range(8):
                STT("dve", acc[:, h, :], acc[:, h, :], Dp[:, h:h + 1], sr[:, h, :], ALU.mult, ALU.add,
                    ["acc", "Dp", srk], ["acc"])
        CP("dve", sinb, acc, ["acc"], ["sinb"])
        for h in range(8):
            g_ = gt[h % 2]
            gk = "gt%d" % (h % 2)
            LD(g_, proj_fm[h * 128:(h + 1) * 128, :], [("pfm", h)], [gk], gk)
            ACT(g_, g_, AF.Silu, [gk], [gk])
            for hf in range(2):
                cs = slice(hf * 512, (hf + 1) * 512)
                MM(ps[hf][:, :], sinb[:, h, :], qsT[:, h, cs], True, True, ["sinb", ("qsT", h)], [PK[hf]])
                TT("dve", ozT[:, h, cs], ozT[:, h, cs], ps[hf][:, :], ALU.add, [PK[hf], ("ozT", h)], [("ozT", h)])
            sq4, rstd4, tmp4 = sq4s[h % 2], rstd4s[h % 2], tmp4s[h % 2]
            p4 = "%d" % (h % 2)
            ACT(sq4, ozT[:, h, :], AF.Square, [("ozT", h)], ["sq4" + p4])
            for hf in range(2):
                MM(ps[6 + hf][:, :], onesb, sq4[:, hf * 512:(hf + 1) * 512], True, True, ["sq4" + p4, "cbf"], [PK[6 + hf]])
            rstd_finish(2, 128.0, NORM_EPS, rstd4, "g4" + p4)
            STT("dve", tmp4, ozT[:, h, :], gnwT[:, li:li + 1], rstd4, ALU.mult, ALU.mult,
                [("ozT", h), "gnwT", "g4" + p4 + "rstd"], ["tmp4" + p4])
            TT("pool", oaT[:, h, :], tmp4, g_, ALU.mult, ["tmp4" + p4, gk, ("qsT", h)], [("qsT", h)])
        R.barrier()
        A.top = S2TOP

        if stop <= 4:
            break
        obT = arena_t[:, RAw:RAw + 4096].bitcast(BF16).rearrange("p (a b) -> p a b", b=TOK)
        NSL = 3
        kres = A.alloc([2, SEQ], BF16)
        vbuf = [A.alloc([8, 257], BF16) for _ in range(NSL)]
        SBK = [0, 1, 6, 7]
        LA = 3
        pT = [a1(512, BF16) for _ in range(4)]
        pM = [a1(512, BF16) for _ in range(4)]
        lp = A.alloc([4, 128], F32)
        lpp = a1(128)
        lamc = a1(4)
        slwb = a1(256)
        rr = a1(4)
        of = a1(256)
        tmpo = a1(256)
        onb = a1(256, BF16)
        junk = a1(256)
        for s in range(NSL):
            MEMSET("pool", vbuf[s][:, :, 256:257], 1.0, ["vbuf%d" % s])
        LD(lp, dlam[li].partition_broadcast(128), [], ["lp"], "k_lp")
        TT("dve", lpp, lp[:, 0, :], lp[:, 1, :], ALU.mult, ["lp"], ["lpp"])
        R.op("dve", lambda e: e.tensor_reduce(out=lamc[:, 0:1], in_=lpp, axis=AX.X, op=ALU.add), ["lpp"], ["lamc"])
        TT("dve", lpp, lp[:, 2, :], lp[:, 3, :], ALU.mult, ["lp", "lamc"], ["lpp"])
        R.op("dve", lambda e: e.tensor_reduce(out=lamc[:, 1:2], in_=lpp, axis=AX.X, op=ALU.add), ["lpp"], ["lamc"])
        ACT(lamc[:, 0:2], lamc[:, 0:2], AF.Exp, ["lamc"], ["lamc"])
        TT("dve", lamc[:, 2:3], lamc[:, 0:1], lamc[:, 1:2], ALU.subtract, ["lamc"], ["lamc"])
        TS1("dve", lamc[:, 2:3], lamc[:, 2:3], float(lam_init), ALU.add, ["lamc"], ["lamc"])
        LD(slwb, slw[li].partition_broadcast(128),
           [], ["slwb"], "k_slwb")
        TS1("dve", slwb, slwb, float(1.0 - lam_init), ALU.mult, ["slwb"], ["slwb"])
        step = 0
        for h in range(4):
            for r in range(NCORES):
                LD(kres[:, :, r * 1024:(r + 1) * 1024],
                   kT_all[(r * 8 + 2 * h) * 128:(r * 8 + 2 * h + 2) * 128, :].rearrange("(c d) t -> d c t", d=128),
                   ["kT_all"], ["kres%d" % r], "kres%d" % r)
            for qg in range(4):
                qs_ = slice(qg * 256, (qg + 1) * 256)
                qc2 = qcb[:, qs_].unsqueeze(1).broadcast_to([128, 2, 256])
                steps = [(r, kt) for r in range(NCORES) for kt in range(8)]
                slot_of = {}

                def seg_load(r):
                    nonlocal step
                    s_ = step % NSL
                    step += 1
                    slot_of[r] = s_
                    vbk = "vbuf%d" % s_
                    LD(vbuf[s_][:, :, 0:256], v_all[r * 1024:(r + 1) * 1024, h * 256:(h + 1) * 256].rearrange(
                        "(kt p) e -> p kt e", p=128), ["v_all"], [vbk], vbk)

                def scores(r, kt):
                    s_ = slot_of[r]
                    ktg = r * 8 + kt
                    sb_ = ktg % 4
                    for c in range(2):
                        MM(ps[SBK[sb_]][:, c * 256:(c + 1) * 256], kres[:, c, ktg * 128:(ktg + 1) * 128],
                           qT[:, 2 * h + c, qs_], True, True, ["kres%d" % r, "qT"], [PK[SBK[sb_]]])

                def softmax_part(r, kt):
                    ktg = r * 8 + kt
                    sb_ = ktg % 4
                    ACT(pT[sb_], ps[SBK[sb_]][:, :], AF.Exp, [PK[SBK[sb_]]], ["pT%d" % sb_])
                    STT("dve", pM[sb_].rearrange("p (c q) -> p c q", c=2), qc2, relv[:, ktg:ktg + 1],
                        pT[sb_].rearrange("p (c q) -> p c q", c=2), ALU.is_ge, ALU.mult,
                        ["pT%d" % sb_, "qcb", "relv"], ["pM%d" % sb_])

                def pv(r, kt):
                    s_ = slot_of[r]
                    ktg = r * 8 + kt
                    sb_ = ktg % 4
                    for c in range(2):
                        for qt in range(2):
                            ai = 2 + c * 2 + qt
                            MM(ps[ai][:, 0:257], pM[sb_][:, c * 256 + qt * 128:c * 256 + (qt + 1) * 128],
                               vbuf[s_][:, kt, :], ktg == 0, ktg == 63, ["pM%d" % sb_, "vbuf%d" % s_], [PK[ai]])

                seg_load(0)
                seg_load(1)
                for k0 in range(LA):
                    scores(*steps[k0])
                for k, (r, kt) in enumerate(steps):
                    softmax_part(r, kt)
                    if k + LA < len(steps):
                        r2, kt2 = steps[k + LA]
                        if kt2 == 0 and r2 + 1 < NCORES:
                            seg_load(r2 + 1)
                        scores(r2, kt2)
                    pv(r, kt)
                for qt in range(2):
                    a0, a1_ = 2 + qt, 4 + qt
                    tcol = slice(qg * 256 + qt * 128, qg * 256 + (qt + 1) * 128)
                    RECIP(rr[:, 0:1], ps[a0][:, 256:257], [PK[a0]], ["rr"])
                    RECIP(rr[:, 1:2], ps[a1_][:, 256:257], [PK[a1_], "rr"], ["rr"])
                    TT("dve", rr[:, 1:2], rr[:, 1:2], lamc[:, 2:3], ALU.mult, ["rr", "lamc"], ["rr"])
                    TS1("dve", tmpo, ps[a1_][:, 0:256], rr[:, 1:2], ALU.mult, [PK[a1_], "rr"], ["tmpo"])
                    STT("dve", of, ps[a0][:, 0:256], rr[:, 0:1], tmpo, ALU.mult, ALU.subtract,
                        [PK[a0], "rr", "tmpo"], ["of"])
                    TT("dve", junk, of, of, ALU.mult, ["of"], ["junk"])
                    R.op("dve", lambda e: e.tensor_reduce(out=rr[:, 2:3], in_=junk, axis=AX.X, op=ALU.add), ["junk"], ["ss"])
                    ACT(rr[:, 3:4], rr[:, 2:3], AF.Sqrt, ["ss", "epsc"], ["rs"], bias=eps_ap(SUBLN_EPS), scale=1.0 / 256.0)
                    RECIP(rr[:, 3:4], rr[:, 3:4], ["rs"], ["rs"])
                    STT("dve", onb, of, rr[:, 3:4], slwb, ALU.mult, ALU.mult, ["of", "rs", "slwb"], ["onb"])
                    pbb = ps[6 + qt][:, :].bitcast(BF16)
                    for e2 in range(2):
                        TR(pbb[:, e2 * 128:(e2 + 1) * 128], onb[:, e2 * 128:(e2 + 1) * 128], identb, ["onb", "cbf"], [PK[6 + qt]])
                    evac(qt, obT[:, 2 * h:2 * h + 2, tcol], pbb[:, 0:256].rearrange("p (a b) -> p a b", b=128),
                         [PK[6 + qt]], ["obT"])
        R.barrier()

        if stop <= 5:
            break
        mT = arena_t[:, RAw + 4096:RAw + 4096 + 8192].bitcast(BF16).rearrange("p (a b) -> p a b", b=TOK)
        A.top = S2TOP
        wab = [A.alloc([2, 8, 512], BF16) for _ in range(2)]
        gat = [a1(TOK) for _ in range(2)]
        gbt = [a1(TOK) for _ in range(2)]
        m1 = a1(TOK)
        m2 = a1(TOK)
        wa_l = w_a[li].rearrange("(kc p) c -> p kc c", p=128)
        wb_l = w_b[li].rearrange("(kc p) c -> p kc c", p=128)
        for j in range(KC):
            g = j // 4
            s = g % 2
            wk = "wab%d" % s
            if j % 4 == 0:
                wload(wab[s][:, 0, :, :], wa_l[:, :, g * 512:(g + 1) * 512], wk)
                wload(wab[s][:, 1, :, :], wb_l[:, :, g * 512:(g + 1) * 512], wk)
            jc = slice((j % 4) * 128, (j % 4 + 1) * 128)
            pb0 = (j % 2) * 4
            ga_, gb_ = gat[j % 2], gbt[j % 2]
            gak, gbk = "gat%d" % (j % 2), "gbt%d" % (j % 2)
            LD(ga_, proj_fm[1024 + j * 128:1024 + (j + 1) * 128, :], [("pfm", 8 + j)], [gak], gak)
            LD(gb_, proj_fm[3072 + j * 128:3072 + (j + 1) * 128, :], [("pfm", 24 + j)], [gbk], gbk)
            ACT(ga_, ga_, AF.Sigmoid, [gak], [gak])
            ACT(gb_, gb_, AF.Sigmoid, [gbk], [gbk])
            for br, srcT, sk_ in ((0, oaT, "qsT"), (1, obT, "obT")):
                for hf in range(2):
                    pi = pb0 + br * 2 + hf
                    for kc in range(8):
                        MM(ps[pi][:, :], wab[s][:, br, kc, jc], srcT[:, kc, hf * 512:(hf + 1) * 512], kc == 0, kc == 7,
                           [wk, sk_], [PK[pi]])
            for hf in range(2):
                cs = slice(hf * 512, (hf + 1) * 512)
                TT("dve", m1[:, cs], ga_[:, cs], ps[pb0 + hf][:, :], ALU.mult, [gak, PK[pb0 + hf]], ["m1"])
                TT("dve", m2[:, cs], gb_[:, cs], ps[pb0 + 2 + hf][:, :], ALU.mult, [gbk, PK[pb0 + 2 + hf]], ["m2"])
            TT("pool", mT[:, j, :], m1, m2, ALU.add, ["m1", "m2"], ["mT"])
        R.barrier()

        if stop <= 6:
            break
        A.top = PERSIST
        sq6 = [a1(512, BF16) for _ in range(2)]
        rstd6 = a1(512)
        tmp6s = [a1(512) for _ in range(2)]
        assert A.top <= RAw + 4096
        A.top = qT_off + 4096
        wo = [A.alloc([KC, 512], BF16) for _ in range(2)]
        zT = A.alloc([KC, 512], F32)
        wo_l = w_o[li].rearrange("(kc p) c -> p kc c", p=128)
        widx = 0
        for th in range(2):
            tcs = slice(th * 512, (th + 1) * 512)
            for jo in range(KC):
                g = jo // 4
                if jo % 4 == 0:
                    s = widx % 2
                    widx += 1
                    wk = "wo%d" % s
                    wload(wo[s], wo_l[:, :, g * 512:(g + 1) * 512], wk)
                pi = jo % 4
                for kc in range(KC):
                    MM(ps[pi][:, :], wo[s][:, kc, (jo % 4) * 128:(jo % 4 + 1) * 128], mT[:, kc, tcs], kc == 0, kc == KC - 1,
                       [wk, "mT"], [PK[pi]])
                CP("dve", zT[:, jo, :], ps[pi][:, :], [PK[pi]], ["zT"])
                sb = sq6[jo % 2]
                sk = "sq6%d" % (jo % 2)
                ACT(sb, ps[pi][:, :], AF.Square, [PK[pi]], [sk])
                MM(ps[6][:, :], onesb, sb, jo == 0, jo == KC - 1, [sk, "cbf"], [PK[6]])
            rstd_finish(1, float(D), NORM_EPS, rstd6, "n6")
            for c in range(KC):
                tmp6, t6k = tmp6s[c % 2], "tmp6%d" % (c % 2)
                STT("dve", tmp6, zT[:, c, :], nwT[:, li * 4 + 1, c:c + 1], rstd6, ALU.mult, ALU.mult,
                    ["zT", "nwT", "n6rstd"], [t6k])
                TT("pool", xT[:, c, tcs], xT[:, c, tcs], tmp6, ALU.add, [t6k, ("xT", c)], [("xT", c)])
        R.barrier()
        A.top = PERSIST

        if stop <= 7:
            break
        sqb = [a1(TOK, BF16) for _ in range(2)]
        rstd = a1(TOK)
        ovl = A.top
        h2T = A.alloc([KC, TOK], BF16)
        rms_rstd(lambda c: (xT[:, c, :], ["xT"]), KC, float(D), NORM_EPS, sqb, rstd, "n7", 0, TOK)
        for c in range(KC):
            STT("dve", h2T[:, c, :], xT[:, c, :], nwT[:, li * 4 + 2, c:c + 1], rstd,
                ALU.mult, ALU.mult, ["xT", "nwT", "n7rstd"], ["h2T"])
        wgu = [A.alloc([2, KC, 256], BF16) for _ in range(2)]
        sgt = [a1(TOK) for _ in range(2)]
        ast = [a1(TOK, BF16) for _ in range(2)]
        wf1_l = w_f1[li].rearrange("(kc p) c -> p kc c", p=128)
        wf2_l = w_f2[li].rearrange("(kc p) c -> p kc c", p=128)
        widx = 0
        for j in range(FC):
            g = j // 2
            if j % 2 == 0:
                s = widx % 2
                widx += 1
                wk = "wgu%d" % s
                wload(wgu[s][:, 0, :, :], wf1_l[:, :, g * 256:(g + 1) * 256], wk)
                wload(wgu[s][:, 1, :, :], wf1_l[:, :, FFH + g * 256:FFH + (g + 1) * 256], wk)
            jc = slice((j % 2) * 128, (j % 2 + 1) * 128)
            pb0 = (j % 2) * 4
            for hf in range(2):
                tcs = slice(hf * 512, (hf + 1) * 512)
                for kc in range(KC):
                    MM(ps[pb0 + hf][:, :], wgu[s][:, 0, kc, jc], h2T[:, kc, tcs], kc == 0, kc == KC - 1,
                       [wk, "h2T"], [PK[pb0 + hf]])
            for hf in range(2):
                tcs = slice(hf * 512, (hf + 1) * 512)
                for kc in range(KC):
                    MM(ps[pb0 + 2 + hf][:, :], wgu[s][:, 1, kc, jc], h2T[:, kc, tcs], kc == 0, kc == KC - 1,
                       [wk, "h2T"], [PK[pb0 + 2 + hf]])
            sg_ = sgt[j % 2]
            sgk = "sgt%d" % (j % 2)
            as_ = ast[j % 2]
            ask = "ast%d" % (j % 2)
            for hf in range(2):
                tcs = slice(hf * 512, (hf + 1) * 512)
                ACT(sg_[:, tcs], ps[pb0 + hf][:, :], AF.Silu, [PK[pb0 + hf]], [sgk])
                TT("dve", as_[:, tcs], sg_[:, tcs], ps[pb0 + 2 + hf][:, :], ALU.mult, [sgk, PK[pb0 + 2 + hf]], [ask])
            LD(a_scr[j * 128:(j + 1) * 128, :], as_, [ask], ["a_scr"], ask)
        R.barrier()
        A.top = ovl
        aT = A.alloc([FC, 512], BF16)
        w2 = [A.alloc([FC, 128], BF16) for _ in range(2)]
        y2 = A.alloc([KC, 512], F32)
        rstd8 = a1(512)
        tmp8s = [a1(512) for _ in range(2)]
        w2idx = 0
        for th in range(2):
            tcs = slice(th * 512, (th + 1) * 512)
            LD(aT, a_scr.rearrange("(j p) t -> p j t", p=128)[:, :, tcs], ["a_scr"], ["aT"], "aT")
            for jo in range(KC):
                s2 = w2idx % 2
                w2idx += 1
                wk2 = "w2%d" % s2
                wload(w2[s2], wf2_l[:, :, jo * 128:(jo + 1) * 128], wk2)
                pi = 4 + (jo % 2)
                for kc in range(FC):
                    MM(ps[pi][:, :], w2[s2][:, kc, :], aT[:, kc, :], kc == 0, kc == FC - 1, [wk2, "aT"], [PK[pi]])
                CP("dve", y2[:, jo, :], ps[pi][:, :], [PK[pi]], ["y2"])
                sb = sqb[jo % 2]
                sk = "sq7%d" % (jo % 2)
                ACT(sb[:, 0:512], ps[pi][:, :], AF.Square, [PK[pi]], [sk])
                MM(ps[6][:, :], onesb, sb[:, 0:512], jo == 0, jo == KC - 1, [sk, "cbf"], [PK[6]])
            rstd_finish(1, float(D), NORM_EPS, rstd8, "n8")
            for c in range(KC):
                tmp8, t8k = tmp8s[c % 2], "tmp8%d" % (c % 2)
                STT("dve", tmp8, y2[:, c, :], nwT[:, li * 4 + 3, c:c + 1], rstd8, ALU.mult, ALU.mult,
                    ["y2", "nwT", "n8rstd"], [t8k])
                TT("pool", xT[:, c, tcs], xT[:, c, tcs], tmp8, ALU.add, [t8k, ("xT", c)], [("xT", c)])
        R.barrier()

    if dbg:
        for nm in dbg:
            t_ = {"oaT": lambda: oaT, "obT": lambda: obT, "mT": lambda: mT, "qT": lambda: qT}[nm]()
            dd = nc.dram_tensor("dbg_" + nm, [128, t_.shape[1], t_.shape[2]], BF16, kind="ExternalOutput").ap()
            LD(dd, t_, [], ["dbg" + nm], "dbg")
        R.barrier()
    if mode != "A":
        A.top = PERSIST
        ost = [a1(512) for _ in range(4)]
        oi = 0
        for i in range(NT):
            for g in range(4):
                pi = oi % 4
                for j in range(4):
                    kc = g * 4 + j
                    TR(ps[pi][:, j * 128:(j + 1) * 128], xT[:, kc, i * 128:(i + 1) * 128], identf, ["xT", "cmat"], [PK[pi]])
                sb = ost[oi % 4]
                sk = "ost%d" % (oi % 4)
                evac(oi, sb, ps[pi][:, :], [PK[pi]], [sk])
                LD(x_out[i * 128:(i + 1) * 128, g * 512:(g + 1) * 512], sb, [sk], ["x_out"], sk)
                oi += 1
    R.barrier()
    with nc.Block() as block:
        R.emit(block)
    es.close()
    return nc


def _consts(core):
    p = np.arange(128)
    ident = np.eye(128, dtype=np.float32)
    same = (p[:, None] // 64) == (p[None, :] // 64)
    LBD = (same & (p[:, None] <= p[None, :])).astype(np.float32)
    UBD = (same & (p[:, None] > p[None, :])).astype(np.float32)
    LF = (p[:, None] <= p[None, :]).astype(np.float32)
    ONES = np.ones((128, 128), np.float32)
    cmat = np.stack([ident, LBD, UBD, LF, ONES], axis=1).astype(np.float32)
    ind = np.stack([(p < 64), (p >= 64)], axis=1).astype(np.float32)
    kt = np.arange(64)
    relv = (2 * kt[None, :] + (p[:, None] // 64) - 16 * core).astype(np.float32)
    qcrow = np.broadcast_to((np.arange(TOK) // 64).astype(np.float32)[None, :], (128, TOK)).copy()
    m = (np.arange(8) < core).astype(np.float32)
    mcore = np.broadcast_to(np.concatenate([m, 1.0 - m])[None, :], (128, 16)).astype(np.float32).copy()
    pos = np.arange(core * TOK, (core + 1) * TOK, dtype=np.float32)
    inv_freq = (np.float32(500000.0) ** (-np.arange(0, 32, 2, dtype=np.float32) / np.float32(32))).astype(np.float32)
    ang = (pos[:, None] * inv_freq[None, :]).astype(np.float32)
    return dict(cmat=cmat, ind=ind, relv=relv, qcrow=qcrow, mcore=mcore,
                rope_cos=np.cos(ang).astype(np.float32), rope_sin=np.sin(ang).astype(np.float32))


_PROGS = {}


def _prog(mode, layers):
    key = (mode, tuple(layers))
    if key not in _PROGS:
        _PROGS[key] = build_program(mode, list(layers))
    return _PROGS[key]


WNAMES = ["w_in", "hgrn_gnorm_w", "diff_lambda", "diff_subln_w", "w_branch_a", "w_branch_b", "w_out",
          "norm_w", "w_ffn_in", "w_ffn_out"]

FUSED = False


def kernel(**inputs):
    inp = {k: np.ascontiguousarray(np.asarray(v)) for k, v in inputs.items()}
    x = inp["x"].reshape(SEQ, D)
    consts = [_consts(c) for c in range(NCORES)]
    xs = [np.ascontiguousarray(x[c * TOK:(c + 1) * TOK]) for c in range(NCORES)]
    if FUSED:
        nc = _prog("fused", range(DEPTH))
        in_maps = []
        for c in range(NCORES):
            m = dict(consts[c])
            m["x_loc"] = xs[c]
            m["hgrn_lower_bounds"] = inp["hgrn_lower_bounds"]
            for n in WNAMES:
                m[n] = inp[n]
            in_maps.append(m)
        res = run_bass_kernel_spmd(nc, in_maps, core_ids=list(range(NCORES)))
        out = np.concatenate([r["x_out"] for r in res.results], axis=0)
        return out.reshape(1, SEQ, D).astype(np.float32)
    for l in range(DEPTH):
        wl = {n: np.ascontiguousarray(inp[n][l:l + 1]) for n in WNAMES}
        base = []
        for c in range(NCORES):
            m = dict(consts[c])
            m["x_loc"] = xs[c]
            m["hgrn_lower_bounds"] = inp["hgrn_lower_bounds"]
            m.update(wl)
            base.append(m)
        ncA = _prog("A", [l])
        resA = run_bass_kernel_spmd(ncA, base, core_ids=list(range(NCORES)))
        kT_all = np.concatenate([r["kT_loc"] for r in resA.results], axis=0)
        v_all = np.concatenate([r["v_loc"] for r in resA.results], axis=0)
        S_all = np.concatenate([r["S_loc"] for r in resA.results], axis=0)
        D_all = np.concatenate([r["D_loc"] for r in resA.results], axis=0)
        ncB = _prog("B", [l])
        inB = []
        for c in range(NCORES):
            m = dict(base[c])
            m.update(kT_all=kT_all, v_all=v_all, S_all=S_all, D_all=D_all)
            for nm in ("c_qsT", "c_ozT", "c_qT"):
                m[nm] = resA.results[c][nm]
            inB.append(m)
        resB = run_bass_kernel_spmd(ncB, inB, core_ids=list(range(NCORES)))
        xs = [np.ascontiguousarray(r["x_out"]) for r in resB.results]
    out = np.concatenate(xs, axis=0)
    return out.reshape(1, SEQ, D).astype(np.float32)
```
